# Optimizing a Trainium2 kernel written in Bass

```python
import jax, jax.numpy as jnp
from jax import lax
import numpy as np

D_MODEL = 1024
BATCH = 2
SEQ = 16384
DEPTH = 4
DEC_BATCH = 8
DEC_SEQ = 32
PAST_LEN = 1024

CHUNK = 64
N_EVEN = (DEPTH + 1) // 2
N_ODD = DEPTH // 2
GM_WIDTH = D_MODEL // 2
GM_GROUPS = 4
GM_GROUP_DIM = GM_WIDTH // GM_GROUPS
GM_CHUNK = 128
CV_WIDTH = D_MODEL // 2
CV_KERNEL = 31
CV_STATE = CV_KERNEL - 1
HG_HEADS = 8
HG_DK = D_MODEL // HG_HEADS
HG_DV = D_MODEL // HG_HEADS
HG_WIDTH = HG_HEADS * HG_DK
PK_HEADS = 8
PK_NKEYS = 128
PK_EXPERTS = PK_NKEYS * PK_NKEYS
PK_QDIM = 256
PK_HALF = PK_QDIM // 2
PK_TOPK = 16
PK_BLOCK = 256

EVEN_IN = 2 * GM_WIDTH + 2 * CV_WIDTH
ODD_IN = 4 * HG_WIDTH
EVEN_OUT = GM_WIDTH + CV_WIDTH
EPS = 1e-6

kernel_name = 'hybrid_gmlp_conformer_hgrn2_peer_stream_step'


def _rmsnorm(x, g):
    xf = x.astype(jnp.float32)
    y = xf * lax.rsqrt(jnp.mean(xf * xf, axis=-1, keepdims=True) + EPS)
    return (y * g.astype(jnp.float32)).astype(x.dtype)


def _layernorm(x, g, b):
    xf = x.astype(jnp.float32)
    mu = jnp.mean(xf, axis=-1, keepdims=True)
    xc = xf - mu
    y = xc * lax.rsqrt(jnp.mean(xc * xc, axis=-1, keepdims=True) + EPS)
    return (y * g.astype(jnp.float32) + b.astype(jnp.float32)).astype(x.dtype)


def _spatial_gate(v, ws, bs):
    bn, t, _ = v.shape
    L = min(t, GM_CHUNK)
    mask = jnp.tril(jnp.ones((L, L), dtype=bool))
    w = jnp.where(mask, ws[:, :L, :L], 0)
    vc = v.reshape(bn, t // L, L, GM_GROUPS, GM_GROUP_DIM)
    mixed = jnp.einsum('gts,bnsgc->bntgc', w, vc) + bs[:, :L].T[None, None, :, :, None]
    return mixed.reshape(bn, t, GM_WIDTH)


def _even_mixer(h, conv_prev, w_in, a_g, a_b, ws, bs, cw, cb, b_g, b_b, w_out):
    z = h @ w_in
    za = jax.nn.gelu(z[..., :2 * GM_WIDTH])
    zb = z[..., 2 * GM_WIDTH:]
    u = za[..., :GM_WIDTH]
    v = _layernorm(za[..., GM_WIDTH:], a_g, a_b)
    a_out = u * _spatial_gate(v, ws, bs)
    glu = zb[..., :CV_WIDTH] * jax.nn.sigmoid(zb[..., CV_WIDTH:])
    xpad = jnp.concatenate([conv_prev.astype(glu.dtype), glu], axis=1)
    conv = lax.conv_general_dilated(
        xpad, cw.reshape(CV_KERNEL, 1, CV_WIDTH).astype(xpad.dtype),
        window_strides=(1,), padding='VALID',
        dimension_numbers=('NWC', 'WIO', 'NWC'), feature_group_count=CV_WIDTH) + cb
    b_out = jax.nn.silu(_layernorm(conv, b_g, b_b))
    out = jnp.concatenate([a_out, b_out.astype(a_out.dtype)], axis=-1) @ w_out
    return out, xpad[:, -CV_STATE:], v


def _gla_chunk(S, inp):
    q, k, v, g = inp
    L = q.shape[2]
    b = jnp.cumsum(g, axis=2)
    causal = jnp.tril(jnp.ones((L, L), dtype=bool))
    rel = b[:, :, :, None, :] - b[:, :, None, :, :]
    decay = jnp.exp(jnp.where(causal[:, :, None], rel, -jnp.inf))
    att = jnp.einsum('bhtd,bhtsd,bhsd->bhts', q, decay, k)
    o = att @ v + jnp.einsum('bhtd,bhde->bhte', q * jnp.exp(b), S)
    b_last = b[:, :, -1]
    S_new = jnp.exp(b_last)[..., None] * S + jnp.einsum(
        'bhsd,bhse->bhde', k * jnp.exp(b_last[:, :, None] - b), v)
    return S_new, o


def _hgrn2_mixer(h, S0, lb, w_in, ng, w_out):
    bn, t, _ = h.shape
    z = h @ w_in
    zq, zf, zi, zg = jnp.split(z, 4, axis=-1)
    zf = zf.astype(jnp.float32)
    q = jax.nn.silu(zq.astype(jnp.float32))
    f = lb + (1.0 - lb) * jax.nn.sigmoid(zf)
    k = (1.0 - lb) * jax.nn.sigmoid(-zf)
    logf = jnp.log(f)
    L = min(t, CHUNK)
    nc = t // L

    def chunks(a):
        return a.reshape(bn, nc, L, HG_HEADS, -1).transpose(1, 0, 3, 2, 4)

    S_fin, o = lax.scan(_gla_chunk, S0.astype(jnp.float32),
                        (chunks(q), chunks(k), chunks(zi.astype(jnp.float32)), chunks(logf)))
    o = o.transpose(1, 0, 3, 2, 4).reshape(bn, t, HG_HEADS, HG_DV)
    o = _rmsnorm(o, ng) * jax.nn.silu(zg.astype(jnp.float32)).reshape(bn, t, HG_HEADS, HG_DV)
    return o.reshape(bn, t, HG_WIDTH) @ w_out, S_fin


def _peer(h, wq, keys, tab_u, tab_v):
    bn, t, d = h.shape
    n = bn * t
    blk = min(PK_BLOCK, n)
    pad = (-n) % blk
    flat = jnp.pad(h.reshape(n, d), ((0, pad), (0, 0)))

    def block(hb):
        tb = hb.shape[0]
        q = (hb @ wq).reshape(tb, PK_HEADS, 2, PK_HALF)
        s = jnp.einsum('thpc,hpkc->thpk', q, keys).astype(jnp.float32)
        s_top, i_top = lax.top_k(s, PK_TOPK)
        cand = (s_top[:, :, 0, :, None] + s_top[:, :, 1, None, :]).reshape(tb, PK_HEADS, PK_TOPK * PK_TOPK)
        cidx = (i_top[:, :, 0, :, None] * PK_NKEYS + i_top[:, :, 1, None, :]).reshape(tb, PK_HEADS, PK_TOPK * PK_TOPK)
        best, pos = lax.top_k(cand, PK_TOPK)
        eidx = jnp.take_along_axis(cidx, pos, axis=-1)
        gate = jax.nn.softmax(best, axis=-1)
        u = jnp.take(tab_u, eidx, axis=0)
        act = jax.nn.gelu(jnp.einsum('thkd,td->thk', u, hb).astype(jnp.float32))
        v = jnp.take(tab_v, eidx, axis=0)
        return jnp.einsum('thk,thkd->td', (gate * act).astype(v.dtype), v)

    out = lax.map(block, flat.reshape(-1, blk, d))
    return out.reshape(-1, d)[:n].reshape(bn, t, d)


def setup_inputs(seed: int = 0) -> dict:
    key = jax.random.key(seed)
    ks = jax.random.split(key, 32)

    def nrm(k, shape, scale):
        return jax.random.normal(k, shape, jnp.float32) * scale

    return {
        'x_prompt': nrm(ks[0], (BATCH, SEQ, D_MODEL), 1.0),
        'x_sample': nrm(ks[1], (DEC_BATCH, DEC_SEQ, D_MODEL), 1.0),
        'state_conv': nrm(ks[2], (N_EVEN, DEC_BATCH, CV_STATE, CV_WIDTH), 0.5),
        'state_hgrn': nrm(ks[3], (N_ODD, DEC_BATCH, HG_HEADS, HG_DK, HG_DV), 0.3),
        'norm_mix': 1.0 + nrm(ks[4], (DEPTH, D_MODEL), 0.02),
        'norm_ffn': 1.0 + nrm(ks[5], (DEPTH, D_MODEL), 0.02),
        'norm_final': 1.0 + nrm(ks[6], (D_MODEL,), 0.02),
        'ev_w_in': nrm(ks[7], (N_EVEN, D_MODEL, EVEN_IN), D_MODEL ** -0.5),
        'ev_a_ln_g': 1.0 + nrm(ks[8], (N_EVEN, GM_WIDTH), 0.02),
        'ev_a_ln_b': nrm(ks[9], (N_EVEN, GM_WIDTH), 0.02),
        'ev_ws': nrm(ks[10], (N_EVEN, GM_GROUPS, GM_CHUNK, GM_CHUNK), 0.5 * GM_CHUNK ** -0.5),
        'ev_bs': 1.0 + nrm(ks[11], (N_EVEN, GM_GROUPS, GM_CHUNK), 0.1),
        'ev_conv_w': nrm(ks[12], (N_EVEN, CV_KERNEL, CV_WIDTH), CV_KERNEL ** -0.5),
        'ev_conv_b': nrm(ks[13], (N_EVEN, CV_WIDTH), 0.02),
        'ev_b_ln_g': 1.0 + nrm(ks[14], (N_EVEN, CV_WIDTH), 0.02),
        'ev_b_ln_b': nrm(ks[15], (N_EVEN, CV_WIDTH), 0.02),
        'ev_w_out': nrm(ks[16], (N_EVEN, EVEN_OUT, D_MODEL), EVEN_OUT ** -0.5),
        'od_w_in': nrm(ks[17], (N_ODD, D_MODEL, ODD_IN), D_MODEL ** -0.5),
        'od_lower': nrm(ks[18], (DEPTH, HG_WIDTH), 0.1),
        'od_norm_g': 1.0 + nrm(ks[19], (N_ODD, HG_DV), 0.02),
        'od_w_out': nrm(ks[20], (N_ODD, HG_WIDTH, D_MODEL), HG_WIDTH ** -0.5),
        'peer_wq': nrm(ks[21], (DEPTH, D_MODEL, PK_HEADS * PK_QDIM), D_MODEL ** -0.5),
        'peer_keys': nrm(ks[22], (DEPTH, PK_HEADS, 2, PK_NKEYS, PK_HALF), PK_HALF ** -0.5),
        'peer_u': nrm(ks[23], (DEPTH, PK_EXPERTS, D_MODEL), D_MODEL ** -0.5),
        'peer_v': nrm(ks[24], (DEPTH, PK_EXPERTS, D_MODEL), D_MODEL ** -0.5),
    }


def reference(x_prompt, x_sample, state_conv, state_hgrn, norm_mix, norm_ffn, norm_final,
              ev_w_in, ev_a_ln_g, ev_a_ln_b, ev_ws, ev_bs, ev_conv_w, ev_conv_b, ev_b_ln_g, ev_b_ln_b, ev_w_out,
              od_w_in, od_lower, od_norm_g, od_w_out, peer_wq, peer_keys, peer_u, peer_v):
    lb_soft = jax.nn.softmax(od_lower.astype(jnp.float32), axis=0)
    lb_all = jnp.cumsum(lb_soft, axis=0) - lb_soft[0]
    xp, xs = x_prompt, x_sample
    conv_p, conv_s, hg_p, hg_s, v_s = [], [], [], [], []
    for l in range(DEPTH):
        j = l // 2
        hp = _rmsnorm(xp, norm_mix[l])
        hs = _rmsnorm(xs, norm_mix[l])
        if l % 2 == 0:
            prm = (ev_w_in[j], ev_a_ln_g[j], ev_a_ln_b[j], ev_ws[j], ev_bs[j], ev_conv_w[j], ev_conv_b[j],
                   ev_b_ln_g[j], ev_b_ln_b[j], ev_w_out[j])
            zeros = jnp.zeros((xp.shape[0], CV_STATE, CV_WIDTH), xp.dtype)
            mp, cp, _ = _even_mixer(hp, zeros, *prm)
            ms, cs, vs = _even_mixer(hs, state_conv[j], *prm)
            conv_p.append(cp)
            conv_s.append(cs)
            v_s.append(vs)
        else:
            S0 = jnp.zeros((xp.shape[0], HG_HEADS, HG_DK, HG_DV), jnp.float32)
            mp, Sp = _hgrn2_mixer(hp, S0, lb_all[l], od_w_in[j], od_norm_g[j], od_w_out[j])
            ms, Ss = _hgrn2_mixer(hs, state_hgrn[j], lb_all[l], od_w_in[j], od_norm_g[j], od_w_out[j])
            hg_p.append(Sp)
            hg_s.append(Ss)
        xp = xp + mp.astype(xp.dtype)
        xs = xs + ms.astype(xs.dtype)
        xp = xp + _peer(_rmsnorm(xp, norm_ffn[l]), peer_wq[l], peer_keys[l], peer_u[l], peer_v[l]).astype(xp.dtype)
        xs = xs + _peer(_rmsnorm(xs, norm_ffn[l]), peer_wq[l], peer_keys[l], peer_u[l], peer_v[l]).astype(xs.dtype)
    y_prompt = _rmsnorm(xp, norm_final)
    y_sample = _rmsnorm(xs, norm_final)
    return (y_prompt, y_sample, jnp.stack(conv_p), jnp.stack(conv_s), jnp.stack(hg_p), jnp.stack(hg_s), jnp.stack(v_s))
```

```python
import contextlib
import numpy as np
import concourse.bass as bass
import concourse.mybir as mybir
from concourse.bass_utils import run_bass_kernel_spmd

F32 = mybir.dt.float32
BF16 = mybir.dt.bfloat16
I32 = mybir.dt.int32
U32 = mybir.dt.uint32
AF = mybir.ActivationFunctionType
ALU = mybir.AluOpType
AX = mybir.AxisListType

D = 1024
EPS = 1e-6
NEG = -1.0e30


class Ctx:
    def __init__(self, nc, es):
        self.nc = nc
        self.es = es
        self.eng = {'pe': nc.tensor, 'act': nc.scalar, 'dve': nc.vector, 'pool': nc.gpsimd, 'sp': nc.sync}
        self.sem = {k: es.enter_context(nc.semaphore("sem_" + k)) for k in self.eng}
        self.dsems = {}
        self.dcnt = {}
        self.reset_state()

    def reset_state(self):
        self.cnt = {k: 0 for k in self.eng}
        for k in self.dcnt:
            self.dcnt[k] = 0
        self.waited = {k: {} for k in self.eng}
        self.lastw = {}
        self.readers = {}

    def _sem_of(self, semkey):
        return self.sem[semkey] if isinstance(semkey, str) else self.dsems[semkey[1]]

    def _wait(self, e, tok):
        semkey, val = tok
        if semkey == 'pe' and e == 'pe':
            return
        w = self.waited[e]
        if w.get(semkey, 0) >= val:
            return
        self.eng[e].wait_ge(self._sem_of(semkey), val)
        w[semkey] = val

    def _deps(self, e, reads, writes):
        for b in reads:
            if b in self.lastw:
                self._wait(e, self.lastw[b])
        for b in writes:
            if b in self.lastw:
                self._wait(e, self.lastw[b])
            for t in self.readers.get(b, ()):
                self._wait(e, t)

    def _record(self, tok, reads, writes):
        for b in reads:
            self.readers.setdefault(b, []).append(tok)
        for b in writes:
            self.lastw[b] = tok
            self.readers[b] = []

    def op(self, e, fn, reads=(), writes=()):
        self._deps(e, reads, writes)
        ins = fn(self.eng[e])
        self.cnt[e] += 1
        ins.then_inc(self.sem[e], 1)
        self._record((e, self.cnt[e]), reads, writes)

    def dma(self, q, key, fn, reads=(), writes=()):
        if key not in self.dsems:
            self.dsems[key] = self.es.enter_context(self.nc.semaphore("d_" + key))
            self.dcnt[key] = 0
        self._deps(q, reads, writes)
        ins = fn(self.eng[q])
        self.dcnt[key] += 16
        ins.then_inc(self.dsems[key], 16)
        self._record((('d', key), self.dcnt[key]), reads, writes)

    def barrier_reset(self):
        for key, c in self.dcnt.items():
            if c > 0:
                self._wait('sp', (('d', key), c))
        self.nc.all_engine_barrier()
        self.nc.gpsimd.dma_reset()
        for s in list(self.sem.values()) + list(self.dsems.values()):
            self.nc.gpsimd.sem_clear(s)
        self.nc.all_engine_barrier()
        self.reset_state()


def build_program(NT, n_phases=8, only=None, NE=16384, stop_at=99):
    nc = bass.Bass("TRN2", target_bir_lowering=False)

    def din(name, shape, dt=F32):
        return nc.dram_tensor(name, list(shape), dt, kind="ExternalInput").ap()

    def dout(name, shape, dt=F32):
        return nc.dram_tensor(name, list(shape), dt, kind="ExternalOutput").ap()

    xp = din("xp", [NT * 128, D])
    xs = din("xs", [32, D])
    st_conv = din("st_conv", [2, 30, 512])
    st_hgrn = din("st_hgrn", [2, 8, 128, 128])
    norm_mix = din("norm_mix", [4, D])
    norm_ffn = din("norm_ffn", [4, D])
    norm_final = din("norm_final", [1, D])
    ev_w_in = din("ev_w_in", [2, D, 2048])
    ev_a_ln_g = din("ev_a_ln_g", [2, 512])
    ev_a_ln_b = din("ev_a_ln_b", [2, 512])
    ev_ws = din("ev_ws", [2, 4, 128, 128])
    ev_bs = din("ev_bs", [2, 4, 128])
    ev_conv_w = din("ev_conv_w", [2, 31, 512])
    ev_conv_b = din("ev_conv_b", [2, 512])
    ev_b_ln_g = din("ev_b_ln_g", [2, 512])
    ev_b_ln_b = din("ev_b_ln_b", [2, 512])
    ev_w_out = din("ev_w_out", [2, D, D])
    od_w_in = din("od_w_in", [2, D, 4096])
    od_lower = din("od_lower", [4, D])
    od_norm_g = din("od_norm_g", [2, 128])
    od_w_out = din("od_w_out", [2, D, D])
    peer_wq = din("peer_wq", [4, D, 2048])
    peer_keys = din("peer_keys", [4, 8, 2, 128, 128])
    peer_u = din("peer_u", [4, NE, D])
    peer_v = din("peer_v", [4, NE, D])

    yp = dout("yp", [NT * 128, D])
    ys = dout("ys", [32, D])
    conv_p = dout("conv_p", [2, 30, 512])
    conv_s = dout("conv_s", [2, 30, 512])
    hg_p = dout("hg_p", [2, 8, 128, 128])
    hg_s = dout("hg_s", [2, 8, 128, 128])
    v_s = dout("v_s", [2, 32, 512])

    xres = nc.dram_tensor("xres", [(NT + 1) * 128, D], F32, kind="Internal").ap()
    tuv = nc.dram_tensor("tuv", [4 * NE, 2 * D], BF16, kind="Internal").ap()

    with contextlib.ExitStack() as es:
        K = Ctx(nc, es)

        uniq = [0]

        def sb(stack, name, shape, dt=F32):
            uniq[0] += 1
            return stack.enter_context(nc.sbuf_tensor("%s_%d" % (name, uniq[0]), list(shape), dt))

        def ps(stack, name, shape, dt=F32):
            uniq[0] += 1
            return stack.enter_context(nc.psum_tensor("%s_%d" % (name, uniq[0]), list(shape), dt))

        pidx_i = sb(es, "pidx_i", [128, 1], I32)
        jidx_i = sb(es, "jidx_i", [128, 128], I32)
        P_f = sb(es, "P_f", [128, 1])
        J_f = sb(es, "J_f", [128, 128])
        ident_f = sb(es, "ident_f", [128, 128])
        ident_bf = sb(es, "ident_bf", [128, 128], BF16)
        eps_t = sb(es, "eps_t", [128, 1])
        iota16 = sb(es, "iota16", [128, 16])
        K.op('pool', lambda e: e.iota(pidx_i[:], pattern=[[0, 1]], base=0, channel_multiplier=1), writes=['pidx_i'])
        K.op('pool', lambda e: e.iota(jidx_i[:], pattern=[[1, 128]], base=0, channel_multiplier=0), writes=['jidx_i'])
        K.op('dve', lambda e: e.tensor_copy(out=P_f[:], in_=pidx_i[:]), reads=['pidx_i'], writes=['P_f'])
        K.op('dve', lambda e: e.tensor_copy(out=J_f[:], in_=jidx_i[:]), reads=['jidx_i'], writes=['J_f'])
        K.op('dve', lambda e: e.tensor_copy(out=iota16[:], in_=jidx_i[:, 0:16]), reads=['jidx_i'], writes=['iota16'])
        K.op('dve', lambda e: e.tensor_scalar(out=ident_f[:], in0=J_f[:], scalar1=P_f[:, 0:1], scalar2=None,
                                              op0=ALU.is_equal), reads=['J_f', 'P_f'], writes=['ident_f'])
        K.op('dve', lambda e: e.tensor_copy(out=ident_bf[:], in_=ident_f[:]), reads=['ident_f'], writes=['ident_bf'])
        K.op('dve', lambda e: e.memset(eps_t[:], EPS), writes=['eps_t'])
        bounds_reg = nc.gpsimd.to_reg(4 * NE - 1)

        def rmsnorm(W, xt, gb, out_f32=None, out_bf=None, tag=""):
            K.op('act', lambda e: e.activation(out=W['junk'][:], in_=xt[:], func=AF.Square, accum_out=W['ss'][:, 0:1]),
                 reads=['xt'], writes=['junk', 'ss'])
            K.op('act', lambda e: e.activation(out=W['ss'][:, 1:2], in_=W['ss'][:, 0:1], func=AF.Sqrt,
                                               scale=1.0 / D, bias=eps_t[:, 0:1]), reads=['ss'], writes=['ss1'])
            K.op('dve', lambda e: e.reciprocal(out=W['ss'][:, 2:3], in_=W['ss'][:, 1:2]), reads=['ss1'], writes=['ss2'])
            if out_f32 is not None:
                K.op('dve', lambda e: e.scalar_tensor_tensor(out=out_f32[0][:], in0=xt[:], scalar=W['ss'][:, 2:3], in1=gb[:],
                                                             op0=ALU.mult, op1=ALU.mult),
                     reads=['xt', 'ss2'], writes=[out_f32[1]])
                if out_bf is not None:
                    K.op('act', lambda e: e.copy(out=out_bf[0][:], in_=out_f32[0][:]), reads=[out_f32[1]], writes=[out_bf[1]])
            else:
                K.op('dve', lambda e: e.scalar_tensor_tensor(out=out_bf[0][:], in0=xt[:], scalar=W['ss'][:, 2:3], in1=gb[:],
                                                             op0=ALU.mult, op1=ALU.mult),
                     reads=['xt', 'ss2'], writes=[out_bf[1]])

        def transpose8(src_bf, src_name, ptp, dstT, dst_name, nblk=8, src_off=0):
            for k in range(nblk):
                K.op('pe', lambda e, k=k: e.transpose(out=ptp[:, k, :], in_=src_bf[:, (src_off + k) * 128:(src_off + k + 1) * 128],
                                                      identity=ident_bf[:]),
                     reads=[src_name, 'ident_bf'], writes=['ptp'])
            K.op('act', lambda e: e.copy(out=dstT[:, 0:nblk, :], in_=ptp[:, 0:nblk, :]), reads=['ptp'], writes=[dst_name])

        def load_w_bf16(dst, name, src2d, N):
            srcv = src2d.rearrange("(k p) n -> p k n", p=128)
            for n0 in range(0, N, 512):
                K.dma('pool', name, lambda e, n0=n0: e.dma_start(out=dst[:, :, n0:n0 + 512], in_=srcv[:, :, n0:n0 + 512]),
                      writes=[name])

        def bcast_load(dst, name, src_row):
            K.dma('sp', name, lambda e: e.dma_start(out=dst[:], in_=src_row.partition_broadcast(128)), writes=[name])

        def run_tiles(body):
            K.barrier_reset()
            with nc.Fori(0, NT) as i:
                body(i, False)
                K.barrier_reset()
            body(None, True)
            K.barrier_reset()

        def phase_even(l, first):
            j = l // 2
            with contextlib.ExitStack() as ph:
                w_in = sb(ph, "e_w_in", [128, 8, 2048], BF16)
                w_out = sb(ph, "e_w_out", [128, 8, 1024], BF16)
                gmix = sb(ph, "e_gmix", [128, D])
                ag = sb(ph, "e_ag", [128, 512]); ab = sb(ph, "e_ab", [128, 512])
                bg = sb(ph, "e_bg", [128, 512]); bb = sb(ph, "e_bb", [128, 512])
                wgT = sb(ph, "e_wgT", [128, 4, 128], BF16)
                bs_t = sb(ph, "e_bs_t", [128, 4])
                cw = sb(ph, "e_cw", [128, 4, 32])
                cb = sb(ph, "e_cb", [128, 4])
                nat = sb(ph, "e_nat", [128, 512])
                nat_bf = sb(ph, "e_nat_bf", [128, 128], BF16)
                trilm = sb(ph, "e_trilm", [128, 128])
                W = dict(junk=sb(ph, "e_junk", [128, D]), ss=sb(ph, "e_ss", [128, 4]))
                xt = sb(ph, "e_xt", [128, D])
                hn_bf = sb(ph, "e_hn_bf", [128, D], BF16)
                hnT = sb(ph, "e_hnT", [128, 8, 128], BF16)
                u = sb(ph, "e_u", [128, 512]); vpre = sb(ph, "e_vpre", [128, 512])
                v = sb(ph, "e_v", [128, 512]); v_bf = sb(ph, "e_v_bf", [128, 512], BF16)
                sig = sb(ph, "e_sig", [128, 512]); glu = sb(ph, "e_glu", [128, 512])
                gbuf = sb(ph, "e_gbuf", [128, 4, 160])
                acc = sb(ph, "e_acc", [128, 4, 128])
                cvn = sb(ph, "e_cvn", [128, 512])
                cat_bf = sb(ph, "e_cat_bf", [128, D], BF16)
                catT = sb(ph, "e_catT", [128, 8, 128], BF16)
                st6 = sb(ph, "e_st6", [128, 6]); mv = sb(ph, "e_mv", [128, 4])
                st6b = sb(ph, "e_st6b", [128, 6]); mvb = sb(ph, "e_mvb", [128, 4])
                hal = sb(ph, "e_hal", [128, 512])
                pz = [ps(ph, "e_pz%d" % b, [128, 512]) for b in range(4)]
                ptp = ps(ph, "e_ptp", [128, 8, 128], BF16)
                pmx = ps(ph, "e_pmx", [128, 512])
                ptf = ps(ph, "e_ptf", [128, 4, 128])
                pcv = ps(ph, "e_pcv", [128, 512])

                load_w_bf16(w_in, 'w_in', ev_w_in[j], 2048)
                load_w_bf16(w_out, 'w_out', ev_w_out[j], 1024)
                bcast_load(gmix, 'gmix', norm_mix[l])
                bcast_load(ag, 'ag', ev_a_ln_g[j]); bcast_load(ab, 'ab', ev_a_ln_b[j])
                bcast_load(bg, 'bg', ev_b_ln_g[j]); bcast_load(bb, 'bb', ev_b_ln_b[j])
                K.op('dve', lambda e: e.tensor_scalar(out=trilm[:], in0=J_f[:], scalar1=P_f[:, 0:1], scalar2=None,
                                                      op0=ALU.is_le), reads=['J_f', 'P_f'], writes=['trilm'])
                for g in range(4):
                    K.dma('sp', 'nat', lambda e, g=g: e.dma_start(out=nat[:, 0:128], in_=ev_ws[j, g]), writes=['nat'])
                    K.op('dve', lambda e: e.tensor_tensor(out=nat_bf[:], in0=nat[:, 0:128], in1=trilm[:], op=ALU.mult),
                         reads=['nat', 'trilm'], writes=['nat_bf'])
                    K.op('pe', lambda e: e.transpose(out=ptp[:, 0, :], in_=nat_bf[:], identity=ident_bf[:]),
                         reads=['nat_bf', 'ident_bf'], writes=['ptp'])
                    K.op('act', lambda e, g=g: e.copy(out=wgT[:, g, :], in_=ptp[:, 0, :]), reads=['ptp'], writes=['wgT'])
                K.dma('sp', 'nat', lambda e: e.dma_start(out=nat[0:4, 0:128], in_=ev_bs[j]), writes=['nat'])
                K.op('pe', lambda e: e.transpose(out=ptf[:, 0, 0:4], in_=nat[0:4, 0:128], identity=ident_f[0:4, 0:4]),
                     reads=['nat', 'ident_f'], writes=['ptf'])
                K.op('act', lambda e: e.copy(out=bs_t[:], in_=ptf[:, 0, 0:4]), reads=['ptf'], writes=['bs_t'])
                K.dma('sp', 'nat', lambda e: e.dma_start(out=nat[0:4, 0:128],
                                                         in_=ev_conv_b[j].rearrange("(ch p) -> ch p", p=128)), writes=['nat'])
                K.op('pe', lambda e: e.transpose(out=ptf[:, 0, 0:4], in_=nat[0:4, 0:128], identity=ident_f[0:4, 0:4]),
                     reads=['nat', 'ident_f'], writes=['ptf'])
                K.op('act', lambda e: e.copy(out=cb[:], in_=ptf[:, 0, 0:4]), reads=['ptf'], writes=['cb'])
                K.dma('sp', 'nat', lambda e: e.dma_start(out=nat[0:31, :], in_=ev_conv_w[j]), writes=['nat'])
                for ch in range(4):
                    K.op('pe', lambda e, ch=ch: e.transpose(out=ptf[:, ch, 0:31], in_=nat[0:31, ch * 128:(ch + 1) * 128],
                                                            identity=ident_f[0:31, 0:31]),
                         reads=['nat', 'ident_f'], writes=['ptf'])
                K.op('act', lambda e: e.copy(out=cw[:, :, 0:31], in_=ptf[:, :, 0:31]), reads=['ptf'], writes=['cw'])
                K.op('dve', lambda e: e.memset(gbuf[:], 0.0), writes=['gbuf'])

                def body(i, sample):
                    if sample:
                        src = (xs[:, :] if first else xres[NT * 128:NT * 128 + 32, :])
                        if first:
                            K.op('dve', lambda e: e.memset(xt[:], 0.0), writes=['xt'])
                        K.dma('sp', 'xt', lambda e: e.dma_start(out=xt[0:32, :] if first else xt[:], in_=src if first else xres[NT * 128:(NT + 1) * 128, :]),
                              writes=['xt'])
                        K.dma('sp', 'hal', lambda e: e.dma_start(out=hal[0:30, :], in_=st_conv[j]), writes=['hal'])
                        for ch in range(4):
                            K.op('pe', lambda e, ch=ch: e.transpose(out=ptf[:, ch, 0:30], in_=hal[0:30, ch * 128:(ch + 1) * 128],
                                                                    identity=ident_f[0:30, 0:30]),
                                 reads=['hal', 'ident_f'], writes=['ptf'])
                        K.op('act', lambda e: e.copy(out=gbuf[:, :, 0:30], in_=ptf[:, :, 0:30]), reads=['ptf'], writes=['gbuf'])
                        dst = xres[NT * 128:(NT + 1) * 128, :]
                    else:
                        srcT = xp if first else xres
                        K.dma('sp', 'xt', lambda e: e.dma_start(out=xt[:], in_=srcT[bass.ds(i * 128, 128), :]), writes=['xt'])
                        dst = xres[bass.ds(i * 128, 128), :]
                    rmsnorm(W, xt, gmix, out_bf=(hn_bf, 'hn_bf'))
                    transpose8(hn_bf, 'hn_bf', ptp, hnT, 'hnT')
                    for nb in range(4):
                        for k in range(8):
                            K.op('pe', lambda e, nb=nb, k=k: e.matmul(pz[nb][:], lhsT=hnT[:, k, :], rhs=w_in[:, k, nb * 512:(nb + 1) * 512],
                                                                      start=(k == 0), stop=(k == 7)),
                                 reads=['hnT', 'w_in'], writes=['pz%d' % nb])
                    K.op('act', lambda e: e.activation(out=u[:], in_=pz[0][:], func=AF.Gelu_apprx_tanh), reads=['pz0'], writes=['u'])
                    K.op('act', lambda e: e.activation(out=vpre[:], in_=pz[1][:], func=AF.Gelu_apprx_tanh), reads=['pz1'], writes=['vpre'])
                    K.op('act', lambda e: e.activation(out=sig[:], in_=pz[3][:], func=AF.Sigmoid), reads=['pz3'], writes=['sig'])
                    K.op('dve', lambda e: e.tensor_tensor(out=glu[:], in0=pz[2][:], in1=sig[:], op=ALU.mult),
                         reads=['pz2', 'sig'], writes=['glu'])
                    K.op('dve', lambda e: e.bn_stats(out=st6[:], in_=vpre[:]), reads=['vpre'], writes=['st6'])
                    K.op('dve', lambda e: e.bn_aggr(out=mv[:, 0:2], in_=st6[:]), reads=['st6'], writes=['mv'])
                    K.op('act', lambda e: e.activation(out=mv[:, 2:3], in_=mv[:, 1:2], func=AF.Sqrt, bias=eps_t[:, 0:1]),
                         reads=['mv'], writes=['mv2'])
                    K.op('dve', lambda e: e.reciprocal(out=mv[:, 3:4], in_=mv[:, 2:3]), reads=['mv2'], writes=['mv3'])
                    K.op('dve', lambda e: e.tensor_scalar(out=v[:], in0=vpre[:], scalar1=mv[:, 0:1], scalar2=mv[:, 3:4],
                                                          op0=ALU.subtract, op1=ALU.mult), reads=['vpre', 'mv', 'mv3'], writes=['v'])
                    K.op('dve', lambda e: e.tensor_tensor(out=v[:], in0=v[:], in1=ag[:], op=ALU.mult), reads=['v', 'ag'], writes=['v'])
                    K.op('dve', lambda e: e.tensor_tensor(out=v[:], in0=v[:], in1=ab[:], op=ALU.add), reads=['v', 'ab'], writes=['v'])
                    K.op('act', lambda e: e.copy(out=v_bf[:], in_=v[:]), reads=['v'], writes=['v_bf'])
                    if sample:
                        K.dma('sp', 'v', lambda e: e.dma_start(out=v_s[j], in_=v[0:32, :]), reads=['v'])
                    for g in range(4):
                        K.op('pe', lambda e, g=g: e.matmul(pmx[:, g * 128:(g + 1) * 128], lhsT=wgT[:, g, :],
                                                           rhs=v_bf[:, g * 128:(g + 1) * 128], start=True, stop=True),
                             reads=['wgT', 'v_bf'], writes=['pmx'])
                    for g in range(4):
                        K.op('dve', lambda e, g=g: e.scalar_tensor_tensor(out=cat_bf[:, g * 128:(g + 1) * 128],
                                                                          in0=pmx[:, g * 128:(g + 1) * 128], scalar=bs_t[:, g:g + 1],
                                                                          in1=u[:, g * 128:(g + 1) * 128], op0=ALU.add, op1=ALU.mult),
                             reads=['pmx', 'bs_t', 'u'], writes=['cat_a'])
                    if sample:
                        K.dma('sp', 'glu', lambda e: e.dma_start(out=conv_s[j], in_=glu[2:32, :]), reads=['glu'])
                    for ch in range(4):
                        K.op('pe', lambda e, ch=ch: e.transpose(out=ptf[:, ch, :], in_=glu[:, ch * 128:(ch + 1) * 128], identity=ident_f[:]),
                             reads=['glu', 'ident_f'], writes=['ptf'])
                    K.op('act', lambda e: e.copy(out=gbuf[:, :, 30:158], in_=ptf[:, :, :]), reads=['ptf'], writes=['gbuf'])
                    for ch in range(4):
                        K.op('dve', lambda e, ch=ch: e.tensor_scalar(out=acc[:, ch, :], in0=gbuf[:, ch, 0:128], scalar1=cw[:, ch, 0:1],
                                                                     scalar2=cb[:, ch:ch + 1], op0=ALU.mult, op1=ALU.add),
                             reads=['gbuf', 'cw', 'cb'], writes=['acc%d' % ch])
                    for k in range(1, 31):
                        for ch in range(4):
                            K.op('dve', lambda e, ch=ch, k=k: e.scalar_tensor_tensor(out=acc[:, ch, :], in0=gbuf[:, ch, k:k + 128],
                                                                                     scalar=cw[:, ch, k:k + 1], in1=acc[:, ch, :],
                                                                                     op0=ALU.mult, op1=ALU.add),
                                 reads=['gbuf', 'cw', 'acc%d' % ch], writes=['acc%d' % ch])
                    K.op('pool', lambda e: e.tensor_copy(out=gbuf[:, :, 0:30], in_=gbuf[:, :, 128:158]), reads=['gbuf'], writes=['gbuf'])
                    for ch in range(4):
                        K.op('pe', lambda e, ch=ch: e.transpose(out=pcv[:, ch * 128:(ch + 1) * 128], in_=acc[:, ch, :], identity=ident_f[:]),
                             reads=['acc%d' % ch, 'ident_f'], writes=['pcv'])
                    K.op('dve', lambda e: e.bn_stats(out=st6b[:], in_=pcv[:]), reads=['pcv'], writes=['st6b'])
                    K.op('dve', lambda e: e.bn_aggr(out=mvb[:, 0:2], in_=st6b[:]), reads=['st6b'], writes=['mvb'])
                    K.op('act', lambda e: e.activation(out=mvb[:, 2:3], in_=mvb[:, 1:2], func=AF.Sqrt, bias=eps_t[:, 0:1]),
                         reads=['mvb'], writes=['mvb2'])
                    K.op('dve', lambda e: e.reciprocal(out=mvb[:, 3:4], in_=mvb[:, 2:3]), reads=['mvb2'], writes=['mvb3'])
                    K.op('dve', lambda e: e.tensor_scalar(out=cvn[:], in0=pcv[:], scalar1=mvb[:, 0:1], scalar2=mvb[:, 3:4],
                                                          op0=ALU.subtract, op1=ALU.mult), reads=['pcv', 'mvb', 'mvb3'], writes=['cvn'])
                    K.op('dve', lambda e: e.tensor_tensor(out=cvn[:], in0=cvn[:], in1=bg[:], op=ALU.mult), reads=['cvn', 'bg'], writes=['cvn'])
                    K.op('dve', lambda e: e.tensor_tensor(out=cvn[:], in0=cvn[:], in1=bb[:], op=ALU.add), reads=['cvn', 'bb'], writes=['cvn'])
                    K.op('act', lambda e: e.activation(out=cat_bf[:, 512:1024], in_=cvn[:], func=AF.Silu), reads=['cvn'], writes=['cat_b'])
                    for k in range(8):
                        K.op('pe', lambda e, k=k: e.transpose(out=ptp[:, k, :], in_=cat_bf[:, k * 128:(k + 1) * 128], identity=ident_bf[:]),
                             reads=['cat_a', 'cat_b', 'ident_bf'], writes=['ptp'])
                    K.op('act', lambda e: e.copy(out=catT[:], in_=ptp[:]), reads=['ptp'], writes=['catT'])
                    for nb in range(2):
                        for k in range(8):
                            K.op('pe', lambda e, nb=nb, k=k: e.matmul(pz[nb][:], lhsT=catT[:, k, :], rhs=w_out[:, k, nb * 512:(nb + 1) * 512],
                                                                      start=(k == 0), stop=(k == 7)),
                                 reads=['catT', 'w_out'], writes=['pz%d' % nb])
                    for nb in range(2):
                        K.op('dve', lambda e, nb=nb: e.tensor_tensor(out=xt[:, nb * 512:(nb + 1) * 512], in0=pz[nb][:],
                                                                     in1=xt[:, nb * 512:(nb + 1) * 512], op=ALU.add),
                             reads=['pz%d' % nb, 'xt'], writes=['xt'])
                    K.dma('sp', 'xt', lambda e: e.dma_start(out=dst, in_=xt[:]), reads=['xt'])

                K.barrier_reset()
                with nc.Fori(0, NT) as i:
                    body(i, False)
                    K.barrier_reset()
                K.dma('sp', 'glu', lambda e: e.dma_start(out=conv_p[j], in_=glu[98:128, :]), reads=['glu'])
                body(None, True)
                K.barrier_reset()

        def phase_odd(l, dbg_src=False):
            j = l // 2
            with contextlib.ExitStack() as ph:
                w_in = sb(ph, "o_w_in", [128, 8, 4096], BF16)
                w_out = sb(ph, "o_w_out", [128, 8, 1024], BF16)
                gmix = sb(ph, "o_gmix", [128, D])
                lb_b = sb(ph, "o_lb", [128, D]); oml_b = sb(ph, "o_oml", [128, D])
                ng_b = sb(ph, "o_ng", [128, 128])
                S = sb(ph, "o_S", [128, 8, 128])
                attmask = sb(ph, "o_attmask", [128, 128]); trirel = sb(ph, "o_trirel", [128, 128])
                selc = sb(ph, "o_selc", [128, 4]); rowm = sb(ph, "o_rowm", [128, 1])
                W = dict(junk=sb(ph, "o_junk", [128, D]), ss=sb(ph, "o_ss", [128, 4]))
                xt = sb(ph, "o_xt", [128, D])
                hn_bf = sb(ph, "o_hn_bf", [128, D], BF16)
                hnT = sb(ph, "o_hnT", [128, 8, 128], BF16)
                q = sb(ph, "o_q", [128, D]); sg = sb(ph, "o_sg", [128, D])
                fg = sb(ph, "o_fg", [128, D]); logf = sb(ph, "o_logf", [128, D]); kk = sb(ph, "o_kk", [128, D])
                ep = sb(ph, "o_ep", [128, D]); em = sb(ph, "o_em", [128, D])
                v_bf = sb(ph, "o_v_bf", [128, D], BF16)
                qt_bf = sb(ph, "o_qt_bf", [128, D], BF16); kt_bf = sb(ph, "o_kt_bf", [128, D], BF16)
                ktc = [sb(ph, "o_ktc%d" % c, [128, D], BF16) for c in range(2)]
                qT = sb(ph, "o_qT", [128, 8, 128], BF16); qT0 = sb(ph, "o_qT0", [128, 8, 128], BF16)
                qT1 = sb(ph, "o_qT1", [128, 8, 128], BF16); kT = sb(ph, "o_kT", [128, 8, 128], BF16)
                attm = sb(ph, "o_attm", [128, 8, 128], BF16)
                Sp0 = sb(ph, "o_Sp0", [128, 8, 128], BF16); Sp1 = sb(ph, "o_Sp1", [128, 8, 128], BF16)
                tmpS = sb(ph, "o_tmpS", [128, 8, 128])
                fs = sb(ph, "o_fs", [128, 8, 4]); arg = sb(ph, "o_arg", [128, 8, 6]); E = sb(ph, "o_E", [128, 8, 6])
                ssq = sb(ph, "o_ssq", [128, 8]); rst8 = sb(ph, "o_rst8", [128, 8])
                o = sb(ph, "o_o", [128, D]); o_bf = sb(ph, "o_o_bf", [128, D], BF16)
                oT = sb(ph, "o_oT", [128, 8, 128], BF16)
                pA = ps(ph, "o_pA", [128, 1024]); pB = ps(ph, "o_pB", [128, 1024]); pC = ps(ph, "o_pC", [128, 1024])
                ptp = ps(ph, "o_ptp", [128, 8, 128], BF16)
                psm = ps(ph, "o_psm", [128, 512])

                load_w_bf16(w_in, 'w_in', od_w_in[j], 4096)
                load_w_bf16(w_out, 'w_out', od_w_out[j], 1024)
                bcast_load(gmix, 'gmix', norm_mix[l])
                bcast_load(ng_b, 'ng_b', od_norm_g[j])
                for r in range(4):
                    dstt = [q, sg, fg, kk][r]
                    K.dma('sp', 'lw%d' % r, lambda e, r=r, dstt=dstt: e.dma_start(out=dstt[:], in_=od_lower[r].partition_broadcast(128)),
                          writes=['lw%d' % r])
                lw = [q, sg, fg, kk]
                K.op('dve', lambda e: e.tensor_tensor(out=ep[:], in0=lw[0][:], in1=lw[1][:], op=ALU.max), reads=['lw0', 'lw1'], writes=['ep'])
                K.op('dve', lambda e: e.tensor_tensor(out=ep[:], in0=ep[:], in1=lw[2][:], op=ALU.max), reads=['ep', 'lw2'], writes=['ep'])
                K.op('dve', lambda e: e.tensor_tensor(out=ep[:], in0=ep[:], in1=lw[3][:], op=ALU.max), reads=['ep', 'lw3'], writes=['ep'])
                for r in range(4):
                    K.op('dve', lambda e, r=r: e.tensor_tensor(out=lw[r][:], in0=lw[r][:], in1=ep[:], op=ALU.subtract),
                         reads=['lw%d' % r, 'ep'], writes=['lw%d' % r])
                    K.op('act', lambda e, r=r: e.activation(out=lw[r][:], in_=lw[r][:], func=AF.Exp), reads=['lw%d' % r], writes=['lw%d' % r])
                K.op('dve', lambda e: e.tensor_tensor(out=em[:], in0=lw[0][:], in1=lw[1][:], op=ALU.add), reads=['lw0', 'lw1'], writes=['em'])
                K.op('dve', lambda e: e.tensor_tensor(out=em[:], in0=em[:], in1=lw[2][:], op=ALU.add), reads=['em', 'lw2'], writes=['em'])
                K.op('dve', lambda e: e.tensor_tensor(out=em[:], in0=em[:], in1=lw[3][:], op=ALU.add), reads=['em', 'lw3'], writes=['em'])
                K.op('dve', lambda e: e.reciprocal(out=em[:], in_=em[:]), reads=['em'], writes=['em'])
                K.op('dve', lambda e: e.tensor_copy(out=lb_b[:], in_=lw[1][:]), reads=['lw1'], writes=['lb_b'])
                for r in range(2, l + 1):
                    K.op('dve', lambda e, r=r: e.tensor_tensor(out=lb_b[:], in0=lb_b[:], in1=lw[r][:], op=ALU.add),
                         reads=['lb_b', 'lw%d' % r], writes=['lb_b'])
                K.op('dve', lambda e: e.tensor_tensor(out=lb_b[:], in0=lb_b[:], in1=em[:], op=ALU.mult), reads=['lb_b', 'em'], writes=['lb_b'])
                K.op('dve', lambda e: e.tensor_scalar(out=oml_b[:], in0=lb_b[:], scalar1=-1.0, scalar2=1.0, op0=ALU.mult, op1=ALU.add),
                     reads=['lb_b'], writes=['oml_b'])
                le = ep; jc = em; same = o
                K.op('dve', lambda e: e.tensor_scalar(out=le[:, 0:128], in0=J_f[:], scalar1=P_f[:, 0:1], scalar2=None, op0=ALU.is_ge),
                     reads=['J_f', 'P_f', 'ep'], writes=['ep'])
                K.op('dve', lambda e: e.tensor_scalar(out=jc[:, 0:128], in0=J_f[:], scalar1=64.0, scalar2=None, op0=ALU.is_ge),
                     reads=['J_f', 'em'], writes=['em'])
                K.op('dve', lambda e: e.tensor_scalar(out=rowm[:], in0=P_f[:], scalar1=64.0, scalar2=None, op0=ALU.is_ge),
                     reads=['P_f'], writes=['rowm'])
                K.op('dve', lambda e: e.tensor_scalar(out=same[:, 0:128], in0=jc[:, 0:128], scalar1=rowm[:, 0:1], scalar2=None,
                                                      op0=ALU.is_equal), reads=['em', 'rowm'], writes=['o'])
                K.op('dve', lambda e: e.tensor_tensor(out=attmask[:], in0=le[:, 0:128], in1=same[:, 0:128], op=ALU.mult),
                     reads=['ep', 'o'], writes=['attmask'])
                K.op('dve', lambda e: e.tensor_scalar(out=jc[:, 0:128], in0=jc[:, 0:128], scalar1=64.0, scalar2=31.0,
                                                      op0=ALU.mult, op1=ALU.add), reads=['em'], writes=['em'])
                K.op('dve', lambda e: e.tensor_scalar(out=jc[:, 0:128], in0=jc[:, 0:128], scalar1=P_f[:, 0:1], scalar2=None,
                                                      op0=ALU.is_ge), reads=['em', 'P_f'], writes=['em'])
                K.op('dve', lambda e: e.tensor_tensor(out=le[:, 0:128], in0=le[:, 0:128], in1=jc[:, 0:128], op=ALU.subtract),
                     reads=['ep', 'em'], writes=['ep'])
                K.op('dve', lambda e: e.tensor_tensor(out=trirel[:], in0=le[:, 0:128], in1=same[:, 0:128], op=ALU.mult),
                     reads=['ep', 'o'], writes=['trirel'])
                K.op('dve', lambda e: e.tensor_scalar(out=selc[:, 0:1], in0=P_f[:], scalar1=64.0, scalar2=None, op0=ALU.is_lt),
                     reads=['P_f'], writes=['selc'])
                K.op('dve', lambda e: e.tensor_scalar(out=selc[:, 1:2], in0=P_f[:], scalar1=31.0, scalar2=None, op0=ALU.is_le),
                     reads=['P_f'], writes=['selc'])
                K.op('dve', lambda e: e.tensor_scalar(out=selc[:, 2:3], in0=P_f[:], scalar1=64.0, scalar2=None, op0=ALU.is_ge),
                     reads=['P_f'], writes=['selc'])
                K.op('dve', lambda e: e.tensor_scalar(out=selc[:, 3:4], in0=P_f[:], scalar1=95.0, scalar2=None, op0=ALU.is_le),
                     reads=['P_f'], writes=['selc'])
                K.op('dve', lambda e: e.tensor_tensor(out=selc[:, 3:4], in0=selc[:, 3:4], in1=selc[:, 2:3], op=ALU.mult),
                     reads=['selc'], writes=['selc'])
                K.op('dve', lambda e: e.tensor_scalar(out=rowm[:], in0=P_f[:], scalar1=32.0, scalar2=None, op0=ALU.is_lt),
                     reads=['P_f', 'rowm'], writes=['rowm'])
                K.op('dve', lambda e: e.memset(S[:], 0.0), writes=['S'])
                K.op('pool', lambda e: e.memset(qT0[:], 0.0), writes=['qT0'])
                K.op('pool', lambda e: e.memset(qT1[:], 0.0), writes=['qT1'])

                def hb(t, h):
                    return t[:, h * 128:(h + 1) * 128]

                def body(i, sample):
                    if sample:
                        if dbg_src:
                            K.op('dve', lambda e: e.memset(xt[:], 0.0), writes=['xt'])
                            K.dma('sp', 'xt', lambda e: e.dma_start(out=xt[0:32, :], in_=xs[:, :]), writes=['xt'])
                        else:
                            K.dma('sp', 'xt', lambda e: e.dma_start(out=xt[:], in_=xres[NT * 128:(NT + 1) * 128, :]), writes=['xt'])
                        K.dma('sp', 'S', lambda e: e.dma_start(out=S[:], in_=st_hgrn[j].rearrange("h d e -> d h e")), writes=['S'])
                        dst = xres[NT * 128:(NT + 1) * 128, :]
                    else:
                        srcT = xp if dbg_src else xres
                        K.dma('sp', 'xt', lambda e: e.dma_start(out=xt[:], in_=srcT[bass.ds(i * 128, 128), :]), writes=['xt'])
                        dst = xres[bass.ds(i * 128, 128), :]
                    rmsnorm(W, xt, gmix, out_bf=(hn_bf, 'hn_bf'))
                    transpose8(hn_bf, 'hn_bf', ptp, hnT, 'hnT')

                    def zmm(pt, pname, cb0):
                        for nb in range(2):
                            for k in range(8):
                                K.op('pe', lambda e, nb=nb, k=k: e.matmul(pt[:, nb * 512:(nb + 1) * 512], lhsT=hnT[:, k, :],
                                                                          rhs=w_in[:, k, cb0 + nb * 512:cb0 + (nb + 1) * 512],
                                                                          start=(k == 0), stop=(k == 7)),
                                     reads=['hnT', 'w_in'], writes=[pname])
                    zmm(pA, 'pA', 0)
                    K.op('act', lambda e: e.activation(out=q[:], in_=pA[:], func=AF.Silu), reads=['pA'], writes=['q'])
                    zmm(pB, 'pB', 1024)
                    K.op('act', lambda e: e.activation(out=fg[:], in_=pB[:], func=AF.Sigmoid), reads=['pB'], writes=['fg'])
                    K.op('dve', lambda e: e.tensor_tensor(out=fg[:], in0=fg[:], in1=oml_b[:], op=ALU.mult), reads=['fg', 'oml_b'], writes=['fg'])
                    K.op('dve', lambda e: e.tensor_tensor(out=fg[:], in0=fg[:], in1=lb_b[:], op=ALU.add), reads=['fg', 'lb_b'], writes=['fg'])
                    K.op('act', lambda e: e.activation(out=logf[:], in_=fg[:], func=AF.Ln), reads=['fg'], writes=['logf'])
                    K.op('dve', lambda e: e.tensor_scalar(out=kk[:], in0=fg[:], scalar1=-1.0, scalar2=1.0, op0=ALU.mult, op1=ALU.add),
                         reads=['fg'], writes=['kk'])
                    if sample:
                        K.op('dve', lambda e: e.tensor_scalar(out=logf[:], in0=logf[:], scalar1=rowm[:, 0:1], scalar2=None, op0=ALU.mult),
                             reads=['logf', 'rowm'], writes=['logf'])
                        K.op('dve', lambda e: e.tensor_scalar(out=kk[:], in0=kk[:], scalar1=rowm[:, 0:1], scalar2=None, op0=ALU.mult),
                             reads=['kk', 'rowm'], writes=['kk'])
                    zmm(pA, 'pA', 2048)
                    K.op('act', lambda e: e.copy(out=v_bf[:], in_=pA[:]), reads=['pA'], writes=['v_bf'])
                    zmm(pB, 'pB', 3072)
                    K.op('act', lambda e: e.activation(out=sg[:], in_=pB[:], func=AF.Silu), reads=['pB'], writes=['sg'])
                    if stop_at <= 1:
                        K.dma('sp', 'xt', lambda e: e.dma_start(out=dst, in_=xt[:]), reads=['xt'])
                        return
                    for nb in range(2):
                        K.op('pe', lambda e, nb=nb: e.matmul(pC[:, nb * 512:(nb + 1) * 512], lhsT=trirel[:], rhs=logf[:, nb * 512:(nb + 1) * 512],
                                                             start=True, stop=True), reads=['trirel', 'logf'], writes=['pC'])
                    K.op('act', lambda e: e.activation(out=ep[:], in_=pC[:], func=AF.Exp), reads=['pC'], writes=['ep'])
                    K.op('act', lambda e: e.activation(out=em[:], in_=pC[:], func=AF.Exp, scale=-1.0), reads=['pC'], writes=['em'])
                    K.op('dve', lambda e: e.tensor_tensor(out=qt_bf[:], in0=q[:], in1=ep[:], op=ALU.mult), reads=['q', 'ep'], writes=['qt_bf'])
                    K.op('dve', lambda e: e.tensor_tensor(out=kt_bf[:], in0=kk[:], in1=em[:], op=ALU.mult), reads=['kk', 'em'], writes=['kt_bf'])
                    K.op('dve', lambda e: e.scalar_tensor_tensor(out=ktc[0][:], in0=kk[:], scalar=selc[:, 0:1], in1=em[:], op0=ALU.mult, op1=ALU.mult),
                         reads=['kk', 'em', 'selc'], writes=['ktc0'])
                    K.op('dve', lambda e: e.scalar_tensor_tensor(out=ktc[1][:], in0=kk[:], scalar=selc[:, 2:3], in1=em[:], op0=ALU.mult, op1=ALU.mult),
                         reads=['kk', 'em', 'selc'], writes=['ktc1'])
                    if stop_at <= 2:
                        K.dma('sp', 'xt', lambda e: e.dma_start(out=dst, in_=xt[:]), reads=['xt'])
                        return
                    for h in range(8):
                        K.op('pe', lambda e, h=h: e.matmul(psm[:, h * 4:(h + 1) * 4], lhsT=hb(logf, h), rhs=selc[:], start=True, stop=True),
                             reads=['logf', 'selc'], writes=['psm'])
                    K.op('dve', lambda e: e.tensor_copy(out=fs[:], in_=psm[:, 0:32].rearrange("p (h c) -> p h c", c=4)),
                         reads=['psm'], writes=['fs'])
                    for c in range(2):
                        K.op('dve', lambda e, c=c: e.tensor_copy(out=arg[:, :, 3 * c:3 * c + 1], in_=fs[:, :, 2 * c:2 * c + 1]),
                             reads=['fs'], writes=['arg'])
                        K.op('dve', lambda e, c=c: e.tensor_tensor(out=arg[:, :, 3 * c + 1:3 * c + 2], in0=fs[:, :, 2 * c:2 * c + 1],
                                                                   in1=fs[:, :, 2 * c + 1:2 * c + 2], op=ALU.subtract),
                             reads=['fs'], writes=['arg'])
                        K.op('dve', lambda e, c=c: e.tensor_copy(out=arg[:, :, 3 * c + 2:3 * c + 3], in_=fs[:, :, 2 * c + 1:2 * c + 2]),
                             reads=['fs'], writes=['arg'])
                    K.op('act', lambda e: e.activation(out=E[:], in_=arg[:], func=AF.Exp), reads=['arg'], writes=['E'])
                    if stop_at <= 3:
                        K.dma('sp', 'xt', lambda e: e.dma_start(out=dst, in_=xt[:]), reads=['xt'])
                        return
                    for h in range(8):
                        K.op('pe', lambda e, h=h: e.transpose(out=ptp[:, h, :], in_=hb(qt_bf, h), identity=ident_bf[:]),
                             reads=['qt_bf', 'ident_bf'], writes=['ptp'])
                    K.op('act', lambda e: e.copy(out=qT[:], in_=ptp[:]), reads=['ptp'], writes=['qT'])
                    K.op('act', lambda e: e.copy(out=qT0[:, :, 0:64], in_=qT[:, :, 0:64]), reads=['qT'], writes=['qT0'])
                    K.op('act', lambda e: e.copy(out=qT1[:, :, 64:128], in_=qT[:, :, 64:128]), reads=['qT'], writes=['qT1'])
                    for h in range(8):
                        K.op('pe', lambda e, h=h: e.transpose(out=ptp[:, h, :], in_=hb(kt_bf, h), identity=ident_bf[:]),
                             reads=['kt_bf', 'ident_bf'], writes=['ptp'])
                    K.op('act', lambda e: e.copy(out=kT[:], in_=ptp[:]), reads=['ptp'], writes=['kT'])
                    if stop_at <= 4:
                        K.dma('sp', 'xt', lambda e: e.dma_start(out=dst, in_=xt[:]), reads=['xt'])
                        return
                    for h in range(8):
                        K.op('pe', lambda e, h=h: e.matmul(hb(pA, h), lhsT=kT[:, h, :], rhs=qT[:, h, :], start=True, stop=True),
                             reads=['kT', 'qT'], writes=['pA'])
                    K.op('dve', lambda e: e.tensor_tensor(out=attm[:], in0=pA[:].rearrange("p (h t) -> p h t", t=128),
                                                          in1=attmask[:].unsqueeze(1).to_broadcast([128, 8, 128]), op=ALU.mult),
                         reads=['pA', 'attmask'], writes=['attm'])
                    if stop_at <= 5:
                        K.dma('sp', 'xt', lambda e: e.dma_start(out=dst, in_=xt[:]), reads=['xt'])
                        return
                    for c in range(2):
                        Sp = Sp0 if c == 0 else Sp1
                        spn = 'Sp%d' % c
                        K.op('dve', lambda e, c=c, Sp=Sp: e.tensor_tensor(out=Sp[:], in0=S[:],
                                                                          in1=E[:, :, 3 * c + 2:3 * c + 3].to_broadcast([128, 8, 128]), op=ALU.mult),
                             reads=['S', 'E'], writes=[spn])
                        for h in range(8):
                            K.op('pe', lambda e, h=h, c=c: e.matmul(hb(pC, h), lhsT=hb(ktc[c], h), rhs=hb(v_bf, h), start=True, stop=True),
                                 reads=['ktc%d' % c, 'v_bf'], writes=['pC'])
                        K.op('dve', lambda e, c=c: e.tensor_tensor(out=tmpS[:], in0=pC[:].rearrange("p (h t) -> p h t", t=128),
                                                                   in1=E[:, :, 3 * c + 1:3 * c + 2].to_broadcast([128, 8, 128]), op=ALU.mult),
                             reads=['pC', 'E'], writes=['tmpS'])
                        K.op('dve', lambda e, c=c: e.tensor_tensor(out=S[:], in0=S[:], in1=E[:, :, 3 * c:3 * c + 1].to_broadcast([128, 8, 128]),
                                                                   op=ALU.mult), reads=['S', 'E'], writes=['S'])
                        K.op('dve', lambda e: e.tensor_tensor(out=S[:], in0=S[:], in1=tmpS[:], op=ALU.add), reads=['S', 'tmpS'], writes=['S'])
                    if stop_at <= 6:
                        K.dma('sp', 'xt', lambda e: e.dma_start(out=dst, in_=xt[:]), reads=['xt'])
                        return
                    for h in range(8):
                        K.op('pe', lambda e, h=h: e.matmul(hb(pB, h), lhsT=attm[:, h, :], rhs=hb(v_bf, h), start=True, stop=False),
                             reads=['attm', 'v_bf'], writes=['pB'])
                        K.op('pe', lambda e, h=h: e.matmul(hb(pB, h), lhsT=qT0[:, h, :], rhs=Sp0[:, h, :], start=False, stop=False),
                             reads=['qT0', 'Sp0'], writes=['pB'])
                        K.op('pe', lambda e, h=h: e.matmul(hb(pB, h), lhsT=qT1[:, h, :], rhs=Sp1[:, h, :], start=False, stop=True),
                             reads=['qT1', 'Sp1'], writes=['pB'])
                    if stop_at <= 7:
                        K.dma('sp', 'xt', lambda e: e.dma_start(out=dst, in_=xt[:]), reads=['xt'])
                        return
                    K.op('act', lambda e: e.activation(out=W['junk'][:], in_=pB[:], func=AF.Square), reads=['pB'], writes=['junk'])
                    K.op('dve', lambda e: e.tensor_reduce(out=ssq[:], in_=W['junk'][:].rearrange("p (h t) -> p h t", t=128), axis=AX.X, op=ALU.add),
                         reads=['junk'], writes=['ssq'])
                    K.op('act', lambda e: e.activation(out=ssq[:], in_=ssq[:], func=AF.Sqrt, scale=1.0 / 128, bias=eps_t[:, 0:1]),
                         reads=['ssq'], writes=['ssq'])
                    K.op('dve', lambda e: e.reciprocal(out=rst8[:], in_=ssq[:]), reads=['ssq'], writes=['rst8'])
                    K.op('dve', lambda e: e.tensor_tensor(out=o[:].rearrange("p (h t) -> p h t", t=128),
                                                          in0=pB[:].rearrange("p (h t) -> p h t", t=128),
                                                          in1=rst8[:].unsqueeze(2).to_broadcast([128, 8, 128]), op=ALU.mult),
                         reads=['pB', 'rst8'], writes=['o'])
                    K.op('dve', lambda e: e.tensor_tensor(out=o[:].rearrange("p (h t) -> p h t", t=128),
                                                          in0=o[:].rearrange("p (h t) -> p h t", t=128),
                                                          in1=ng_b[:].unsqueeze(1).to_broadcast([128, 8, 128]), op=ALU.mult),
                         reads=['o', 'ng_b'], writes=['o'])
                    K.op('dve', lambda e: e.tensor_tensor(out=o_bf[:], in0=o[:], in1=sg[:], op=ALU.mult), reads=['o', 'sg'], writes=['o_bf'])
                    transpose8(o_bf, 'o_bf', ptp, oT, 'oT')
                    for nb in range(2):
                        for k in range(8):
                            K.op('pe', lambda e, nb=nb, k=k: e.matmul(pA[:, nb * 512:(nb + 1) * 512], lhsT=oT[:, k, :],
                                                                      rhs=w_out[:, k, nb * 512:(nb + 1) * 512], start=(k == 0), stop=(k == 7)),
                                 reads=['oT', 'w_out'], writes=['pA'])
                    K.op('dve', lambda e: e.tensor_tensor(out=xt[:], in0=pA[:], in1=xt[:], op=ALU.add), reads=['pA', 'xt'], writes=['xt'])
                    K.dma('sp', 'xt', lambda e: e.dma_start(out=dst, in_=xt[:]), reads=['xt'])

                K.barrier_reset()
                with nc.Fori(0, NT) as i:
                    body(i, False)
                    K.barrier_reset()
                K.dma('sp', 'S', lambda e: e.dma_start(out=hg_p[j].rearrange("h d e -> d h e"), in_=S[:]), reads=['S'])
                body(None, True)
                K.dma('sp', 'S', lambda e: e.dma_start(out=hg_s[j].rearrange("h d e -> d h e"), in_=S[:]), reads=['S'])
                K.barrier_reset()

        def phase_peer(l, last):
            NB = 16
            with contextlib.ExitStack() as ph:
                wq = sb(ph, "p_wq", [128, 8, 2048], BF16)
                keysT = sb(ph, "p_keysT", [128, 16, 128], BF16)
                gffn = sb(ph, "p_gffn", [128, D])
                gfin = sb(ph, "p_gfin", [128, D]) if last else None
                W = dict(junk=sb(ph, "p_junk", [128, D], BF16), ss=sb(ph, "p_ss", [128, 4]))
                xt = sb(ph, "p_xt", [128, D])
                hn_bf = sb(ph, "p_hn_bf", [128, D], BF16)
                hnT = sb(ph, "p_hnT", [128, 8, 128], BF16)
                q_bf = sb(ph, "p_q_bf", [128, 2048], BF16)
                hn = q_bf[:].bitcast(F32)
                qT = sb(ph, "p_qT", [128, 16, 128], BF16)
                sc = sb(ph, "p_sc", [128, 2048]); sc2 = sb(ph, "p_sc2", [128, 2048])
                top = sb(ph, "p_top", [128, 16, 16]); topi = sb(ph, "p_topi", [128, 16, 16], U32)
                topf = sb(ph, "p_topf", [128, 16, 16])
                i128 = sb(ph, "p_i128", [128, 8, 16])
                cand = sb(ph, "p_cand", [128, 8, 256]); cand2 = sc2[:].rearrange("p (h c) -> p h c", c=256)
                best = sb(ph, "p_best", [128, 8, 16]); pos = sb(ph, "p_pos", [128, 8, 16], U32)
                a_u = sb(ph, "p_a_u", [128, 8, 16], U32); b_u = sb(ph, "p_b_u", [128, 8, 16], U32)
                a_f = sb(ph, "p_a_f", [128, 8, 16]); b_f = sb(ph, "p_b_f", [128, 8, 16])
                eq = sc[:].rearrange("p (h a b) -> p h a b", a=16, b=16)
                isel = sb(ph, "p_isel", [128, 8, 16]); jsel = sb(ph, "p_jsel", [128, 8, 16])
                eidx = sb(ph, "p_eidx", [128, 128], I32)
                ge = sb(ph, "p_ge", [128, 8, 16]); gs = sb(ph, "p_gs", [128, 8])
                actv = sb(ph, "p_actv", [128, 128]); wgt = sb(ph, "p_wgt", [128, 128])
                dg = [sb(ph, "p_dg%d" % b, [128, 8, 128], BF16) for b in range(2)]
                uvb = [sb(ph, "p_uvb%d" % b, [128, 2 * D], BF16) for b in range(NB)]
                uvb_extra_names = []
                for (tt, nm) in ((sc, ['sc0', 'sc1', 'sc2', 'sc3', 'eq']), (sc2, ['sc2_%d' % x for x in range(16)]),
                                 (cand[:].rearrange("p h c -> p (h c)"), ['cand'])):
                    for half in range(2):
                        uvb.append(tt[:, half * 1024:(half + 1) * 1024].bitcast(BF16))
                        uvb_extra_names.append(nm)
                NBT = len(uvb)
                pbig = ps(ph, "p_pbig", [128, 2048])
                pz = [pbig[:, b * 512:(b + 1) * 512] for b in range(4)]
                hn_ps = ps(ph, "p_hnps", [128, D])
                ptp = ps(ph, "p_ptp", [128, 8, 128], BF16)
                oacc = pbig[:, 0:1024]

                load_w_bf16(wq, 'wq', peer_wq[l], 2048)
                bcast_load(gffn, 'gffn', norm_ffn[l])
                if last:
                    bcast_load(gfin, 'gfin', norm_final[0])
                nat = sc[:, 0:1024].rearrange("p (a c) -> p a c", c=128)
                nat_bf = q_bf[:].rearrange("p (a c) -> p a c", c=128)
                for half in range(2):
                    K.dma('sp', 'nat', lambda e, half=half: e.dma_start(
                        out=nat, in_=peer_keys[l, half * 4:(half + 1) * 4].rearrange("h p k c -> k (h p) c")), writes=['nat'])
                    K.op('dve', lambda e, half=half: e.tensor_copy(out=nat_bf[:, half * 8:(half + 1) * 8, :], in_=nat), reads=['nat'], writes=['nat_bf'])
                for half in range(2):
                    for k in range(8):
                        K.op('pe', lambda e, k=k, half=half: e.transpose(out=ptp[:, k, :], in_=nat_bf[:, half * 8 + k, :], identity=ident_bf[:]),
                             reads=['nat_bf', 'ident_bf'], writes=['ptp'])
                    K.op('act', lambda e, half=half: e.copy(out=keysT[:, half * 8:(half + 1) * 8, :], in_=ptp[:]), reads=['ptp'], writes=['keysT'])


                def body(i, sample):
                    if sample:
                        K.dma('sp', 'xt', lambda e: e.dma_start(out=xt[:], in_=xres[NT * 128:(NT + 1) * 128, :]), writes=['xt'])
                    else:
                        K.dma('sp', 'xt', lambda e: e.dma_start(out=xt[:], in_=xres[bass.ds(i * 128, 128), :]), writes=['xt'])
                    rmsnorm(W, xt, gffn, out_f32=(hn, 'hn'), out_bf=(hn_bf, 'hn_bf'))
                    for hh in range(2):
                        K.op('act', lambda e, hh=hh: e.copy(out=hn_ps[:, hh * 512:(hh + 1) * 512], in_=hn[:, hh * 512:(hh + 1) * 512]),
                             reads=['hn'], writes=['hn_ps'])
                    transpose8(hn_bf, 'hn_bf', ptp, hnT, 'hnT')
                    for nb in range(4):
                        for k in range(8):
                            K.op('pe', lambda e, nb=nb, k=k: e.matmul(pz[nb][:], lhsT=hnT[:, k, :], rhs=wq[:, k, nb * 512:(nb + 1) * 512],
                                                                      start=(k == 0), stop=(k == 7)),
                                 reads=['hnT', 'wq'], writes=['pz%d' % nb])
                        K.op('act', lambda e, nb=nb: e.copy(out=q_bf[:, nb * 512:(nb + 1) * 512], in_=pz[nb][:]),
                             reads=['pz%d' % nb], writes=['q_bf', 'hn'])
                    for half in range(2):
                        for k in range(8):
                            K.op('pe', lambda e, k=k, half=half: e.transpose(out=ptp[:, k, :], in_=q_bf[:, (half * 8 + k) * 128:(half * 8 + k + 1) * 128],
                                                                             identity=ident_bf[:]),
                                 reads=['q_bf', 'ident_bf'], writes=['ptp'])
                        K.op('act', lambda e, half=half: e.copy(out=qT[:, half * 8:(half + 1) * 8, :], in_=ptp[:]), reads=['ptp'], writes=['qT'])
                    for hp in range(16):
                        K.op('pe', lambda e, hp=hp: e.matmul(pz[hp // 4][:, (hp % 4) * 128:(hp % 4 + 1) * 128], lhsT=qT[:, hp, :],
                                                             rhs=keysT[:, hp, :], start=True, stop=True),
                             reads=['qT', 'keysT'], writes=['pz%d' % (hp // 4)])
                    for nb in range(4):
                        K.op('act', lambda e, nb=nb: e.copy(out=sc[:, nb * 512:(nb + 1) * 512], in_=pz[nb][:]),
                             reads=['pz%d' % nb], writes=['sc%d' % nb])
                    def grp_steps(hp):
                        blk = slice(hp * 128, (hp + 1) * 128)
                        scn = 'sc%d' % (hp // 4)
                        T, TI, S2 = 'top_%d' % hp, 'topi_%d' % hp, 'sc2_%d' % hp
                        return [
                            lambda: K.op('dve', lambda e: e.max(out=top[:, hp, 0:8], in_=sc[:, blk]), reads=[scn], writes=[T]),
                            lambda: K.op('dve', lambda e: e.max_index(out=topi[:, hp, 0:8], in_max=top[:, hp, 0:8], in_values=sc[:, blk]),
                                         reads=[scn, T], writes=[TI]),
                            lambda: K.op('dve', lambda e: e.match_replace(out=sc2[:, blk], in_to_replace=top[:, hp, 0:8],
                                                                          in_values=sc[:, blk], imm_value=NEG), reads=[scn, T], writes=[S2]),
                            lambda: K.op('dve', lambda e: e.max(out=top[:, hp, 8:16], in_=sc2[:, blk]), reads=[S2], writes=[T]),
                            lambda: K.op('dve', lambda e: e.max_index(out=topi[:, hp, 8:16], in_max=top[:, hp, 8:16], in_values=sc2[:, blk]),
                                         reads=[S2, T], writes=[TI]),
                        ]
                    for hp0 in range(0, 16, 4):
                        chains = [grp_steps(hp0 + x) for x in range(4)]
                        for st in range(5):
                            for c in chains:
                                c[st]()
                    ALLT = ['top_%d' % x for x in range(16)]
                    ALLTI = ['topi_%d' % x for x in range(16)]
                    K.op('dve', lambda e: e.tensor_copy(out=topf[:], in_=topi[:]), reads=ALLTI, writes=['topf'])
                    top4 = top[:].rearrange("p (h two) k -> p h two k", two=2)
                    topf4 = topf[:].rearrange("p (h two) k -> p h two k", two=2)
                    K.op('dve', lambda e: e.tensor_tensor(out=cand[:].rearrange("p h (a b) -> p h a b", b=16),
                                                          in0=top4[:, :, 0, :].unsqueeze(3).to_broadcast([128, 8, 16, 16]),
                                                          in1=top4[:, :, 1, :].unsqueeze(2).to_broadcast([128, 8, 16, 16]), op=ALU.add),
                         reads=ALLT, writes=['cand'])
                    K.op('dve', lambda e: e.tensor_scalar(out=i128[:], in0=topf4[:, :, 0, :], scalar1=128.0, scalar2=None, op0=ALU.mult),
                         reads=['topf'], writes=['i128'])

                    def head_steps(h):
                        BE, PO = 'best_%d' % h, 'pos_%d' % h
                        S2 = ['sc2_%d' % (2 * h), 'sc2_%d' % (2 * h + 1)]
                        return [
                            lambda: K.op('dve', lambda e: e.max(out=best[:, h, 0:8], in_=cand[:, h, :]), reads=['cand'], writes=[BE]),
                            lambda: K.op('dve', lambda e: e.max_index(out=pos[:, h, 0:8], in_max=best[:, h, 0:8], in_values=cand[:, h, :]),
                                         reads=['cand', BE], writes=[PO]),
                            lambda: K.op('dve', lambda e: e.match_replace(out=cand2[:, h, :], in_to_replace=best[:, h, 0:8], in_values=cand[:, h, :],
                                                                          imm_value=NEG), reads=['cand', BE], writes=S2),
                            lambda: K.op('dve', lambda e: e.max(out=best[:, h, 8:16], in_=cand2[:, h, :]), reads=S2, writes=[BE]),
                            lambda: K.op('dve', lambda e: e.max_index(out=pos[:, h, 8:16], in_max=best[:, h, 8:16], in_values=cand2[:, h, :]),
                                         reads=S2 + [BE], writes=[PO]),
                        ]
                    for h0 in range(0, 8, 4):
                        chains = [head_steps(h0 + x) for x in range(4)]
                        for st in range(5):
                            for c in chains:
                                c[st]()
                    ALLB = ['best_%d' % x for x in range(8)]
                    ALLP = ['pos_%d' % x for x in range(8)]
                    K.op('dve', lambda e: e.tensor_single_scalar(out=a_u[:], in_=pos[:], scalar=4, op=ALU.logical_shift_right),
                         reads=ALLP, writes=['a_u'])
                    K.op('dve', lambda e: e.tensor_single_scalar(out=b_u[:], in_=pos[:], scalar=15, op=ALU.bitwise_and),
                         reads=ALLP, writes=['b_u'])
                    K.op('dve', lambda e: e.tensor_copy(out=a_f[:], in_=a_u[:]), reads=['a_u'], writes=['a_f'])
                    K.op('dve', lambda e: e.tensor_copy(out=b_f[:], in_=b_u[:]), reads=['b_u'], writes=['b_f'])
                    io4 = iota16[:].unsqueeze(1).unsqueeze(1).to_broadcast([128, 8, 16, 16])
                    for (xf, xn, src, srcn, dstt, dn) in ((a_f, 'a_f', i128[:], 'i128', isel, 'isel'),
                                                          (b_f, 'b_f', topf4[:, :, 1, :], 'topf', jsel, 'jsel')):
                        K.op('dve', lambda e, xf=xf: e.tensor_tensor(out=eq[:], in0=xf[:].unsqueeze(3).to_broadcast([128, 8, 16, 16]),
                                                                     in1=io4, op=ALU.is_equal), reads=[xn, 'iota16'],
                             writes=['eq', 'sc0', 'sc1', 'sc2', 'sc3'])
                        K.op('dve', lambda e, src=src: e.tensor_tensor(out=eq[:], in0=eq[:], in1=src.unsqueeze(2).to_broadcast([128, 8, 16, 16]),
                                                                       op=ALU.mult), reads=['eq', srcn], writes=['eq'])
                        K.op('dve', lambda e, dstt=dstt: e.tensor_reduce(out=dstt[:], in_=eq[:], axis=AX.X, op=ALU.add),
                             reads=['eq'], writes=[dn])
                    K.op('dve', lambda e: e.scalar_tensor_tensor(out=isel[:], in0=isel[:], scalar=float(l * NE), in1=jsel[:],
                                                                 op0=ALU.add, op1=ALU.add), reads=['isel', 'jsel'], writes=['isel'])
                    K.op('dve', lambda e: e.tensor_copy(out=eidx[:].rearrange("p (h k) -> p h k", k=16), in_=isel[:]), reads=['isel'], writes=['eidx'])
                    K.op('dve', lambda e: e.tensor_tensor(out=ge[:], in0=best[:], in1=best[:, :, 0:1].to_broadcast([128, 8, 16]), op=ALU.subtract),
                         reads=ALLB, writes=['ge'])
                    K.op('act', lambda e: e.activation(out=ge[:], in_=ge[:], func=AF.Exp), reads=['ge'], writes=['ge'])
                    K.op('dve', lambda e: e.tensor_reduce(out=gs[:], in_=ge[:], axis=AX.X, op=ALU.add), reads=['ge'], writes=['gs'])
                    K.op('dve', lambda e: e.reciprocal(out=gs[:], in_=gs[:]), reads=['gs'], writes=['gs'])
                    K.op('dve', lambda e: e.tensor_tensor(out=ge[:], in0=ge[:], in1=gs[:].unsqueeze(2).to_broadcast([128, 8, 16]), op=ALU.mult),
                         reads=['ge', 'gs'], writes=['ge'])
                    ge2 = ge[:].rearrange("p h k -> p (h k)")
                    def dots(g):
                        for ei in range(g * 8, g * 8 + 8):
                            b = ei % NBT
                            K.dma('pool', 'uvb%d' % b, lambda e, ei=ei, b=b: e.indirect_dma_start(
                                out=uvb[b][:], out_offset=None, in_=tuv,
                                in_offset=bass.IndirectOffsetOnAxis(ap=eidx[:, ei:ei + 1], axis=0), bounds_check=bounds_reg, oob_is_err=False),
                                reads=['eidx'], writes=['uvb%d' % b] + (uvb_extra_names[b - NB] if b >= NB else []))
                            K.op('dve', lambda e, ei=ei, b=b: e.scalar_tensor_tensor(out=W['junk'][:], in0=uvb[b][:, 0:D], scalar=1.0, in1=hn_ps[:],
                                                                                     op0=ALU.mult, op1=ALU.mult, accum_out=actv[:, ei:ei + 1]),
                                 reads=['uvb%d' % b, 'hn_ps'], writes=['actv_e%d' % ei])
                        gs8 = slice(g * 8, g * 8 + 8)
                        K.op('act', lambda e: e.activation(out=wgt[:, gs8], in_=actv[:, gs8], func=AF.Gelu_apprx_tanh),
                             reads=['actv_e%d' % x for x in range(g * 8, g * 8 + 8)], writes=['wgt%d' % g])

                    def finish(g):
                        gs8 = slice(g * 8, g * 8 + 8)
                        K.op('dve', lambda e: e.tensor_tensor(out=wgt[:, gs8], in0=wgt[:, gs8], in1=ge2[:, gs8], op=ALU.mult),
                             reads=['wgt%d' % g, 'ge'], writes=['wgt%d' % g])
                        dgt = dg[g % 2]
                        dgn = 'dg%d' % (g % 2)
                        K.op('dve', lambda e: e.tensor_tensor(
                            out=dgt[:], in0=ident_f[:].unsqueeze(1).to_broadcast([128, 8, 128]),
                            in1=wgt[:, gs8].unsqueeze(2).to_broadcast([128, 8, 128]), op=ALU.mult),
                            reads=['wgt%d' % g, 'ident_f'], writes=[dgn])
                        for ei in range(g * 8, g * 8 + 8):
                            b = ei % NBT
                            for hh in range(2):
                                K.op('pe', lambda e, ei=ei, b=b, hh=hh: e.matmul(
                                    oacc[:, hh * 512:(hh + 1) * 512], lhsT=dgt[:, ei % 8, :], rhs=uvb[b][:, D + hh * 512:D + (hh + 1) * 512],
                                    start=(ei == 0), stop=(ei == 127)),
                                    reads=[dgn, 'uvb%d' % b], writes=['pz%d' % hh])

                    for g in range(16):
                        dots(g)
                        finish(g)
                    K.op('dve', lambda e: e.tensor_tensor(out=xt[:], in0=oacc[:], in1=xt[:], op=ALU.add), reads=['xt', 'pz0', 'pz1'], writes=['xt'])
                    if last:
                        rmsnorm(W, xt, gfin, out_f32=(hn, 'hn'))
                        if sample:
                            K.dma('sp', 'hn', lambda e: e.dma_start(out=ys[:, :], in_=hn[0:32, :]), reads=['hn'])
                        else:
                            K.dma('sp', 'hn', lambda e: e.dma_start(out=yp[bass.ds(i * 128, 128), :], in_=hn[:]), reads=['hn'])
                    else:
                        if sample:
                            K.dma('sp', 'xt', lambda e: e.dma_start(out=xres[NT * 128:(NT + 1) * 128, :], in_=xt[:]), reads=['xt'])
                        else:
                            K.dma('sp', 'xt', lambda e: e.dma_start(out=xres[bass.ds(i * 128, 128), :], in_=xt[:]), reads=['xt'])

                run_tiles(body)

        def phase_convert_tables():
            R = 4
            nrow = 4 * NE
            with contextlib.ExitStack() as ph:
                cin = [sb(ph, "cv_in%d" % t, [128, R, D]) for t in range(2)]
                cout = [sb(ph, "cv_out%d" % t, [128, R, D], BF16) for t in range(2)]
                srcs = [peer_u.rearrange("l e d -> (l e) d"), peer_v.rearrange("l e d -> (l e) d")]
                dsts = [tuv[:, 0:D], tuv[:, D:2 * D]]
                K.barrier_reset()
                with nc.Fori(0, nrow // (128 * R)) as i:
                    for t in range(2):
                        K.dma('sp', 'cin%d' % t, lambda e, t=t: e.dma_start(
                            out=cin[t][:], in_=srcs[t][bass.ds(i * (128 * R), 128 * R), :].rearrange("(p r) d -> p r d", r=R)),
                            writes=['cin%d' % t])
                    K.op('act', lambda e: e.copy(out=cout[0][:], in_=cin[0][:]), reads=['cin0'], writes=['cout0'])
                    K.op('dve', lambda e: e.tensor_copy(out=cout[1][:], in_=cin[1][:]), reads=['cin1'], writes=['cout1'])
                    for t in range(2):
                        K.dma('sp', 'cout%d' % t, lambda e, t=t: e.dma_start(
                            out=dsts[t][bass.ds(i * (128 * R), 128 * R), :].rearrange("(p r) d -> p r d", r=R), in_=cout[t][:]),
                            reads=['cout%d' % t])
                    K.barrier_reset()

        phases = []
        for l in range(4):
            if l % 2 == 0:
                phases.append(lambda l=l: phase_even(l, first=(l == 0)))
            else:
                phases.append(lambda l=l: phase_odd(l))
            phases.append(lambda l=l: phase_peer(l, last=(l == 3)))
        if only == 'odd':
            phase_odd(1, dbg_src=True)
        else:
            if n_phases >= 2:
                phase_convert_tables()
            for pi, p in enumerate(phases):
                if pi < n_phases:
                    p()
        if n_phases < 8:
            with contextlib.ExitStack() as ph:
                xt = sb(ph, "dbg_xt", [128, D])
                K.barrier_reset()
                for t in range(NT + 1):
                    K.dma('sp', 'dbg_xt', lambda e, t=t: e.dma_start(out=xt[:], in_=xres[t * 128:(t + 1) * 128, :]), writes=['dbg'])
                    if t < NT:
                        K.dma('sp', 'dbg_xt', lambda e, t=t: e.dma_start(out=yp[t * 128:(t + 1) * 128, :], in_=xt[:]), reads=['dbg'])
                    else:
                        K.dma('sp', 'dbg_xt', lambda e, t=t: e.dma_start(out=ys[:, :], in_=xt[0:32, :]), reads=['dbg'])
                K.barrier_reset()
    return nc


_W_NAMES = ["norm_mix", "norm_ffn", "ev_w_in", "ev_a_ln_g", "ev_a_ln_b", "ev_ws", "ev_bs", "ev_conv_w", "ev_conv_b",
            "ev_b_ln_g", "ev_b_ln_b", "ev_w_out", "od_w_in", "od_lower", "od_norm_g", "od_w_out", "peer_wq", "peer_keys",
            "peer_u", "peer_v"]


def run(inputs, n_phases=8, only=None, NE=16384, stop_at=99):
    x_prompt = np.asarray(inputs["x_prompt"], dtype=np.float32)
    x_sample = np.asarray(inputs["x_sample"], dtype=np.float32)
    B, T, _ = x_prompt.shape
    NT = T // 128
    nc = build_program(NT, n_phases, only=only, NE=NE, stop_at=stop_at)
    shared = {k: np.ascontiguousarray(np.asarray(inputs[k], dtype=np.float32)) for k in _W_NAMES}
    shared["norm_final"] = np.ascontiguousarray(np.asarray(inputs["norm_final"], dtype=np.float32).reshape(1, D))
    st_conv = np.asarray(inputs["state_conv"], dtype=np.float32)
    st_hgrn = np.asarray(inputs["state_hgrn"], dtype=np.float32)
    in_maps = []
    for c in range(8):
        m = dict(shared)
        m["xp"] = np.ascontiguousarray(x_prompt[c % B])
        m["xs"] = np.ascontiguousarray(x_sample[c])
        m["st_conv"] = np.ascontiguousarray(st_conv[:, c])
        m["st_hgrn"] = np.ascontiguousarray(st_hgrn[:, c])
        in_maps.append(m)
    res = run_bass_kernel_spmd(nc, in_maps, core_ids=list(range(8)))
    R = res.results
    y_prompt = np.stack([R[b]["yp"] for b in range(B)]).astype(np.float32)
    y_sample = np.stack([R[c]["ys"] for c in range(8)]).astype(np.float32)
    conv_prompt = np.stack([R[b]["conv_p"] for b in range(B)], axis=1).astype(np.float32)
    conv_sample = np.stack([R[c]["conv_s"] for c in range(8)], axis=1).astype(np.float32)
    hgrn_prompt = np.stack([R[b]["hg_p"] for b in range(B)], axis=1).astype(np.float32)
    hgrn_sample = np.stack([R[c]["hg_s"] for c in range(8)], axis=1).astype(np.float32)
    gmlp_v_sample = np.stack([R[c]["v_s"] for c in range(8)], axis=1).astype(np.float32)
    return (y_prompt, y_sample, conv_prompt, conv_sample, hgrn_prompt, hgrn_sample, gmlp_v_sample)


def kernel(**inputs):
    return run(inputs)
```

```python
import contextlib
import numpy as np
import concourse.bass as bass
import concourse.mybir as mybir
from concourse.bass_utils import run_bass_kernel_spmd

F32 = mybir.dt.float32
BF16 = mybir.dt.bfloat16
I32 = mybir.dt.int32
U32 = mybir.dt.uint32
AF = mybir.ActivationFunctionType
ALU = mybir.AluOpType
AX = mybir.AxisListType

D = 1024
EPS = 1e-6
NEG = -1.0e30


class Ctx:
    def __init__(self, nc, es):
        self.nc = nc
        self.es = es
        self.eng = {'pe': nc.tensor, 'act': nc.scalar, 'dve': nc.vector, 'pool': nc.gpsimd, 'sp': nc.sync}
        self.sem = {k: es.enter_context(nc.semaphore("sem_" + k)) for k in self.eng}
        self.dsems = {}
        self.dcnt = {}
        self.reset_state()

    def reset_state(self):
        self.cnt = {k: 0 for k in self.eng}
        for k in self.dcnt:
            self.dcnt[k] = 0
        self.waited = {k: {} for k in self.eng}
        self.lastw = {}
        self.readers = {}

    def _sem_of(self, semkey):
        return self.sem[semkey] if isinstance(semkey, str) else self.dsems[semkey[1]]

    def _wait(self, e, tok):
        semkey, val = tok
        if semkey == 'pe' and e == 'pe':
            return
        w = self.waited[e]
        if w.get(semkey, 0) >= val:
            return
        self.eng[e].wait_ge(self._sem_of(semkey), val)
        w[semkey] = val

    def _deps(self, e, reads, writes):
        for b in reads:
            if b in self.lastw:
                self._wait(e, self.lastw[b])
        for b in writes:
            if b in self.lastw:
                self._wait(e, self.lastw[b])
            for t in self.readers.get(b, ()):
                self._wait(e, t)

    def _record(self, tok, reads, writes):
        for b in reads:
            self.readers.setdefault(b, []).append(tok)
        for b in writes:
            self.lastw[b] = tok
            self.readers[b] = []

    def op(self, e, fn, reads=(), writes=()):
        self._deps(e, reads, writes)
        ins = fn(self.eng[e])
        self.cnt[e] += 1
        ins.then_inc(self.sem[e], 1)
        self._record((e, self.cnt[e]), reads, writes)

    def dma(self, q, key, fn, reads=(), writes=()):
        if key not in self.dsems:
            self.dsems[key] = self.es.enter_context(self.nc.semaphore("d_" + key))
            self.dcnt[key] = 0
        self._deps(q, reads, writes)
        ins = fn(self.eng[q])
        self.dcnt[key] += 16
        ins.then_inc(self.dsems[key], 16)
        self._record((('d', key), self.dcnt[key]), reads, writes)

    def barrier_reset(self):
        for key, c in self.dcnt.items():
            if c > 0:
                self._wait('sp', (('d', key), c))
        self.nc.all_engine_barrier()
        self.nc.gpsimd.dma_reset()
        for s in list(self.sem.values()) + list(self.dsems.values()):
            self.nc.gpsimd.sem_clear(s)
        self.nc.all_engine_barrier()
        self.reset_state()


def build_program(NT, n_phases=8, only=None, NE=16384, stop_at=99):
    nc = bass.Bass("TRN2", target_bir_lowering=False)

    def din(name, shape, dt=F32):
        return nc.dram_tensor(name, list(shape), dt, kind="ExternalInput").ap()

    def dout(name, shape, dt=F32):
        return nc.dram_tensor(name, list(shape), dt, kind="ExternalOutput").ap()

    xp = din("xp", [NT * 128, D])
    xs = din("xs", [32, D])
    st_conv = din("st_conv", [2, 30, 512])
    st_hgrn = din("st_hgrn", [2, 8, 128, 128])
    norm_mix = din("norm_mix", [4, D])
    norm_ffn = din("norm_ffn", [4, D])
    norm_final = din("norm_final", [1, D])
    ev_w_in = din("ev_w_in", [2, D, 2048])
    ev_a_ln_g = din("ev_a_ln_g", [2, 512])
    ev_a_ln_b = din("ev_a_ln_b", [2, 512])
    ev_ws = din("ev_ws", [2, 4, 128, 128])
    ev_bs = din("ev_bs", [2, 4, 128])
    ev_conv_w = din("ev_conv_w", [2, 31, 512])
    ev_conv_b = din("ev_conv_b", [2, 512])
    ev_b_ln_g = din("ev_b_ln_g", [2, 512])
    ev_b_ln_b = din("ev_b_ln_b", [2, 512])
    ev_w_out = din("ev_w_out", [2, D, D])
    od_w_in = din("od_w_in", [2, D, 4096])
    od_lower = din("od_lower", [4, D])
    od_norm_g = din("od_norm_g", [2, 128])
    od_w_out = din("od_w_out", [2, D, D])
    peer_wq = din("peer_wq", [4, D, 2048])
    peer_keys = din("peer_keys", [4, 8, 2, 128, 128])
    peer_u = din("peer_u", [4, NE, D])
    peer_v = din("peer_v", [4, NE, D])

    yp = dout("yp", [NT * 128, D])
    ys = dout("ys", [32, D])
    conv_p = dout("conv_p", [2, 30, 512])
    conv_s = dout("conv_s", [2, 30, 512])
    hg_p = dout("hg_p", [2, 8, 128, 128])
    hg_s = dout("hg_s", [2, 8, 128, 128])
    v_s = dout("v_s", [2, 32, 512])

    xres = nc.dram_tensor("xres", [(NT + 1) * 128, D], F32, kind="Internal").ap()
    tuv = nc.dram_tensor("tuv", [4 * NE, 2 * D], BF16, kind="Internal").ap()

    with contextlib.ExitStack() as es:
        K = Ctx(nc, es)

        uniq = [0]

        def sb(stack, name, shape, dt=F32):
            uniq[0] += 1
            return stack.enter_context(nc.sbuf_tensor("%s_%d" % (name, uniq[0]), list(shape), dt))

        def ps(stack, name, shape, dt=F32):
            uniq[0] += 1
            return stack.enter_context(nc.psum_tensor("%s_%d" % (name, uniq[0]), list(shape), dt))

        pidx_i = sb(es, "pidx_i", [128, 1], I32)
        jidx_i = sb(es, "jidx_i", [128, 128], I32)
        P_f = sb(es, "P_f", [128, 1])
        J_f = sb(es, "J_f", [128, 128])
        ident_f = sb(es, "ident_f", [128, 128])
        ident_bf = sb(es, "ident_bf", [128, 128], BF16)
        eps_t = sb(es, "eps_t", [128, 1])
        iota16 = sb(es, "iota16", [128, 16])
        K.op('pool', lambda e: e.iota(pidx_i[:], pattern=[[0, 1]], base=0, channel_multiplier=1), writes=['pidx_i'])
        K.op('pool', lambda e: e.iota(jidx_i[:], pattern=[[1, 128]], base=0, channel_multiplier=0), writes=['jidx_i'])
        K.op('dve', lambda e: e.tensor_copy(out=P_f[:], in_=pidx_i[:]), reads=['pidx_i'], writes=['P_f'])
        K.op('dve', lambda e: e.tensor_copy(out=J_f[:], in_=jidx_i[:]), reads=['jidx_i'], writes=['J_f'])
        K.op('dve', lambda e: e.tensor_copy(out=iota16[:], in_=jidx_i[:, 0:16]), reads=['jidx_i'], writes=['iota16'])
        K.op('dve', lambda e: e.tensor_scalar(out=ident_f[:], in0=J_f[:], scalar1=P_f[:, 0:1], scalar2=None,
                                              op0=ALU.is_equal), reads=['J_f', 'P_f'], writes=['ident_f'])
        K.op('dve', lambda e: e.tensor_copy(out=ident_bf[:], in_=ident_f[:]), reads=['ident_f'], writes=['ident_bf'])
        K.op('dve', lambda e: e.memset(eps_t[:], EPS), writes=['eps_t'])
        bounds_reg = nc.gpsimd.to_reg(4 * NE - 1)

        def rmsnorm(W, xt, gb, out_f32=None, out_bf=None, tag=""):
            K.op('act', lambda e: e.activation(out=W['junk'][:], in_=xt[:], func=AF.Square, accum_out=W['ss'][:, 0:1]),
                 reads=['xt'], writes=['junk', 'ss'])
            K.op('act', lambda e: e.activation(out=W['ss'][:, 1:2], in_=W['ss'][:, 0:1], func=AF.Sqrt,
                                               scale=1.0 / D, bias=eps_t[:, 0:1]), reads=['ss'], writes=['ss1'])
            K.op('dve', lambda e: e.reciprocal(out=W['ss'][:, 2:3], in_=W['ss'][:, 1:2]), reads=['ss1'], writes=['ss2'])
            if out_f32 is not None:
                K.op('dve', lambda e: e.scalar_tensor_tensor(out=out_f32[0][:], in0=xt[:], scalar=W['ss'][:, 2:3], in1=gb[:],
                                                             op0=ALU.mult, op1=ALU.mult),
                     reads=['xt', 'ss2'], writes=[out_f32[1]])
                if out_bf is not None:
                    K.op('act', lambda e: e.copy(out=out_bf[0][:], in_=out_f32[0][:]), reads=[out_f32[1]], writes=[out_bf[1]])
            else:
                K.op('dve', lambda e: e.scalar_tensor_tensor(out=out_bf[0][:], in0=xt[:], scalar=W['ss'][:, 2:3], in1=gb[:],
                                                             op0=ALU.mult, op1=ALU.mult),
                     reads=['xt', 'ss2'], writes=[out_bf[1]])

        def transpose8(src_bf, src_name, ptp, dstT, dst_name, nblk=8, src_off=0):
            for k in range(nblk):
                K.op('pe', lambda e, k=k: e.transpose(out=ptp[:, k, :], in_=src_bf[:, (src_off + k) * 128:(src_off + k + 1) * 128],
                                                      identity=ident_bf[:]),
                     reads=[src_name, 'ident_bf'], writes=['ptp'])
            K.op('act', lambda e: e.copy(out=dstT[:, 0:nblk, :], in_=ptp[:, 0:nblk, :]), reads=['ptp'], writes=[dst_name])

        def load_w_bf16(dst, name, src2d, N):
            srcv = src2d.rearrange("(k p) n -> p k n", p=128)
            for n0 in range(0, N, 512):
                K.dma('pool', name, lambda e, n0=n0: e.dma_start(out=dst[:, :, n0:n0 + 512], in_=srcv[:, :, n0:n0 + 512]),
                      writes=[name])

        def bcast_load(dst, name, src_row):
            K.dma('sp', name, lambda e: e.dma_start(out=dst[:], in_=src_row.partition_broadcast(128)), writes=[name])

        def run_tiles(body):
            K.barrier_reset()
            with nc.Fori(0, NT) as i:
                body(i, False)
                K.barrier_reset()
            body(None, True)
            K.barrier_reset()

        def phase_even(l, first):
            j = l // 2
            with contextlib.ExitStack() as ph:
                w_in = sb(ph, "e_w_in", [128, 8, 2048], BF16)
                w_out = sb(ph, "e_w_out", [128, 8, 1024], BF16)
                gmix = sb(ph, "e_gmix", [128, D])
                ag = sb(ph, "e_ag", [128, 512]); ab = sb(ph, "e_ab", [128, 512])
                bg = sb(ph, "e_bg", [128, 512]); bb = sb(ph, "e_bb", [128, 512])
                wgT = sb(ph, "e_wgT", [128, 4, 128], BF16)
                bs_t = sb(ph, "e_bs_t", [128, 4])
                cw = sb(ph, "e_cw", [128, 4, 32])
                cb = sb(ph, "e_cb", [128, 4])
                nat = sb(ph, "e_nat", [128, 512])
                nat_bf = sb(ph, "e_nat_bf", [128, 128], BF16)
                trilm = sb(ph, "e_trilm", [128, 128])
                W = dict(junk=sb(ph, "e_junk", [128, D]), ss=sb(ph, "e_ss", [128, 4]))
                xt = sb(ph, "e_xt", [128, D])
                hn_bf = sb(ph, "e_hn_bf", [128, D], BF16)
                hnT = sb(ph, "e_hnT", [128, 8, 128], BF16)
                u = sb(ph, "e_u", [128, 512]); vpre = sb(ph, "e_vpre", [128, 512])
                v = sb(ph, "e_v", [128, 512]); v_bf = sb(ph, "e_v_bf", [128, 512], BF16)
                sig = sb(ph, "e_sig", [128, 512]); glu = sb(ph, "e_glu", [128, 512])
                gbuf = sb(ph, "e_gbuf", [128, 4, 160])
                acc = sb(ph, "e_acc", [128, 4, 128])
                cvn = sb(ph, "e_cvn", [128, 512])
                cat_bf = sb(ph, "e_cat_bf", [128, D], BF16)
                catT = sb(ph, "e_catT", [128, 8, 128], BF16)
                st6 = sb(ph, "e_st6", [128, 6]); mv = sb(ph, "e_mv", [128, 4])
                st6b = sb(ph, "e_st6b", [128, 6]); mvb = sb(ph, "e_mvb", [128, 4])
                hal = sb(ph, "e_hal", [128, 512])
                pz = [ps(ph, "e_pz%d" % b, [128, 512]) for b in range(4)]
                ptp = ps(ph, "e_ptp", [128, 8, 128], BF16)
                pmx = ps(ph, "e_pmx", [128, 512])
                ptf = ps(ph, "e_ptf", [128, 4, 128])
                pcv = ps(ph, "e_pcv", [128, 512])

                load_w_bf16(w_in, 'w_in', ev_w_in[j], 2048)
                load_w_bf16(w_out, 'w_out', ev_w_out[j], 1024)
                bcast_load(gmix, 'gmix', norm_mix[l])
                bcast_load(ag, 'ag', ev_a_ln_g[j]); bcast_load(ab, 'ab', ev_a_ln_b[j])
                bcast_load(bg, 'bg', ev_b_ln_g[j]); bcast_load(bb, 'bb', ev_b_ln_b[j])
                K.op('dve', lambda e: e.tensor_scalar(out=trilm[:], in0=J_f[:], scalar1=P_f[:, 0:1], scalar2=None,
                                                      op0=ALU.is_le), reads=['J_f', 'P_f'], writes=['trilm'])
                for g in range(4):
                    K.dma('sp', 'nat', lambda e, g=g: e.dma_start(out=nat[:, 0:128], in_=ev_ws[j, g]), writes=['nat'])
                    K.op('dve', lambda e: e.tensor_tensor(out=nat_bf[:], in0=nat[:, 0:128], in1=trilm[:], op=ALU.mult),
                         reads=['nat', 'trilm'], writes=['nat_bf'])
                    K.op('pe', lambda e: e.transpose(out=ptp[:, 0, :], in_=nat_bf[:], identity=ident_bf[:]),
                         reads=['nat_bf', 'ident_bf'], writes=['ptp'])
                    K.op('act', lambda e, g=g: e.copy(out=wgT[:, g, :], in_=ptp[:, 0, :]), reads=['ptp'], writes=['wgT'])
                K.dma('sp', 'nat', lambda e: e.dma_start(out=nat[0:4, 0:128], in_=ev_bs[j]), writes=['nat'])
                K.op('pe', lambda e: e.transpose(out=ptf[:, 0, 0:4], in_=nat[0:4, 0:128], identity=ident_f[0:4, 0:4]),
                     reads=['nat', 'ident_f'], writes=['ptf'])
                K.op('act', lambda e: e.copy(out=bs_t[:], in_=ptf[:, 0, 0:4]), reads=['ptf'], writes=['bs_t'])
                K.dma('sp', 'nat', lambda e: e.dma_start(out=nat[0:4, 0:128],
                                                         in_=ev_conv_b[j].rearrange("(ch p) -> ch p", p=128)), writes=['nat'])
                K.op('pe', lambda e: e.transpose(out=ptf[:, 0, 0:4], in_=nat[0:4, 0:128], identity=ident_f[0:4, 0:4]),
                     reads=['nat', 'ident_f'], writes=['ptf'])
                K.op('act', lambda e: e.copy(out=cb[:], in_=ptf[:, 0, 0:4]), reads=['ptf'], writes=['cb'])
                K.dma('sp', 'nat', lambda e: e.dma_start(out=nat[0:31, :], in_=ev_conv_w[j]), writes=['nat'])
                for ch in range(4):
                    K.op('pe', lambda e, ch=ch: e.transpose(out=ptf[:, ch, 0:31], in_=nat[0:31, ch * 128:(ch + 1) * 128],
                                                            identity=ident_f[0:31, 0:31]),
                         reads=['nat', 'ident_f'], writes=['ptf'])
                K.op('act', lambda e: e.copy(out=cw[:, :, 0:31], in_=ptf[:, :, 0:31]), reads=['ptf'], writes=['cw'])
                K.op('dve', lambda e: e.memset(gbuf[:], 0.0), writes=['gbuf'])

                def body(i, sample):
                    if sample:
                        src = (xs[:, :] if first else xres[NT * 128:NT * 128 + 32, :])
                        if first:
                            K.op('dve', lambda e: e.memset(xt[:], 0.0), writes=['xt'])
                        K.dma('sp', 'xt', lambda e: e.dma_start(out=xt[0:32, :] if first else xt[:], in_=src if first else xres[NT * 128:(NT + 1) * 128, :]),
                              writes=['xt'])
                        K.dma('sp', 'hal', lambda e: e.dma_start(out=hal[0:30, :], in_=st_conv[j]), writes=['hal'])
                        for ch in range(4):
                            K.op('pe', lambda e, ch=ch: e.transpose(out=ptf[:, ch, 0:30], in_=hal[0:30, ch * 128:(ch + 1) * 128],
                                                                    identity=ident_f[0:30, 0:30]),
                                 reads=['hal', 'ident_f'], writes=['ptf'])
                        K.op('act', lambda e: e.copy(out=gbuf[:, :, 0:30], in_=ptf[:, :, 0:30]), reads=['ptf'], writes=['gbuf'])
                        dst = xres[NT * 128:(NT + 1) * 128, :]
                    else:
                        srcT = xp if first else xres
                        K.dma('sp', 'xt', lambda e: e.dma_start(out=xt[:], in_=srcT[bass.ds(i * 128, 128), :]), writes=['xt'])
                        dst = xres[bass.ds(i * 128, 128), :]
                    rmsnorm(W, xt, gmix, out_bf=(hn_bf, 'hn_bf'))
                    transpose8(hn_bf, 'hn_bf', ptp, hnT, 'hnT')
                    for nb in range(4):
                        for k in range(8):
                            K.op('pe', lambda e, nb=nb, k=k: e.matmul(pz[nb][:], lhsT=hnT[:, k, :], rhs=w_in[:, k, nb * 512:(nb + 1) * 512],
                                                                      start=(k == 0), stop=(k == 7)),
                                 reads=['hnT', 'w_in'], writes=['pz%d' % nb])
                    K.op('act', lambda e: e.activation(out=u[:], in_=pz[0][:], func=AF.Gelu_apprx_tanh), reads=['pz0'], writes=['u'])
                    K.op('act', lambda e: e.activation(out=vpre[:], in_=pz[1][:], func=AF.Gelu_apprx_tanh), reads=['pz1'], writes=['vpre'])
                    K.op('act', lambda e: e.activation(out=sig[:], in_=pz[3][:], func=AF.Sigmoid), reads=['pz3'], writes=['sig'])
                    K.op('dve', lambda e: e.tensor_tensor(out=glu[:], in0=pz[2][:], in1=sig[:], op=ALU.mult),
                         reads=['pz2', 'sig'], writes=['glu'])
                    K.op('dve', lambda e: e.bn_stats(out=st6[:], in_=vpre[:]), reads=['vpre'], writes=['st6'])
                    K.op('dve', lambda e: e.bn_aggr(out=mv[:, 0:2], in_=st6[:]), reads=['st6'], writes=['mv'])
                    K.op('act', lambda e: e.activation(out=mv[:, 2:3], in_=mv[:, 1:2], func=AF.Sqrt, bias=eps_t[:, 0:1]),
                         reads=['mv'], writes=['mv2'])
                    K.op('dve', lambda e: e.reciprocal(out=mv[:, 3:4], in_=mv[:, 2:3]), reads=['mv2'], writes=['mv3'])
                    K.op('dve', lambda e: e.tensor_scalar(out=v[:], in0=vpre[:], scalar1=mv[:, 0:1], scalar2=mv[:, 3:4],
                                                          op0=ALU.subtract, op1=ALU.mult), reads=['vpre', 'mv', 'mv3'], writes=['v'])
                    K.op('dve', lambda e: e.tensor_tensor(out=v[:], in0=v[:], in1=ag[:], op=ALU.mult), reads=['v', 'ag'], writes=['v'])
                    K.op('dve', lambda e: e.tensor_tensor(out=v[:], in0=v[:], in1=ab[:], op=ALU.add), reads=['v', 'ab'], writes=['v'])
                    K.op('act', lambda e: e.copy(out=v_bf[:], in_=v[:]), reads=['v'], writes=['v_bf'])
                    if sample:
                        K.dma('sp', 'v', lambda e: e.dma_start(out=v_s[j], in_=v[0:32, :]), reads=['v'])
                    for g in range(4):
                        K.op('pe', lambda e, g=g: e.matmul(pmx[:, g * 128:(g + 1) * 128], lhsT=wgT[:, g, :],
                                                           rhs=v_bf[:, g * 128:(g + 1) * 128], start=True, stop=True),
                             reads=['wgT', 'v_bf'], writes=['pmx'])
                    for g in range(4):
                        K.op('dve', lambda e, g=g: e.scalar_tensor_tensor(out=cat_bf[:, g * 128:(g + 1) * 128],
                                                                          in0=pmx[:, g * 128:(g + 1) * 128], scalar=bs_t[:, g:g + 1],
                                                                          in1=u[:, g * 128:(g + 1) * 128], op0=ALU.add, op1=ALU.mult),
                             reads=['pmx', 'bs_t', 'u'], writes=['cat_a'])
                    if sample:
                        K.dma('sp', 'glu', lambda e: e.dma_start(out=conv_s[j], in_=glu[2:32, :]), reads=['glu'])
                    for ch in range(4):
                        K.op('pe', lambda e, ch=ch: e.transpose(out=ptf[:, ch, :], in_=glu[:, ch * 128:(ch + 1) * 128], identity=ident_f[:]),
                             reads=['glu', 'ident_f'], writes=['ptf'])
                    K.op('act', lambda e: e.copy(out=gbuf[:, :, 30:158], in_=ptf[:, :, :]), reads=['ptf'], writes=['gbuf'])
                    for ch in range(4):
                        K.op('dve', lambda e, ch=ch: e.tensor_scalar(out=acc[:, ch, :], in0=gbuf[:, ch, 0:128], scalar1=cw[:, ch, 0:1],
                                                                     scalar2=cb[:, ch:ch + 1], op0=ALU.mult, op1=ALU.add),
                             reads=['gbuf', 'cw', 'cb'], writes=['acc%d' % ch])
                    for k in range(1, 31):
                        for ch in range(4):
                            K.op('dve', lambda e, ch=ch, k=k: e.scalar_tensor_tensor(out=acc[:, ch, :], in0=gbuf[:, ch, k:k + 128],
                                                                                     scalar=cw[:, ch, k:k + 1], in1=acc[:, ch, :],
                                                                                     op0=ALU.mult, op1=ALU.add),
                                 reads=['gbuf', 'cw', 'acc%d' % ch], writes=['acc%d' % ch])
                    K.op('pool', lambda e: e.tensor_copy(out=gbuf[:, :, 0:30], in_=gbuf[:, :, 128:158]), reads=['gbuf'], writes=['gbuf'])
                    for ch in range(4):
                        K.op('pe', lambda e, ch=ch: e.transpose(out=pcv[:, ch * 128:(ch + 1) * 128], in_=acc[:, ch, :], identity=ident_f[:]),
                             reads=['acc%d' % ch, 'ident_f'], writes=['pcv'])
                    K.op('dve', lambda e: e.bn_stats(out=st6b[:], in_=pcv[:]), reads=['pcv'], writes=['st6b'])
                    K.op('dve', lambda e: e.bn_aggr(out=mvb[:, 0:2], in_=st6b[:]), reads=['st6b'], writes=['mvb'])
                    K.op('act', lambda e: e.activation(out=mvb[:, 2:3], in_=mvb[:, 1:2], func=AF.Sqrt, bias=eps_t[:, 0:1]),
                         reads=['mvb'], writes=['mvb2'])
                    K.op('dve', lambda e: e.reciprocal(out=mvb[:, 3:4], in_=mvb[:, 2:3]), reads=['mvb2'], writes=['mvb3'])
                    K.op('dve', lambda e: e.tensor_scalar(out=cvn[:], in0=pcv[:], scalar1=mvb[:, 0:1], scalar2=mvb[:, 3:4],
                                                          op0=ALU.subtract, op1=ALU.mult), reads=['pcv', 'mvb', 'mvb3'], writes=['cvn'])
                    K.op('dve', lambda e: e.tensor_tensor(out=cvn[:], in0=cvn[:], in1=bg[:], op=ALU.mult), reads=['cvn', 'bg'], writes=['cvn'])
                    K.op('dve', lambda e: e.tensor_tensor(out=cvn[:], in0=cvn[:], in1=bb[:], op=ALU.add), reads=['cvn', 'bb'], writes=['cvn'])
                    K.op('act', lambda e: e.activation(out=cat_bf[:, 512:1024], in_=cvn[:], func=AF.Silu), reads=['cvn'], writes=['cat_b'])
                    for k in range(8):
                        K.op('pe', lambda e, k=k: e.transpose(out=ptp[:, k, :], in_=cat_bf[:, k * 128:(k + 1) * 128], identity=ident_bf[:]),
                             reads=['cat_a', 'cat_b', 'ident_bf'], writes=['ptp'])
                    K.op('act', lambda e: e.copy(out=catT[:], in_=ptp[:]), reads=['ptp'], writes=['catT'])
                    for nb in range(2):
                        for k in range(8):
                            K.op('pe', lambda e, nb=nb, k=k: e.matmul(pz[nb][:], lhsT=catT[:, k, :], rhs=w_out[:, k, nb * 512:(nb + 1) * 512],
                                                                      start=(k == 0), stop=(k == 7)),
                                 reads=['catT', 'w_out'], writes=['pz%d' % nb])
                    for nb in range(2):
                        K.op('dve', lambda e, nb=nb: e.tensor_tensor(out=xt[:, nb * 512:(nb + 1) * 512], in0=pz[nb][:],
                                                                     in1=xt[:, nb * 512:(nb + 1) * 512], op=ALU.add),
                             reads=['pz%d' % nb, 'xt'], writes=['xt'])
                    K.dma('sp', 'xt', lambda e: e.dma_start(out=dst, in_=xt[:]), reads=['xt'])

                K.barrier_reset()
                with nc.Fori(0, NT) as i:
                    body(i, False)
                    K.barrier_reset()
                K.dma('sp', 'glu', lambda e: e.dma_start(out=conv_p[j], in_=glu[98:128, :]), reads=['glu'])
                body(None, True)
                K.barrier_reset()

        def phase_odd(l, dbg_src=False):
            j = l // 2
            with contextlib.ExitStack() as ph:
                w_in = sb(ph, "o_w_in", [128, 8, 4096], BF16)
                w_out = sb(ph, "o_w_out", [128, 8, 1024], BF16)
                gmix = sb(ph, "o_gmix", [128, D])
                lb_b = sb(ph, "o_lb", [128, D]); oml_b = sb(ph, "o_oml", [128, D])
                ng_b = sb(ph, "o_ng", [128, 128])
                S = sb(ph, "o_S", [128, 8, 128])
                attmask = sb(ph, "o_attmask", [128, 128]); trirel = sb(ph, "o_trirel", [128, 128])
                selc = sb(ph, "o_selc", [128, 4]); rowm = sb(ph, "o_rowm", [128, 1])
                W = dict(junk=sb(ph, "o_junk", [128, D]), ss=sb(ph, "o_ss", [128, 4]))
                xt = sb(ph, "o_xt", [128, D])
                hn_bf = sb(ph, "o_hn_bf", [128, D], BF16)
                hnT = sb(ph, "o_hnT", [128, 8, 128], BF16)
                q = sb(ph, "o_q", [128, D]); sg = sb(ph, "o_sg", [128, D])
                fg = sb(ph, "o_fg", [128, D]); logf = sb(ph, "o_logf", [128, D]); kk = sb(ph, "o_kk", [128, D])
                ep = sb(ph, "o_ep", [128, D]); em = sb(ph, "o_em", [128, D])
                v_bf = sb(ph, "o_v_bf", [128, D], BF16)
                qt_bf = sb(ph, "o_qt_bf", [128, D], BF16); kt_bf = sb(ph, "o_kt_bf", [128, D], BF16)
                ktc = [sb(ph, "o_ktc%d" % c, [128, D], BF16) for c in range(2)]
                qT = sb(ph, "o_qT", [128, 8, 128], BF16); qT0 = sb(ph, "o_qT0", [128, 8, 128], BF16)
                qT1 = sb(ph, "o_qT1", [128, 8, 128], BF16); kT = sb(ph, "o_kT", [128, 8, 128], BF16)
                attm = sb(ph, "o_attm", [128, 8, 128], BF16)
                Sp0 = sb(ph, "o_Sp0", [128, 8, 128], BF16); Sp1 = sb(ph, "o_Sp1", [128, 8, 128], BF16)
                tmpS = sb(ph, "o_tmpS", [128, 8, 128])
                fs = sb(ph, "o_fs", [128, 8, 4]); arg = sb(ph, "o_arg", [128, 8, 6]); E = sb(ph, "o_E", [128, 8, 6])
                ssq = sb(ph, "o_ssq", [128, 8]); rst8 = sb(ph, "o_rst8", [128, 8])
                o = sb(ph, "o_o", [128, D]); o_bf = sb(ph, "o_o_bf", [128, D], BF16)
                oT = sb(ph, "o_oT", [128, 8, 128], BF16)
                pA = ps(ph, "o_pA", [128, 1024]); pB = ps(ph, "o_pB", [128, 1024]); pC = ps(ph, "o_pC", [128, 1024])
                ptp = ps(ph, "o_ptp", [128, 8, 128], BF16)
                psm = ps(ph, "o_psm", [128, 512])

                load_w_bf16(w_in, 'w_in', od_w_in[j], 4096)
                load_w_bf16(w_out, 'w_out', od_w_out[j], 1024)
                bcast_load(gmix, 'gmix', norm_mix[l])
                bcast_load(ng_b, 'ng_b', od_norm_g[j])
                for r in range(4):
                    dstt = [q, sg, fg, kk][r]
                    K.dma('sp', 'lw%d' % r, lambda e, r=r, dstt=dstt: e.dma_start(out=dstt[:], in_=od_lower[r].partition_broadcast(128)),
                          writes=['lw%d' % r])
                lw = [q, sg, fg, kk]
                K.op('dve', lambda e: e.tensor_tensor(out=ep[:], in0=lw[0][:], in1=lw[1][:], op=ALU.max), reads=['lw0', 'lw1'], writes=['ep'])
                K.op('dve', lambda e: e.tensor_tensor(out=ep[:], in0=ep[:], in1=lw[2][:], op=ALU.max), reads=['ep', 'lw2'], writes=['ep'])
                K.op('dve', lambda e: e.tensor_tensor(out=ep[:], in0=ep[:], in1=lw[3][:], op=ALU.max), reads=['ep', 'lw3'], writes=['ep'])
                for r in range(4):
                    K.op('dve', lambda e, r=r: e.tensor_tensor(out=lw[r][:], in0=lw[r][:], in1=ep[:], op=ALU.subtract),
                         reads=['lw%d' % r, 'ep'], writes=['lw%d' % r])
                    K.op('act', lambda e, r=r: e.activation(out=lw[r][:], in_=lw[r][:], func=AF.Exp), reads=['lw%d' % r], writes=['lw%d' % r])
                K.op('dve', lambda e: e.tensor_tensor(out=em[:], in0=lw[0][:], in1=lw[1][:], op=ALU.add), reads=['lw0', 'lw1'], writes=['em'])
                K.op('dve', lambda e: e.tensor_tensor(out=em[:], in0=em[:], in1=lw[2][:], op=ALU.add), reads=['em', 'lw2'], writes=['em'])
                K.op('dve', lambda e: e.tensor_tensor(out=em[:], in0=em[:], in1=lw[3][:], op=ALU.add), reads=['em', 'lw3'], writes=['em'])
                K.op('dve', lambda e: e.reciprocal(out=em[:], in_=em[:]), reads=['em'], writes=['em'])
                K.op('dve', lambda e: e.tensor_copy(out=lb_b[:], in_=lw[1][:]), reads=['lw1'], writes=['lb_b'])
                for r in range(2, l + 1):
                    K.op('dve', lambda e, r=r: e.tensor_tensor(out=lb_b[:], in0=lb_b[:], in1=lw[r][:], op=ALU.add),
                         reads=['lb_b', 'lw%d' % r], writes=['lb_b'])
                K.op('dve', lambda e: e.tensor_tensor(out=lb_b[:], in0=lb_b[:], in1=em[:], op=ALU.mult), reads=['lb_b', 'em'], writes=['lb_b'])
                K.op('dve', lambda e: e.tensor_scalar(out=oml_b[:], in0=lb_b[:], scalar1=-1.0, scalar2=1.0, op0=ALU.mult, op1=ALU.add),
                     reads=['lb_b'], writes=['oml_b'])
                le = ep; jc = em; same = o
                K.op('dve', lambda e: e.tensor_scalar(out=le[:, 0:128], in0=J_f[:], scalar1=P_f[:, 0:1], scalar2=None, op0=ALU.is_ge),
                     reads=['J_f', 'P_f', 'ep'], writes=['ep'])
                K.op('dve', lambda e: e.tensor_scalar(out=jc[:, 0:128], in0=J_f[:], scalar1=64.0, scalar2=None, op0=ALU.is_ge),
                     reads=['J_f', 'em'], writes=['em'])
                K.op('dve', lambda e: e.tensor_scalar(out=rowm[:], in0=P_f[:], scalar1=64.0, scalar2=None, op0=ALU.is_ge),
                     reads=['P_f'], writes=['rowm'])
                K.op('dve', lambda e: e.tensor_scalar(out=same[:, 0:128], in0=jc[:, 0:128], scalar1=rowm[:, 0:1], scalar2=None,
                                                      op0=ALU.is_equal), reads=['em', 'rowm'], writes=['o'])
                K.op('dve', lambda e: e.tensor_tensor(out=attmask[:], in0=le[:, 0:128], in1=same[:, 0:128], op=ALU.mult),
                     reads=['ep', 'o'], writes=['attmask'])
                K.op('dve', lambda e: e.tensor_scalar(out=jc[:, 0:128], in0=jc[:, 0:128], scalar1=64.0, scalar2=31.0,
                                                      op0=ALU.mult, op1=ALU.add), reads=['em'], writes=['em'])
                K.op('dve', lambda e: e.tensor_scalar(out=jc[:, 0:128], in0=jc[:, 0:128], scalar1=P_f[:, 0:1], scalar2=None,
                                                      op0=ALU.is_ge), reads=['em', 'P_f'], writes=['em'])
                K.op('dve', lambda e: e.tensor_tensor(out=le[:, 0:128], in0=le[:, 0:128], in1=jc[:, 0:128], op=ALU.subtract),
                     reads=['ep', 'em'], writes=['ep'])
                K.op('dve', lambda e: e.tensor_tensor(out=trirel[:], in0=le[:, 0:128], in1=same[:, 0:128], op=ALU.mult),
                     reads=['ep', 'o'], writes=['trirel'])
                K.op('dve', lambda e: e.tensor_scalar(out=selc[:, 0:1], in0=P_f[:], scalar1=64.0, scalar2=None, op0=ALU.is_lt),
                     reads=['P_f'], writes=['selc'])
                K.op('dve', lambda e: e.tensor_scalar(out=selc[:, 1:2], in0=P_f[:], scalar1=31.0, scalar2=None, op0=ALU.is_le),
                     reads=['P_f'], writes=['selc'])
                K.op('dve', lambda e: e.tensor_scalar(out=selc[:, 2:3], in0=P_f[:], scalar1=64.0, scalar2=None, op0=ALU.is_ge),
                     reads=['P_f'], writes=['selc'])
                K.op('dve', lambda e: e.tensor_scalar(out=selc[:, 3:4], in0=P_f[:], scalar1=95.0, scalar2=None, op0=ALU.is_le),
                     reads=['P_f'], writes=['selc'])
                K.op('dve', lambda e: e.tensor_tensor(out=selc[:, 3:4], in0=selc[:, 3:4], in1=selc[:, 2:3], op=ALU.mult),
                     reads=['selc'], writes=['selc'])
                K.op('dve', lambda e: e.tensor_scalar(out=rowm[:], in0=P_f[:], scalar1=32.0, scalar2=None, op0=ALU.is_lt),
                     reads=['P_f', 'rowm'], writes=['rowm'])
                K.op('dve', lambda e: e.memset(S[:], 0.0), writes=['S'])
                K.op('pool', lambda e: e.memset(qT0[:], 0.0), writes=['qT0'])
                K.op('pool', lambda e: e.memset(qT1[:], 0.0), writes=['qT1'])

                def hb(t, h):
                    return t[:, h * 128:(h + 1) * 128]

                def body(i, sample):
                    if sample:
                        if dbg_src:
                            K.op('dve', lambda e: e.memset(xt[:], 0.0), writes=['xt'])
                            K.dma('sp', 'xt', lambda e: e.dma_start(out=xt[0:32, :], in_=xs[:, :]), writes=['xt'])
                        else:
                            K.dma('sp', 'xt', lambda e: e.dma_start(out=xt[:], in_=xres[NT * 128:(NT + 1) * 128, :]), writes=['xt'])
                        K.dma('sp', 'S', lambda e: e.dma_start(out=S[:], in_=st_hgrn[j].rearrange("h d e -> d h e")), writes=['S'])
                        dst = xres[NT * 128:(NT + 1) * 128, :]
                    else:
                        srcT = xp if dbg_src else xres
                        K.dma('sp', 'xt', lambda e: e.dma_start(out=xt[:], in_=srcT[bass.ds(i * 128, 128), :]), writes=['xt'])
                        dst = xres[bass.ds(i * 128, 128), :]
                    rmsnorm(W, xt, gmix, out_bf=(hn_bf, 'hn_bf'))
                    transpose8(hn_bf, 'hn_bf', ptp, hnT, 'hnT')

                    def zmm(pt, pname, cb0):
                        for nb in range(2):
                            for k in range(8):
                                K.op('pe', lambda e, nb=nb, k=k: e.matmul(pt[:, nb * 512:(nb + 1) * 512], lhsT=hnT[:, k, :],
                                                                          rhs=w_in[:, k, cb0 + nb * 512:cb0 + (nb + 1) * 512],
                                                                          start=(k == 0), stop=(k == 7)),
                                     reads=['hnT', 'w_in'], writes=[pname])
                    zmm(pA, 'pA', 0)
                    K.op('act', lambda e: e.activation(out=q[:], in_=pA[:], func=AF.Silu), reads=['pA'], writes=['q'])
                    zmm(pB, 'pB', 1024)
                    K.op('act', lambda e: e.activation(out=fg[:], in_=pB[:], func=AF.Sigmoid), reads=['pB'], writes=['fg'])
                    K.op('dve', lambda e: e.tensor_tensor(out=fg[:], in0=fg[:], in1=oml_b[:], op=ALU.mult), reads=['fg', 'oml_b'], writes=['fg'])
                    K.op('dve', lambda e: e.tensor_tensor(out=fg[:], in0=fg[:], in1=lb_b[:], op=ALU.add), reads=['fg', 'lb_b'], writes=['fg'])
                    K.op('act', lambda e: e.activation(out=logf[:], in_=fg[:], func=AF.Ln), reads=['fg'], writes=['logf'])
                    K.op('dve', lambda e: e.tensor_scalar(out=kk[:], in0=fg[:], scalar1=-1.0, scalar2=1.0, op0=ALU.mult, op1=ALU.add),
                         reads=['fg'], writes=['kk'])
                    if sample:
                        K.op('dve', lambda e: e.tensor_scalar(out=logf[:], in0=logf[:], scalar1=rowm[:, 0:1], scalar2=None, op0=ALU.mult),
                             reads=['logf', 'rowm'], writes=['logf'])
                        K.op('dve', lambda e: e.tensor_scalar(out=kk[:], in0=kk[:], scalar1=rowm[:, 0:1], scalar2=None, op0=ALU.mult),
                             reads=['kk', 'rowm'], writes=['kk'])
                    zmm(pA, 'pA', 2048)
                    K.op('act', lambda e: e.copy(out=v_bf[:], in_=pA[:]), reads=['pA'], writes=['v_bf'])
                    zmm(pB, 'pB', 3072)
                    K.op('act', lambda e: e.activation(out=sg[:], in_=pB[:], func=AF.Silu), reads=['pB'], writes=['sg'])
                    if stop_at <= 1:
                        K.dma('sp', 'xt', lambda e: e.dma_start(out=dst, in_=xt[:]), reads=['xt'])
                        return
                    for nb in range(2):
                        K.op('pe', lambda e, nb=nb: e.matmul(pC[:, nb * 512:(nb + 1) * 512], lhsT=trirel[:], rhs=logf[:, nb * 512:(nb + 1) * 512],
                                                             start=True, stop=True), reads=['trirel', 'logf'], writes=['pC'])
                    K.op('act', lambda e: e.activation(out=ep[:], in_=pC[:], func=AF.Exp), reads=['pC'], writes=['ep'])
                    K.op('act', lambda e: e.activation(out=em[:], in_=pC[:], func=AF.Exp, scale=-1.0), reads=['pC'], writes=['em'])
                    K.op('dve', lambda e: e.tensor_tensor(out=qt_bf[:], in0=q[:], in1=ep[:], op=ALU.mult), reads=['q', 'ep'], writes=['qt_bf'])
                    K.op('dve', lambda e: e.tensor_tensor(out=kt_bf[:], in0=kk[:], in1=em[:], op=ALU.mult), reads=['kk', 'em'], writes=['kt_bf'])
                    K.op('dve', lambda e: e.scalar_tensor_tensor(out=ktc[0][:], in0=kk[:], scalar=selc[:, 0:1], in1=em[:], op0=ALU.mult, op1=ALU.mult),
                         reads=['kk', 'em', 'selc'], writes=['ktc0'])
                    K.op('dve', lambda e: e.scalar_tensor_tensor(out=ktc[1][:], in0=kk[:], scalar=selc[:, 2:3], in1=em[:], op0=ALU.mult, op1=ALU.mult),
                         reads=['kk', 'em', 'selc'], writes=['ktc1'])
                    if stop_at <= 2:
                        K.dma('sp', 'xt', lambda e: e.dma_start(out=dst, in_=xt[:]), reads=['xt'])
                        return
                    for h in range(8):
                        K.op('pe', lambda e, h=h: e.matmul(psm[:, h * 4:(h + 1) * 4], lhsT=hb(logf, h), rhs=selc[:], start=True, stop=True),
                             reads=['logf', 'selc'], writes=['psm'])
                    K.op('dve', lambda e: e.tensor_copy(out=fs[:], in_=psm[:, 0:32].rearrange("p (h c) -> p h c", c=4)),
                         reads=['psm'], writes=['fs'])
                    for c in range(2):
                        K.op('dve', lambda e, c=c: e.tensor_copy(out=arg[:, :, 3 * c:3 * c + 1], in_=fs[:, :, 2 * c:2 * c + 1]),
                             reads=['fs'], writes=['arg'])
                        K.op('dve', lambda e, c=c: e.tensor_tensor(out=arg[:, :, 3 * c + 1:3 * c + 2], in0=fs[:, :, 2 * c:2 * c + 1],
                                                                   in1=fs[:, :, 2 * c + 1:2 * c + 2], op=ALU.subtract),
                             reads=['fs'], writes=['arg'])
                        K.op('dve', lambda e, c=c: e.tensor_copy(out=arg[:, :, 3 * c + 2:3 * c + 3], in_=fs[:, :, 2 * c + 1:2 * c + 2]),
                             reads=['fs'], writes=['arg'])
                    K.op('act', lambda e: e.activation(out=E[:], in_=arg[:], func=AF.Exp), reads=['arg'], writes=['E'])
                    if stop_at <= 3:
                        K.dma('sp', 'xt', lambda e: e.dma_start(out=dst, in_=xt[:]), reads=['xt'])
                        return
                    for h in range(8):
                        K.op('pe', lambda e, h=h: e.transpose(out=ptp[:, h, :], in_=hb(qt_bf, h), identity=ident_bf[:]),
                             reads=['qt_bf', 'ident_bf'], writes=['ptp'])
                    K.op('act', lambda e: e.copy(out=qT[:], in_=ptp[:]), reads=['ptp'], writes=['qT'])
                    K.op('act', lambda e: e.copy(out=qT0[:, :, 0:64], in_=qT[:, :, 0:64]), reads=['qT'], writes=['qT0'])
                    K.op('act', lambda e: e.copy(out=qT1[:, :, 64:128], in_=qT[:, :, 64:128]), reads=['qT'], writes=['qT1'])
                    for h in range(8):
                        K.op('pe', lambda e, h=h: e.transpose(out=ptp[:, h, :], in_=hb(kt_bf, h), identity=ident_bf[:]),
                             reads=['kt_bf', 'ident_bf'], writes=['ptp'])
                    K.op('act', lambda e: e.copy(out=kT[:], in_=ptp[:]), reads=['ptp'], writes=['kT'])
                    if stop_at <= 4:
                        K.dma('sp', 'xt', lambda e: e.dma_start(out=dst, in_=xt[:]), reads=['xt'])
                        return
                    for h in range(8):
                        K.op('pe', lambda e, h=h: e.matmul(hb(pA, h), lhsT=kT[:, h, :], rhs=qT[:, h, :], start=True, stop=True),
                             reads=['kT', 'qT'], writes=['pA'])
                    K.op('dve', lambda e: e.tensor_tensor(out=attm[:], in0=pA[:].rearrange("p (h t) -> p h t", t=128),
                                                          in1=attmask[:].unsqueeze(1).to_broadcast([128, 8, 128]), op=ALU.mult),
                         reads=['pA', 'attmask'], writes=['attm'])
                    if stop_at <= 5:
                        K.dma('sp', 'xt', lambda e: e.dma_start(out=dst, in_=xt[:]), reads=['xt'])
                        return
                    for c in range(2):
                        Sp = Sp0 if c == 0 else Sp1
                        spn = 'Sp%d' % c
                        K.op('dve', lambda e, c=c, Sp=Sp: e.tensor_tensor(out=Sp[:], in0=S[:],
                                                                          in1=E[:, :, 3 * c + 2:3 * c + 3].to_broadcast([128, 8, 128]), op=ALU.mult),
                             reads=['S', 'E'], writes=[spn])
                        for h in range(8):
                            K.op('pe', lambda e, h=h, c=c: e.matmul(hb(pC, h), lhsT=hb(ktc[c], h), rhs=hb(v_bf, h), start=True, stop=True),
                                 reads=['ktc%d' % c, 'v_bf'], writes=['pC'])
                        K.op('dve', lambda e, c=c: e.tensor_tensor(out=tmpS[:], in0=pC[:].rearrange("p (h t) -> p h t", t=128),
                                                                   in1=E[:, :, 3 * c + 1:3 * c + 2].to_broadcast([128, 8, 128]), op=ALU.mult),
                             reads=['pC', 'E'], writes=['tmpS'])
                        K.op('dve', lambda e, c=c: e.tensor_tensor(out=S[:], in0=S[:], in1=E[:, :, 3 * c:3 * c + 1].to_broadcast([128, 8, 128]),
                                                                   op=ALU.mult), reads=['S', 'E'], writes=['S'])
                        K.op('dve', lambda e: e.tensor_tensor(out=S[:], in0=S[:], in1=tmpS[:], op=ALU.add), reads=['S', 'tmpS'], writes=['S'])
                    if stop_at <= 6:
                        K.dma('sp', 'xt', lambda e: e.dma_start(out=dst, in_=xt[:]), reads=['xt'])
                        return
                    for h in range(8):
                        K.op('pe', lambda e, h=h: e.matmul(hb(pB, h), lhsT=attm[:, h, :], rhs=hb(v_bf, h), start=True, stop=False),
                             reads=['attm', 'v_bf'], writes=['pB'])
                        K.op('pe', lambda e, h=h: e.matmul(hb(pB, h), lhsT=qT0[:, h, :], rhs=Sp0[:, h, :], start=False, stop=False),
                             reads=['qT0', 'Sp0'], writes=['pB'])
                        K.op('pe', lambda e, h=h: e.matmul(hb(pB, h), lhsT=qT1[:, h, :], rhs=Sp1[:, h, :], start=False, stop=True),
                             reads=['qT1', 'Sp1'], writes=['pB'])
                    if stop_at <= 7:
                        K.dma('sp', 'xt', lambda e: e.dma_start(out=dst, in_=xt[:]), reads=['xt'])
                        return
                    K.op('act', lambda e: e.activation(out=W['junk'][:], in_=pB[:], func=AF.Square), reads=['pB'], writes=['junk'])
                    K.op('dve', lambda e: e.tensor_reduce(out=ssq[:], in_=W['junk'][:].rearrange("p (h t) -> p h t", t=128), axis=AX.X, op=ALU.add),
                         reads=['junk'], writes=['ssq'])
                    K.op('act', lambda e: e.activation(out=ssq[:], in_=ssq[:], func=AF.Sqrt, scale=1.0 / 128, bias=eps_t[:, 0:1]),
                         reads=['ssq'], writes=['ssq'])
                    K.op('dve', lambda e: e.reciprocal(out=rst8[:], in_=ssq[:]), reads=['ssq'], writes=['rst8'])
                    K.op('dve', lambda e: e.tensor_tensor(out=o[:].rearrange("p (h t) -> p h t", t=128),
                                                          in0=pB[:].rearrange("p (h t) -> p h t", t=128),
                                                          in1=rst8[:].unsqueeze(2).to_broadcast([128, 8, 128]), op=ALU.mult),
                         reads=['pB', 'rst8'], writes=['o'])
                    K.op('dve', lambda e: e.tensor_tensor(out=o[:].rearrange("p (h t) -> p h t", t=128),
                                                          in0=o[:].rearrange("p (h t) -> p h t", t=128),
                                                          in1=ng_b[:].unsqueeze(1).to_broadcast([128, 8, 128]), op=ALU.mult),
                         reads=['o', 'ng_b'], writes=['o'])
                    K.op('dve', lambda e: e.tensor_tensor(out=o_bf[:], in0=o[:], in1=sg[:], op=ALU.mult), reads=['o', 'sg'], writes=['o_bf'])
                    transpose8(o_bf, 'o_bf', ptp, oT, 'oT')
                    for nb in range(2):
                        for k in range(8):
                            K.op('pe', lambda e, nb=nb, k=k: e.matmul(pA[:, nb * 512:(nb + 1) * 512], lhsT=oT[:, k, :],
                                                                      rhs=w_out[:, k, nb * 512:(nb + 1) * 512], start=(k == 0), stop=(k == 7)),
                                 reads=['oT', 'w_out'], writes=['pA'])
                    K.op('dve', lambda e: e.tensor_tensor(out=xt[:], in0=pA[:], in1=xt[:], op=ALU.add), reads=['pA', 'xt'], writes=['xt'])
                    K.dma('sp', 'xt', lambda e: e.dma_start(out=dst, in_=xt[:]), reads=['xt'])

                K.barrier_reset()
                with nc.Fori(0, NT) as i:
                    body(i, False)
                    K.barrier_reset()
                K.dma('sp', 'S', lambda e: e.dma_start(out=hg_p[j].rearrange("h d e -> d h e"), in_=S[:]), reads=['S'])
                body(None, True)
                K.dma('sp', 'S', lambda e: e.dma_start(out=hg_s[j].rearrange("h d e -> d h e"), in_=S[:]), reads=['S'])
                K.barrier_reset()

        def phase_peer(l, last):
            NB = 16
            with contextlib.ExitStack() as ph:
                wq = sb(ph, "p_wq", [128, 8, 2048], BF16)
                keysT = sb(ph, "p_keysT", [128, 16, 128], BF16)
                gffn = sb(ph, "p_gffn", [128, D])
                gfin = sb(ph, "p_gfin", [128, D]) if last else None
                W = dict(junk=sb(ph, "p_junk", [128, D], BF16), ss=sb(ph, "p_ss", [128, 4]))
                xt = sb(ph, "p_xt", [128, D])
                hn_bf = sb(ph, "p_hn_bf", [128, D], BF16)
                hnT = sb(ph, "p_hnT", [128, 8, 128], BF16)
                q_bf = sb(ph, "p_q_bf", [128, 2048], BF16)
                hn = q_bf[:].bitcast(F32)
                qT = sb(ph, "p_qT", [128, 16, 128], BF16)
                sc = sb(ph, "p_sc", [128, 2048]); sc2 = sb(ph, "p_sc2", [128, 2048])
                top = sb(ph, "p_top", [128, 16, 16]); topi = sb(ph, "p_topi", [128, 16, 16], U32)
                topf = sb(ph, "p_topf", [128, 16, 16])
                i128 = sb(ph, "p_i128", [128, 8, 16])
                cand = sb(ph, "p_cand", [128, 8, 256]); cand2 = sc2[:].rearrange("p (h c) -> p h c", c=256)
                best = sb(ph, "p_best", [128, 8, 16]); pos = sb(ph, "p_pos", [128, 8, 16], U32)
                a_u = sb(ph, "p_a_u", [128, 8, 16], U32); b_u = sb(ph, "p_b_u", [128, 8, 16], U32)
                a_f = sb(ph, "p_a_f", [128, 8, 16]); b_f = sb(ph, "p_b_f", [128, 8, 16])
                eq = sc[:].rearrange("p (h a b) -> p h a b", a=16, b=16)
                isel = sb(ph, "p_isel", [128, 8, 16]); jsel = sb(ph, "p_jsel", [128, 8, 16])
                eidx = sb(ph, "p_eidx", [128, 128], I32)
                ge = sb(ph, "p_ge", [128, 8, 16]); gs = sb(ph, "p_gs", [128, 8])
                actv = sb(ph, "p_actv", [128, 128]); wgt = sb(ph, "p_wgt", [128, 128])
                dg = [sb(ph, "p_dg%d" % b, [128, 8, 128], BF16) for b in range(2)]
                uvb = [sb(ph, "p_uvb%d" % b, [128, 2 * D], BF16) for b in range(NB)]
                pbig = ps(ph, "p_pbig", [128, 2048])
                pz = [pbig[:, b * 512:(b + 1) * 512] for b in range(4)]
                hn_ps = ps(ph, "p_hnps", [128, D])
                ptp = ps(ph, "p_ptp", [128, 8, 128], BF16)
                oacc = pbig[:, 0:1024]

                load_w_bf16(wq, 'wq', peer_wq[l], 2048)
                bcast_load(gffn, 'gffn', norm_ffn[l])
                if last:
                    bcast_load(gfin, 'gfin', norm_final[0])
                nat = sc[:, 0:1024].rearrange("p (a c) -> p a c", c=128)
                nat_bf = q_bf[:].rearrange("p (a c) -> p a c", c=128)
                for half in range(2):
                    K.dma('sp', 'nat', lambda e, half=half: e.dma_start(
                        out=nat, in_=peer_keys[l, half * 4:(half + 1) * 4].rearrange("h p k c -> k (h p) c")), writes=['nat'])
                    K.op('dve', lambda e, half=half: e.tensor_copy(out=nat_bf[:, half * 8:(half + 1) * 8, :], in_=nat), reads=['nat'], writes=['nat_bf'])
                for half in range(2):
                    for k in range(8):
                        K.op('pe', lambda e, k=k, half=half: e.transpose(out=ptp[:, k, :], in_=nat_bf[:, half * 8 + k, :], identity=ident_bf[:]),
                             reads=['nat_bf', 'ident_bf'], writes=['ptp'])
                    K.op('act', lambda e, half=half: e.copy(out=keysT[:, half * 8:(half + 1) * 8, :], in_=ptp[:]), reads=['ptp'], writes=['keysT'])


                def body(i, sample):
                    if sample:
                        K.dma('sp', 'xt', lambda e: e.dma_start(out=xt[:], in_=xres[NT * 128:(NT + 1) * 128, :]), writes=['xt'])
                    else:
                        K.dma('sp', 'xt', lambda e: e.dma_start(out=xt[:], in_=xres[bass.ds(i * 128, 128), :]), writes=['xt'])
                    rmsnorm(W, xt, gffn, out_f32=(hn, 'hn'), out_bf=(hn_bf, 'hn_bf'))
                    for hh in range(2):
                        K.op('act', lambda e, hh=hh: e.copy(out=hn_ps[:, hh * 512:(hh + 1) * 512], in_=hn[:, hh * 512:(hh + 1) * 512]),
                             reads=['hn'], writes=['hn_ps'])
                    transpose8(hn_bf, 'hn_bf', ptp, hnT, 'hnT')
                    for nb in range(4):
                        for k in range(8):
                            K.op('pe', lambda e, nb=nb, k=k: e.matmul(pz[nb][:], lhsT=hnT[:, k, :], rhs=wq[:, k, nb * 512:(nb + 1) * 512],
                                                                      start=(k == 0), stop=(k == 7)),
                                 reads=['hnT', 'wq'], writes=['pz%d' % nb])
                        K.op('act', lambda e, nb=nb: e.copy(out=q_bf[:, nb * 512:(nb + 1) * 512], in_=pz[nb][:]),
                             reads=['pz%d' % nb], writes=['q_bf', 'hn'])
                    for half in range(2):
                        for k in range(8):
                            K.op('pe', lambda e, k=k, half=half: e.transpose(out=ptp[:, k, :], in_=q_bf[:, (half * 8 + k) * 128:(half * 8 + k + 1) * 128],
                                                                             identity=ident_bf[:]),
                                 reads=['q_bf', 'ident_bf'], writes=['ptp'])
                        K.op('act', lambda e, half=half: e.copy(out=qT[:, half * 8:(half + 1) * 8, :], in_=ptp[:]), reads=['ptp'], writes=['qT'])
                    for hp in range(16):
                        K.op('pe', lambda e, hp=hp: e.matmul(pz[hp // 4][:, (hp % 4) * 128:(hp % 4 + 1) * 128], lhsT=qT[:, hp, :],
                                                             rhs=keysT[:, hp, :], start=True, stop=True),
                             reads=['qT', 'keysT'], writes=['pz%d' % (hp // 4)])
                    for nb in range(4):
                        K.op('act', lambda e, nb=nb: e.copy(out=sc[:, nb * 512:(nb + 1) * 512], in_=pz[nb][:]),
                             reads=['pz%d' % nb], writes=['sc%d' % nb])
                    def grp_steps(hp):
                        blk = slice(hp * 128, (hp + 1) * 128)
                        scn = 'sc%d' % (hp // 4)
                        T, TI, S2 = 'top_%d' % hp, 'topi_%d' % hp, 'sc2_%d' % hp
                        return [
                            lambda: K.op('dve', lambda e: e.max(out=top[:, hp, 0:8], in_=sc[:, blk]), reads=[scn], writes=[T]),
                            lambda: K.op('dve', lambda e: e.max_index(out=topi[:, hp, 0:8], in_max=top[:, hp, 0:8], in_values=sc[:, blk]),
                                         reads=[scn, T], writes=[TI]),
                            lambda: K.op('dve', lambda e: e.match_replace(out=sc2[:, blk], in_to_replace=top[:, hp, 0:8],
                                                                          in_values=sc[:, blk], imm_value=NEG), reads=[scn, T], writes=[S2]),
                            lambda: K.op('dve', lambda e: e.max(out=top[:, hp, 8:16], in_=sc2[:, blk]), reads=[S2], writes=[T]),
                            lambda: K.op('dve', lambda e: e.max_index(out=topi[:, hp, 8:16], in_max=top[:, hp, 8:16], in_values=sc2[:, blk]),
                                         reads=[S2, T], writes=[TI]),
                        ]
                    for hp0 in range(0, 16, 4):
                        chains = [grp_steps(hp0 + x) for x in range(4)]
                        for st in range(5):
                            for c in chains:
                                c[st]()
                    ALLT = ['top_%d' % x for x in range(16)]
                    ALLTI = ['topi_%d' % x for x in range(16)]
                    K.op('dve', lambda e: e.tensor_copy(out=topf[:], in_=topi[:]), reads=ALLTI, writes=['topf'])
                    top4 = top[:].rearrange("p (h two) k -> p h two k", two=2)
                    topf4 = topf[:].rearrange("p (h two) k -> p h two k", two=2)
                    K.op('dve', lambda e: e.tensor_tensor(out=cand[:].rearrange("p h (a b) -> p h a b", b=16),
                                                          in0=top4[:, :, 0, :].unsqueeze(3).to_broadcast([128, 8, 16, 16]),
                                                          in1=top4[:, :, 1, :].unsqueeze(2).to_broadcast([128, 8, 16, 16]), op=ALU.add),
                         reads=ALLT, writes=['cand'])
                    K.op('dve', lambda e: e.tensor_scalar(out=i128[:], in0=topf4[:, :, 0, :], scalar1=128.0, scalar2=None, op0=ALU.mult),
                         reads=['topf'], writes=['i128'])

                    def head_steps(h):
                        BE, PO = 'best_%d' % h, 'pos_%d' % h
                        S2 = ['sc2_%d' % (2 * h), 'sc2_%d' % (2 * h + 1)]
                        return [
                            lambda: K.op('dve', lambda e: e.max(out=best[:, h, 0:8], in_=cand[:, h, :]), reads=['cand'], writes=[BE]),
                            lambda: K.op('dve', lambda e: e.max_index(out=pos[:, h, 0:8], in_max=best[:, h, 0:8], in_values=cand[:, h, :]),
                                         reads=['cand', BE], writes=[PO]),
                            lambda: K.op('dve', lambda e: e.match_replace(out=cand2[:, h, :], in_to_replace=best[:, h, 0:8], in_values=cand[:, h, :],
                                                                          imm_value=NEG), reads=['cand', BE], writes=S2),
                            lambda: K.op('dve', lambda e: e.max(out=best[:, h, 8:16], in_=cand2[:, h, :]), reads=S2, writes=[BE]),
                            lambda: K.op('dve', lambda e: e.max_index(out=pos[:, h, 8:16], in_max=best[:, h, 8:16], in_values=cand2[:, h, :]),
                                         reads=S2 + [BE], writes=[PO]),
                        ]
                    for h0 in range(0, 8, 4):
                        chains = [head_steps(h0 + x) for x in range(4)]
                        for st in range(5):
                            for c in chains:
                                c[st]()
                    ALLB = ['best_%d' % x for x in range(8)]
                    ALLP = ['pos_%d' % x for x in range(8)]
                    K.op('dve', lambda e: e.tensor_single_scalar(out=a_u[:], in_=pos[:], scalar=4, op=ALU.logical_shift_right),
                         reads=ALLP, writes=['a_u'])
                    K.op('dve', lambda e: e.tensor_single_scalar(out=b_u[:], in_=pos[:], scalar=15, op=ALU.bitwise_and),
                         reads=ALLP, writes=['b_u'])
                    K.op('dve', lambda e: e.tensor_copy(out=a_f[:], in_=a_u[:]), reads=['a_u'], writes=['a_f'])
                    K.op('dve', lambda e: e.tensor_copy(out=b_f[:], in_=b_u[:]), reads=['b_u'], writes=['b_f'])
                    io4 = iota16[:].unsqueeze(1).unsqueeze(1).to_broadcast([128, 8, 16, 16])
                    eqb = cand[:].rearrange("p h (a b) -> p h a b", b=16)
                    dec = ((a_f, 'a_f', i128[:], 'i128', isel, 'isel', eq[:], ['eq', 'sc0', 'sc1', 'sc2', 'sc3']),
                           (b_f, 'b_f', topf4[:, :, 1, :], 'topf', jsel, 'jsel', eqb, ['cand']))
                    for (xf, xn, src, srcn, dstt, dn, eqt, eqn) in dec:
                        K.op('dve', lambda e, xf=xf, eqt=eqt: e.tensor_tensor(out=eqt, in0=xf[:].unsqueeze(3).to_broadcast([128, 8, 16, 16]),
                                                                              in1=io4, op=ALU.is_equal), reads=[xn, 'iota16'], writes=eqn)
                    for (xf, xn, src, srcn, dstt, dn, eqt, eqn) in dec:
                        K.op('dve', lambda e, src=src, eqt=eqt: e.tensor_tensor(out=eqt, in0=eqt, in1=src.unsqueeze(2).to_broadcast([128, 8, 16, 16]),
                                                                                op=ALU.mult), reads=[eqn[0], srcn], writes=[eqn[0]])
                    for (xf, xn, src, srcn, dstt, dn, eqt, eqn) in dec:
                        K.op('dve', lambda e, dstt=dstt, eqt=eqt: e.tensor_reduce(out=dstt[:], in_=eqt, axis=AX.X, op=ALU.add),
                             reads=[eqn[0]], writes=[dn])
                    K.op('dve', lambda e: e.scalar_tensor_tensor(out=isel[:], in0=isel[:], scalar=float(l * NE), in1=jsel[:],
                                                                 op0=ALU.add, op1=ALU.add), reads=['isel', 'jsel'], writes=['isel'])
                    K.op('dve', lambda e: e.tensor_copy(out=eidx[:].rearrange("p (h k) -> p h k", k=16), in_=isel[:]), reads=['isel'], writes=['eidx'])
                    K.op('dve', lambda e: e.tensor_tensor(out=ge[:], in0=best[:], in1=best[:, :, 0:1].to_broadcast([128, 8, 16]), op=ALU.subtract),
                         reads=ALLB, writes=['ge'])
                    K.op('act', lambda e: e.activation(out=ge[:], in_=ge[:], func=AF.Exp), reads=['ge'], writes=['ge'])
                    K.op('dve', lambda e: e.tensor_reduce(out=gs[:], in_=ge[:], axis=AX.X, op=ALU.add), reads=['ge'], writes=['gs'])
                    K.op('dve', lambda e: e.reciprocal(out=gs[:], in_=gs[:]), reads=['gs'], writes=['gs'])
                    K.op('dve', lambda e: e.tensor_tensor(out=ge[:], in0=ge[:], in1=gs[:].unsqueeze(2).to_broadcast([128, 8, 16]), op=ALU.mult),
                         reads=['ge', 'gs'], writes=['ge'])
                    ge2 = ge[:].rearrange("p h k -> p (h k)")
                    def dots(g):
                        for ei in range(g * 8, g * 8 + 8):
                            b = ei % NB
                            K.dma('pool', 'uvb%d' % b, lambda e, ei=ei, b=b: e.indirect_dma_start(
                                out=uvb[b][:], out_offset=None, in_=tuv,
                                in_offset=bass.IndirectOffsetOnAxis(ap=eidx[:, ei:ei + 1], axis=0), bounds_check=bounds_reg, oob_is_err=False),
                                reads=['eidx'], writes=['uvb%d' % b])
                            K.op('dve', lambda e, ei=ei, b=b: e.scalar_tensor_tensor(out=W['junk'][:], in0=uvb[b][:, 0:D], scalar=1.0, in1=hn_ps[:],
                                                                                     op0=ALU.mult, op1=ALU.mult, accum_out=actv[:, ei:ei + 1]),
                                 reads=['uvb%d' % b, 'hn_ps'], writes=['actv_e%d' % ei])
                        gs8 = slice(g * 8, g * 8 + 8)
                        K.op('act', lambda e: e.activation(out=wgt[:, gs8], in_=actv[:, gs8], func=AF.Gelu_apprx_tanh),
                             reads=['actv_e%d' % x for x in range(g * 8, g * 8 + 8)], writes=['wgt%d' % g])

                    def finish(g):
                        gs8 = slice(g * 8, g * 8 + 8)
                        K.op('dve', lambda e: e.tensor_tensor(out=wgt[:, gs8], in0=wgt[:, gs8], in1=ge2[:, gs8], op=ALU.mult),
                             reads=['wgt%d' % g, 'ge'], writes=['wgt%d' % g])
                        dgt = dg[g % 2]
                        dgn = 'dg%d' % (g % 2)
                        K.op('dve', lambda e: e.tensor_tensor(
                            out=dgt[:], in0=ident_f[:].unsqueeze(1).to_broadcast([128, 8, 128]),
                            in1=wgt[:, gs8].unsqueeze(2).to_broadcast([128, 8, 128]), op=ALU.mult),
                            reads=['wgt%d' % g, 'ident_f'], writes=[dgn])
                        for ei in range(g * 8, g * 8 + 8):
                            b = ei % NB
                            for hh in range(2):
                                K.op('pe', lambda e, ei=ei, b=b, hh=hh: e.matmul(
                                    oacc[:, hh * 512:(hh + 1) * 512], lhsT=dgt[:, ei % 8, :], rhs=uvb[b][:, D + hh * 512:D + (hh + 1) * 512],
                                    start=(ei == 0), stop=(ei == 127)),
                                    reads=[dgn, 'uvb%d' % b], writes=['pz%d' % hh])

                    assert NB == 16
                    for g in range(16):
                        dots(g)
                        finish(g)
                    K.op('dve', lambda e: e.tensor_tensor(out=xt[:], in0=oacc[:], in1=xt[:], op=ALU.add), reads=['xt', 'pz0', 'pz1'], writes=['xt'])
                    if last:
                        rmsnorm(W, xt, gfin, out_f32=(hn, 'hn'))
                        if sample:
                            K.dma('sp', 'hn', lambda e: e.dma_start(out=ys[:, :], in_=hn[0:32, :]), reads=['hn'])
                        else:
                            K.dma('sp', 'hn', lambda e: e.dma_start(out=yp[bass.ds(i * 128, 128), :], in_=hn[:]), reads=['hn'])
                    else:
                        if sample:
                            K.dma('sp', 'xt', lambda e: e.dma_start(out=xres[NT * 128:(NT + 1) * 128, :], in_=xt[:]), reads=['xt'])
                        else:
                            K.dma('sp', 'xt', lambda e: e.dma_start(out=xres[bass.ds(i * 128, 128), :], in_=xt[:]), reads=['xt'])

                run_tiles(body)

        def phase_convert_tables():
            R = 4
            nrow = 4 * NE
            with contextlib.ExitStack() as ph:
                cin = [sb(ph, "cv_in%d" % t, [128, R, D]) for t in range(2)]
                cout = [sb(ph, "cv_out%d" % t, [128, R, D], BF16) for t in range(2)]
                srcs = [peer_u.rearrange("l e d -> (l e) d"), peer_v.rearrange("l e d -> (l e) d")]
                dsts = [tuv[:, 0:D], tuv[:, D:2 * D]]
                K.barrier_reset()
                with nc.Fori(0, nrow // (128 * R)) as i:
                    for t in range(2):
                        K.dma('sp', 'cin%d' % t, lambda e, t=t: e.dma_start(
                            out=cin[t][:], in_=srcs[t][bass.ds(i * (128 * R), 128 * R), :].rearrange("(p r) d -> p r d", r=R)),
                            writes=['cin%d' % t])
                    K.op('act', lambda e: e.copy(out=cout[0][:], in_=cin[0][:]), reads=['cin0'], writes=['cout0'])
                    K.op('dve', lambda e: e.tensor_copy(out=cout[1][:], in_=cin[1][:]), reads=['cin1'], writes=['cout1'])
                    for t in range(2):
                        K.dma('sp', 'cout%d' % t, lambda e, t=t: e.dma_start(
                            out=dsts[t][bass.ds(i * (128 * R), 128 * R), :].rearrange("(p r) d -> p r d", r=R), in_=cout[t][:]),
                            reads=['cout%d' % t])
                    K.barrier_reset()

        phases = []
        for l in range(4):
            if l % 2 == 0:
                phases.append(lambda l=l: phase_even(l, first=(l == 0)))
            else:
                phases.append(lambda l=l: phase_odd(l))
            phases.append(lambda l=l: phase_peer(l, last=(l == 3)))
        if only == 'odd':
            phase_odd(1, dbg_src=True)
        else:
            if n_phases >= 2:
                phase_convert_tables()
            for pi, p in enumerate(phases):
                if pi < n_phases:
                    p()
        if n_phases < 8:
            with contextlib.ExitStack() as ph:
                xt = sb(ph, "dbg_xt", [128, D])
                K.barrier_reset()
                for t in range(NT + 1):
                    K.dma('sp', 'dbg_xt', lambda e, t=t: e.dma_start(out=xt[:], in_=xres[t * 128:(t + 1) * 128, :]), writes=['dbg'])
                    if t < NT:
                        K.dma('sp', 'dbg_xt', lambda e, t=t: e.dma_start(out=yp[t * 128:(t + 1) * 128, :], in_=xt[:]), reads=['dbg'])
                    else:
                        K.dma('sp', 'dbg_xt', lambda e, t=t: e.dma_start(out=ys[:, :], in_=xt[0:32, :]), reads=['dbg'])
                K.barrier_reset()
    return nc


_W_NAMES = ["norm_mix", "norm_ffn", "ev_w_in", "ev_a_ln_g", "ev_a_ln_b", "ev_ws", "ev_bs", "ev_conv_w", "ev_conv_b",
            "ev_b_ln_g", "ev_b_ln_b", "ev_w_out", "od_w_in", "od_lower", "od_norm_g", "od_w_out", "peer_wq", "peer_keys",
            "peer_u", "peer_v"]


def run(inputs, n_phases=8, only=None, NE=16384, stop_at=99):
    x_prompt = np.asarray(inputs["x_prompt"], dtype=np.float32)
    x_sample = np.asarray(inputs["x_sample"], dtype=np.float32)
    B, T, _ = x_prompt.shape
    NT = T // 128
    nc = build_program(NT, n_phases, only=only, NE=NE, stop_at=stop_at)
    shared = {k: np.ascontiguousarray(np.asarray(inputs[k], dtype=np.float32)) for k in _W_NAMES}
    shared["norm_final"] = np.ascontiguousarray(np.asarray(inputs["norm_final"], dtype=np.float32).reshape(1, D))
    st_conv = np.asarray(inputs["state_conv"], dtype=np.float32)
    st_hgrn = np.asarray(inputs["state_hgrn"], dtype=np.float32)
    in_maps = []
    for c in range(8):
        m = dict(shared)
        m["xp"] = np.ascontiguousarray(x_prompt[c % B])
        m["xs"] = np.ascontiguousarray(x_sample[c])
        m["st_conv"] = np.ascontiguousarray(st_conv[:, c])
        m["st_hgrn"] = np.ascontiguousarray(st_hgrn[:, c])
        in_maps.append(m)
    res = run_bass_kernel_spmd(nc, in_maps, core_ids=list(range(8)))
    R = res.results
    y_prompt = np.stack([R[b]["yp"] for b in range(B)]).astype(np.float32)
    y_sample = np.stack([R[c]["ys"] for c in range(8)]).astype(np.float32)
    conv_prompt = np.stack([R[b]["conv_p"] for b in range(B)], axis=1).astype(np.float32)
    conv_sample = np.stack([R[c]["conv_s"] for c in range(8)], axis=1).astype(np.float32)
    hgrn_prompt = np.stack([R[b]["hg_p"] for b in range(B)], axis=1).astype(np.float32)
    hgrn_sample = np.stack([R[c]["hg_s"] for c in range(8)], axis=1).astype(np.float32)
    gmlp_v_sample = np.stack([R[c]["v_s"] for c in range(8)], axis=1).astype(np.float32)
    return (y_prompt, y_sample, conv_prompt, conv_sample, hgrn_prompt, hgrn_sample, gmlp_v_sample)


def kernel(**inputs):
    return run(inputs)
```

```python
import contextlib
import numpy as np
import concourse.bass as bass
import concourse.mybir as mybir
from concourse.bass_utils import run_bass_kernel_spmd

F32 = mybir.dt.float32
BF16 = mybir.dt.bfloat16
I32 = mybir.dt.int32
U32 = mybir.dt.uint32
AF = mybir.ActivationFunctionType
ALU = mybir.AluOpType
AX = mybir.AxisListType

D = 1024
EPS = 1e-6
NEG = -1.0e30


class Ctx:
    def __init__(self, nc, es):
        self.nc = nc
        self.es = es
        self.eng = {'pe': nc.tensor, 'act': nc.scalar, 'dve': nc.vector, 'pool': nc.gpsimd, 'sp': nc.sync}
        self.sem = {k: es.enter_context(nc.semaphore("sem_" + k)) for k in self.eng}
        self.dsems = {}
        self.dcnt = {}
        self.reset_state()

    def reset_state(self):
        self.cnt = {k: 0 for k in self.eng}
        for k in self.dcnt:
            self.dcnt[k] = 0
        self.waited = {k: {} for k in self.eng}
        self.lastw = {}
        self.readers = {}

    def _sem_of(self, semkey):
        return self.sem[semkey] if isinstance(semkey, str) else self.dsems[semkey[1]]

    def _wait(self, e, tok):
        semkey, val = tok
        if semkey == 'pe' and e == 'pe':
            return
        w = self.waited[e]
        if w.get(semkey, 0) >= val:
            return
        self.eng[e].wait_ge(self._sem_of(semkey), val)
        w[semkey] = val

    def _deps(self, e, reads, writes):
        for b in reads:
            if b in self.lastw:
                self._wait(e, self.lastw[b])
        for b in writes:
            if b in self.lastw:
                self._wait(e, self.lastw[b])
            for t in self.readers.get(b, ()):
                self._wait(e, t)

    def _record(self, tok, reads, writes):
        for b in reads:
            self.readers.setdefault(b, []).append(tok)
        for b in writes:
            self.lastw[b] = tok
            self.readers[b] = []

    def op(self, e, fn, reads=(), writes=()):
        self._deps(e, reads, writes)
        ins = fn(self.eng[e])
        self.cnt[e] += 1
        ins.then_inc(self.sem[e], 1)
        self._record((e, self.cnt[e]), reads, writes)

    def dma(self, q, key, fn, reads=(), writes=()):
        if key not in self.dsems:
            self.dsems[key] = self.es.enter_context(self.nc.semaphore("d_" + key))
            self.dcnt[key] = 0
        self._deps(q, reads, writes)
        ins = fn(self.eng[q])
        self.dcnt[key] += 16
        ins.then_inc(self.dsems[key], 16)
        self._record((('d', key), self.dcnt[key]), reads, writes)

    def barrier_reset(self):
        for key, c in self.dcnt.items():
            if c > 0:
                self._wait('sp', (('d', key), c))
        self.nc.all_engine_barrier()
        self.nc.gpsimd.dma_reset()
        for k, s in self.sem.items():
            if self.cnt[k] > 0:
                self.nc.gpsimd.sem_clear(s)
        for k, s in self.dsems.items():
            if self.dcnt[k] > 0:
                self.nc.gpsimd.sem_clear(s)
        self.nc.all_engine_barrier()
        self.reset_state()


def build_program(NT, n_phases=8, only=None, NE=16384, stop_at=99):
    nc = bass.Bass("TRN2", target_bir_lowering=False)

    def din(name, shape, dt=F32):
        return nc.dram_tensor(name, list(shape), dt, kind="ExternalInput").ap()

    def dout(name, shape, dt=F32):
        return nc.dram_tensor(name, list(shape), dt, kind="ExternalOutput").ap()

    xp = din("xp", [NT * 128, D])
    xs = din("xs", [32, D])
    st_conv = din("st_conv", [2, 30, 512])
    st_hgrn = din("st_hgrn", [2, 8, 128, 128])
    norm_mix = din("norm_mix", [4, D])
    norm_ffn = din("norm_ffn", [4, D])
    norm_final = din("norm_final", [1, D])
    ev_w_in = din("ev_w_in", [2, D, 2048])
    ev_a_ln_g = din("ev_a_ln_g", [2, 512])
    ev_a_ln_b = din("ev_a_ln_b", [2, 512])
    ev_ws = din("ev_ws", [2, 4, 128, 128])
    ev_bs = din("ev_bs", [2, 4, 128])
    ev_conv_w = din("ev_conv_w", [2, 31, 512])
    ev_conv_b = din("ev_conv_b", [2, 512])
    ev_b_ln_g = din("ev_b_ln_g", [2, 512])
    ev_b_ln_b = din("ev_b_ln_b", [2, 512])
    ev_w_out = din("ev_w_out", [2, D, D])
    od_w_in = din("od_w_in", [2, D, 4096])
    od_lower = din("od_lower", [4, D])
    od_norm_g = din("od_norm_g", [2, 128])
    od_w_out = din("od_w_out", [2, D, D])
    peer_wq = din("peer_wq", [4, D, 2048])
    peer_keys = din("peer_keys", [4, 8, 2, 128, 128])
    peer_u = din("peer_u", [4, NE, D])
    peer_v = din("peer_v", [4, NE, D])

    yp = dout("yp", [NT * 128, D])
    ys = dout("ys", [32, D])
    conv_p = dout("conv_p", [2, 30, 512])
    conv_s = dout("conv_s", [2, 30, 512])
    hg_p = dout("hg_p", [2, 8, 128, 128])
    hg_s = dout("hg_s", [2, 8, 128, 128])
    v_s = dout("v_s", [2, 32, 512])

    xres = nc.dram_tensor("xres", [(NT + 1) * 128, D], F32, kind="Internal").ap()
    tuv = nc.dram_tensor("tuv", [4 * NE, 2 * D], BF16, kind="Internal").ap()

    with contextlib.ExitStack() as es:
        K = Ctx(nc, es)

        uniq = [0]

        def sb(stack, name, shape, dt=F32):
            uniq[0] += 1
            return stack.enter_context(nc.sbuf_tensor("%s_%d" % (name, uniq[0]), list(shape), dt))

        def ps(stack, name, shape, dt=F32):
            uniq[0] += 1
            return stack.enter_context(nc.psum_tensor("%s_%d" % (name, uniq[0]), list(shape), dt))

        pidx_i = sb(es, "pidx_i", [128, 1], I32)
        jidx_i = sb(es, "jidx_i", [128, 128], I32)
        P_f = sb(es, "P_f", [128, 1])
        J_f = sb(es, "J_f", [128, 128])
        ident_f = sb(es, "ident_f", [128, 128])
        ident_bf = sb(es, "ident_bf", [128, 128], BF16)
        eps_t = sb(es, "eps_t", [128, 1])
        iota16 = sb(es, "iota16", [128, 16])
        K.op('pool', lambda e: e.iota(pidx_i[:], pattern=[[0, 1]], base=0, channel_multiplier=1), writes=['pidx_i'])
        K.op('pool', lambda e: e.iota(jidx_i[:], pattern=[[1, 128]], base=0, channel_multiplier=0), writes=['jidx_i'])
        K.op('dve', lambda e: e.tensor_copy(out=P_f[:], in_=pidx_i[:]), reads=['pidx_i'], writes=['P_f'])
        K.op('dve', lambda e: e.tensor_copy(out=J_f[:], in_=jidx_i[:]), reads=['jidx_i'], writes=['J_f'])
        K.op('dve', lambda e: e.tensor_copy(out=iota16[:], in_=jidx_i[:, 0:16]), reads=['jidx_i'], writes=['iota16'])
        K.op('dve', lambda e: e.tensor_scalar(out=ident_f[:], in0=J_f[:], scalar1=P_f[:, 0:1], scalar2=None,
                                              op0=ALU.is_equal), reads=['J_f', 'P_f'], writes=['ident_f'])
        K.op('dve', lambda e: e.tensor_copy(out=ident_bf[:], in_=ident_f[:]), reads=['ident_f'], writes=['ident_bf'])
        K.op('dve', lambda e: e.memset(eps_t[:], EPS), writes=['eps_t'])
        bounds_reg = nc.gpsimd.to_reg(4 * NE - 1)

        def rmsnorm(W, xt, gb, out_f32=None, out_bf=None, tag=""):
            K.op('act', lambda e: e.activation(out=W['junk'][:], in_=xt[:], func=AF.Square, accum_out=W['ss'][:, 0:1]),
                 reads=['xt'], writes=['junk', 'ss'])
            K.op('act', lambda e: e.activation(out=W['ss'][:, 1:2], in_=W['ss'][:, 0:1], func=AF.Sqrt,
                                               scale=1.0 / D, bias=eps_t[:, 0:1]), reads=['ss'], writes=['ss1'])
            K.op('dve', lambda e: e.reciprocal(out=W['ss'][:, 2:3], in_=W['ss'][:, 1:2]), reads=['ss1'], writes=['ss2'])
            if out_f32 is not None:
                K.op('dve', lambda e: e.scalar_tensor_tensor(out=out_f32[0][:], in0=xt[:], scalar=W['ss'][:, 2:3], in1=gb[:],
                                                             op0=ALU.mult, op1=ALU.mult),
                     reads=['xt', 'ss2'], writes=[out_f32[1]])
                if out_bf is not None:
                    K.op('act', lambda e: e.copy(out=out_bf[0][:], in_=out_f32[0][:]), reads=[out_f32[1]], writes=[out_bf[1]])
            else:
                K.op('dve', lambda e: e.scalar_tensor_tensor(out=out_bf[0][:], in0=xt[:], scalar=W['ss'][:, 2:3], in1=gb[:],
                                                             op0=ALU.mult, op1=ALU.mult),
                     reads=['xt', 'ss2'], writes=[out_bf[1]])

        def transpose8(src_bf, src_name, ptp, dstT, dst_name, nblk=8, src_off=0):
            for k in range(nblk):
                K.op('pe', lambda e, k=k: e.transpose(out=ptp[:, k, :], in_=src_bf[:, (src_off + k) * 128:(src_off + k + 1) * 128],
                                                      identity=ident_bf[:]),
                     reads=[src_name, 'ident_bf'], writes=['ptp'])
            K.op('act', lambda e: e.copy(out=dstT[:, 0:nblk, :], in_=ptp[:, 0:nblk, :]), reads=['ptp'], writes=[dst_name])

        def load_w_bf16(dst, name, src2d, N):
            srcv = src2d.rearrange("(k p) n -> p k n", p=128)
            for n0 in range(0, N, 512):
                K.dma('pool', name, lambda e, n0=n0: e.dma_start(out=dst[:, :, n0:n0 + 512], in_=srcv[:, :, n0:n0 + 512]),
                      writes=[name])

        def bcast_load(dst, name, src_row):
            K.dma('sp', name, lambda e: e.dma_start(out=dst[:], in_=src_row.partition_broadcast(128)), writes=[name])

        def run_tiles(body):
            K.barrier_reset()
            with nc.Fori(0, NT) as i:
                body(i, False)
                K.barrier_reset()
            body(None, True)
            K.barrier_reset()

        def phase_even(l, first):
            j = l // 2
            with contextlib.ExitStack() as ph:
                w_in = sb(ph, "e_w_in", [128, 8, 2048], BF16)
                w_out = sb(ph, "e_w_out", [128, 8, 1024], BF16)
                gmix = sb(ph, "e_gmix", [128, D])
                ag = sb(ph, "e_ag", [128, 512]); ab = sb(ph, "e_ab", [128, 512])
                bg = sb(ph, "e_bg", [128, 512]); bb = sb(ph, "e_bb", [128, 512])
                wgT = sb(ph, "e_wgT", [128, 4, 128], BF16)
                bs_t = sb(ph, "e_bs_t", [128, 4])
                cw = sb(ph, "e_cw", [128, 4, 32])
                cb = sb(ph, "e_cb", [128, 4])
                nat = sb(ph, "e_nat", [128, 512])
                nat_bf = sb(ph, "e_nat_bf", [128, 128], BF16)
                trilm = sb(ph, "e_trilm", [128, 128])
                W = dict(junk=sb(ph, "e_junk", [128, D]), ss=sb(ph, "e_ss", [128, 4]))
                xt = sb(ph, "e_xt", [128, D])
                hn_bf = sb(ph, "e_hn_bf", [128, D], BF16)
                hnT = sb(ph, "e_hnT", [128, 8, 128], BF16)
                u = sb(ph, "e_u", [128, 512]); vpre = sb(ph, "e_vpre", [128, 512])
                v = sb(ph, "e_v", [128, 512]); v_bf = sb(ph, "e_v_bf", [128, 512], BF16)
                sig = sb(ph, "e_sig", [128, 512]); glu = sb(ph, "e_glu", [128, 512])
                gbuf = sb(ph, "e_gbuf", [128, 4, 160])
                acc = sb(ph, "e_acc", [128, 4, 128])
                cvn = sb(ph, "e_cvn", [128, 512])
                cat_bf = sb(ph, "e_cat_bf", [128, D], BF16)
                catT = sb(ph, "e_catT", [128, 8, 128], BF16)
                st6 = sb(ph, "e_st6", [128, 6]); mv = sb(ph, "e_mv", [128, 4])
                st6b = sb(ph, "e_st6b", [128, 6]); mvb = sb(ph, "e_mvb", [128, 4])
                hal = sb(ph, "e_hal", [128, 512])
                pz = [ps(ph, "e_pz%d" % b, [128, 512]) for b in range(4)]
                ptp = ps(ph, "e_ptp", [128, 8, 128], BF16)
                pmx = ps(ph, "e_pmx", [128, 512])
                ptf = ps(ph, "e_ptf", [128, 4, 128])
                pcv = ps(ph, "e_pcv", [128, 512])

                load_w_bf16(w_in, 'w_in', ev_w_in[j], 2048)
                load_w_bf16(w_out, 'w_out', ev_w_out[j], 1024)
                bcast_load(gmix, 'gmix', norm_mix[l])
                bcast_load(ag, 'ag', ev_a_ln_g[j]); bcast_load(ab, 'ab', ev_a_ln_b[j])
                bcast_load(bg, 'bg', ev_b_ln_g[j]); bcast_load(bb, 'bb', ev_b_ln_b[j])
                K.op('dve', lambda e: e.tensor_scalar(out=trilm[:], in0=J_f[:], scalar1=P_f[:, 0:1], scalar2=None,
                                                      op0=ALU.is_le), reads=['J_f', 'P_f'], writes=['trilm'])
                for g in range(4):
                    K.dma('sp', 'nat', lambda e, g=g: e.dma_start(out=nat[:, 0:128], in_=ev_ws[j, g]), writes=['nat'])
                    K.op('dve', lambda e: e.tensor_tensor(out=nat_bf[:], in0=nat[:, 0:128], in1=trilm[:], op=ALU.mult),
                         reads=['nat', 'trilm'], writes=['nat_bf'])
                    K.op('pe', lambda e: e.transpose(out=ptp[:, 0, :], in_=nat_bf[:], identity=ident_bf[:]),
                         reads=['nat_bf', 'ident_bf'], writes=['ptp'])
                    K.op('act', lambda e, g=g: e.copy(out=wgT[:, g, :], in_=ptp[:, 0, :]), reads=['ptp'], writes=['wgT'])
                K.dma('sp', 'nat', lambda e: e.dma_start(out=nat[0:4, 0:128], in_=ev_bs[j]), writes=['nat'])
                K.op('pe', lambda e: e.transpose(out=ptf[:, 0, 0:4], in_=nat[0:4, 0:128], identity=ident_f[0:4, 0:4]),
                     reads=['nat', 'ident_f'], writes=['ptf'])
                K.op('act', lambda e: e.copy(out=bs_t[:], in_=ptf[:, 0, 0:4]), reads=['ptf'], writes=['bs_t'])
                K.dma('sp', 'nat', lambda e: e.dma_start(out=nat[0:4, 0:128],
                                                         in_=ev_conv_b[j].rearrange("(ch p) -> ch p", p=128)), writes=['nat'])
                K.op('pe', lambda e: e.transpose(out=ptf[:, 0, 0:4], in_=nat[0:4, 0:128], identity=ident_f[0:4, 0:4]),
                     reads=['nat', 'ident_f'], writes=['ptf'])
                K.op('act', lambda e: e.copy(out=cb[:], in_=ptf[:, 0, 0:4]), reads=['ptf'], writes=['cb'])
                K.dma('sp', 'nat', lambda e: e.dma_start(out=nat[0:31, :], in_=ev_conv_w[j]), writes=['nat'])
                for ch in range(4):
                    K.op('pe', lambda e, ch=ch: e.transpose(out=ptf[:, ch, 0:31], in_=nat[0:31, ch * 128:(ch + 1) * 128],
                                                            identity=ident_f[0:31, 0:31]),
                         reads=['nat', 'ident_f'], writes=['ptf'])
                K.op('act', lambda e: e.copy(out=cw[:, :, 0:31], in_=ptf[:, :, 0:31]), reads=['ptf'], writes=['cw'])
                K.op('dve', lambda e: e.memset(gbuf[:], 0.0), writes=['gbuf'])

                def body(i, sample):
                    if sample:
                        src = (xs[:, :] if first else xres[NT * 128:NT * 128 + 32, :])
                        if first:
                            K.op('dve', lambda e: e.memset(xt[:], 0.0), writes=['xt'])
                        K.dma('sp', 'xt', lambda e: e.dma_start(out=xt[0:32, :] if first else xt[:], in_=src if first else xres[NT * 128:(NT + 1) * 128, :]),
                              writes=['xt'])
                        K.dma('sp', 'hal', lambda e: e.dma_start(out=hal[0:30, :], in_=st_conv[j]), writes=['hal'])
                        for ch in range(4):
                            K.op('pe', lambda e, ch=ch: e.transpose(out=ptf[:, ch, 0:30], in_=hal[0:30, ch * 128:(ch + 1) * 128],
                                                                    identity=ident_f[0:30, 0:30]),
                                 reads=['hal', 'ident_f'], writes=['ptf'])
                        K.op('act', lambda e: e.copy(out=gbuf[:, :, 0:30], in_=ptf[:, :, 0:30]), reads=['ptf'], writes=['gbuf'])
                        dst = xres[NT * 128:(NT + 1) * 128, :]
                    else:
                        srcT = xp if first else xres
                        K.dma('sp', 'xt', lambda e: e.dma_start(out=xt[:], in_=srcT[bass.ds(i * 128, 128), :]), writes=['xt'])
                        dst = xres[bass.ds(i * 128, 128), :]
                    rmsnorm(W, xt, gmix, out_bf=(hn_bf, 'hn_bf'))
                    transpose8(hn_bf, 'hn_bf', ptp, hnT, 'hnT')
                    for nb in range(4):
                        for k in range(8):
                            K.op('pe', lambda e, nb=nb, k=k: e.matmul(pz[nb][:], lhsT=hnT[:, k, :], rhs=w_in[:, k, nb * 512:(nb + 1) * 512],
                                                                      start=(k == 0), stop=(k == 7)),
                                 reads=['hnT', 'w_in'], writes=['pz%d' % nb])
                    K.op('act', lambda e: e.activation(out=u[:], in_=pz[0][:], func=AF.Gelu_apprx_tanh), reads=['pz0'], writes=['u'])
                    K.op('act', lambda e: e.activation(out=vpre[:], in_=pz[1][:], func=AF.Gelu_apprx_tanh), reads=['pz1'], writes=['vpre'])
                    K.op('act', lambda e: e.activation(out=sig[:], in_=pz[3][:], func=AF.Sigmoid), reads=['pz3'], writes=['sig'])
                    K.op('dve', lambda e: e.tensor_tensor(out=glu[:], in0=pz[2][:], in1=sig[:], op=ALU.mult),
                         reads=['pz2', 'sig'], writes=['glu'])
                    K.op('dve', lambda e: e.bn_stats(out=st6[:], in_=vpre[:]), reads=['vpre'], writes=['st6'])
                    K.op('dve', lambda e: e.bn_aggr(out=mv[:, 0:2], in_=st6[:]), reads=['st6'], writes=['mv'])
                    K.op('act', lambda e: e.activation(out=mv[:, 2:3], in_=mv[:, 1:2], func=AF.Sqrt, bias=eps_t[:, 0:1]),
                         reads=['mv'], writes=['mv2'])
                    K.op('dve', lambda e: e.reciprocal(out=mv[:, 3:4], in_=mv[:, 2:3]), reads=['mv2'], writes=['mv3'])
                    K.op('dve', lambda e: e.tensor_scalar(out=v[:], in0=vpre[:], scalar1=mv[:, 0:1], scalar2=mv[:, 3:4],
                                                          op0=ALU.subtract, op1=ALU.mult), reads=['vpre', 'mv', 'mv3'], writes=['v'])
                    K.op('dve', lambda e: e.tensor_tensor(out=v[:], in0=v[:], in1=ag[:], op=ALU.mult), reads=['v', 'ag'], writes=['v'])
                    K.op('dve', lambda e: e.tensor_tensor(out=v[:], in0=v[:], in1=ab[:], op=ALU.add), reads=['v', 'ab'], writes=['v'])
                    K.op('act', lambda e: e.copy(out=v_bf[:], in_=v[:]), reads=['v'], writes=['v_bf'])
                    if sample:
                        K.dma('sp', 'v', lambda e: e.dma_start(out=v_s[j], in_=v[0:32, :]), reads=['v'])
                    for g in range(4):
                        K.op('pe', lambda e, g=g: e.matmul(pmx[:, g * 128:(g + 1) * 128], lhsT=wgT[:, g, :],
                                                           rhs=v_bf[:, g * 128:(g + 1) * 128], start=True, stop=True),
                             reads=['wgT', 'v_bf'], writes=['pmx'])
                    for g in range(4):
                        K.op('dve', lambda e, g=g: e.scalar_tensor_tensor(out=cat_bf[:, g * 128:(g + 1) * 128],
                                                                          in0=pmx[:, g * 128:(g + 1) * 128], scalar=bs_t[:, g:g + 1],
                                                                          in1=u[:, g * 128:(g + 1) * 128], op0=ALU.add, op1=ALU.mult),
                             reads=['pmx', 'bs_t', 'u'], writes=['cat_a'])
                    if sample:
                        K.dma('sp', 'glu', lambda e: e.dma_start(out=conv_s[j], in_=glu[2:32, :]), reads=['glu'])
                    for ch in range(4):
                        K.op('pe', lambda e, ch=ch: e.transpose(out=ptf[:, ch, :], in_=glu[:, ch * 128:(ch + 1) * 128], identity=ident_f[:]),
                             reads=['glu', 'ident_f'], writes=['ptf'])
                    K.op('act', lambda e: e.copy(out=gbuf[:, :, 30:158], in_=ptf[:, :, :]), reads=['ptf'], writes=['gbuf'])
                    for ch in range(4):
                        K.op('dve', lambda e, ch=ch: e.tensor_scalar(out=acc[:, ch, :], in0=gbuf[:, ch, 0:128], scalar1=cw[:, ch, 0:1],
                                                                     scalar2=cb[:, ch:ch + 1], op0=ALU.mult, op1=ALU.add),
                             reads=['gbuf', 'cw', 'cb'], writes=['acc%d' % ch])
                    for k in range(1, 31):
                        for ch in range(4):
                            K.op('dve', lambda e, ch=ch, k=k: e.scalar_tensor_tensor(out=acc[:, ch, :], in0=gbuf[:, ch, k:k + 128],
                                                                                     scalar=cw[:, ch, k:k + 1], in1=acc[:, ch, :],
                                                                                     op0=ALU.mult, op1=ALU.add),
                                 reads=['gbuf', 'cw', 'acc%d' % ch], writes=['acc%d' % ch])
                    K.op('pool', lambda e: e.tensor_copy(out=gbuf[:, :, 0:30], in_=gbuf[:, :, 128:158]), reads=['gbuf'], writes=['gbuf'])
                    for ch in range(4):
                        K.op('pe', lambda e, ch=ch: e.transpose(out=pcv[:, ch * 128:(ch + 1) * 128], in_=acc[:, ch, :], identity=ident_f[:]),
                             reads=['acc%d' % ch, 'ident_f'], writes=['pcv'])
                    K.op('dve', lambda e: e.bn_stats(out=st6b[:], in_=pcv[:]), reads=['pcv'], writes=['st6b'])
                    K.op('dve', lambda e: e.bn_aggr(out=mvb[:, 0:2], in_=st6b[:]), reads=['st6b'], writes=['mvb'])
                    K.op('act', lambda e: e.activation(out=mvb[:, 2:3], in_=mvb[:, 1:2], func=AF.Sqrt, bias=eps_t[:, 0:1]),
                         reads=['mvb'], writes=['mvb2'])
                    K.op('dve', lambda e: e.reciprocal(out=mvb[:, 3:4], in_=mvb[:, 2:3]), reads=['mvb2'], writes=['mvb3'])
                    K.op('dve', lambda e: e.tensor_scalar(out=cvn[:], in0=pcv[:], scalar1=mvb[:, 0:1], scalar2=mvb[:, 3:4],
                                                          op0=ALU.subtract, op1=ALU.mult), reads=['pcv', 'mvb', 'mvb3'], writes=['cvn'])
                    K.op('dve', lambda e: e.tensor_tensor(out=cvn[:], in0=cvn[:], in1=bg[:], op=ALU.mult), reads=['cvn', 'bg'], writes=['cvn'])
                    K.op('dve', lambda e: e.tensor_tensor(out=cvn[:], in0=cvn[:], in1=bb[:], op=ALU.add), reads=['cvn', 'bb'], writes=['cvn'])
                    K.op('act', lambda e: e.activation(out=cat_bf[:, 512:1024], in_=cvn[:], func=AF.Silu), reads=['cvn'], writes=['cat_b'])
                    for k in range(8):
                        K.op('pe', lambda e, k=k: e.transpose(out=ptp[:, k, :], in_=cat_bf[:, k * 128:(k + 1) * 128], identity=ident_bf[:]),
                             reads=['cat_a', 'cat_b', 'ident_bf'], writes=['ptp'])
                    K.op('act', lambda e: e.copy(out=catT[:], in_=ptp[:]), reads=['ptp'], writes=['catT'])
                    for nb in range(2):
                        for k in range(8):
                            K.op('pe', lambda e, nb=nb, k=k: e.matmul(pz[nb][:], lhsT=catT[:, k, :], rhs=w_out[:, k, nb * 512:(nb + 1) * 512],
                                                                      start=(k == 0), stop=(k == 7)),
                                 reads=['catT', 'w_out'], writes=['pz%d' % nb])
                    for nb in range(2):
                        K.op('dve', lambda e, nb=nb: e.tensor_tensor(out=xt[:, nb * 512:(nb + 1) * 512], in0=pz[nb][:],
                                                                     in1=xt[:, nb * 512:(nb + 1) * 512], op=ALU.add),
                             reads=['pz%d' % nb, 'xt'], writes=['xt'])
                    K.dma('sp', 'xt', lambda e: e.dma_start(out=dst, in_=xt[:]), reads=['xt'])

                K.barrier_reset()
                with nc.Fori(0, NT) as i:
                    body(i, False)
                    K.barrier_reset()
                K.dma('sp', 'glu', lambda e: e.dma_start(out=conv_p[j], in_=glu[98:128, :]), reads=['glu'])
                body(None, True)
                K.barrier_reset()

        def phase_odd(l, dbg_src=False):
            j = l // 2
            with contextlib.ExitStack() as ph:
                w_in = sb(ph, "o_w_in", [128, 8, 4096], BF16)
                w_out = sb(ph, "o_w_out", [128, 8, 1024], BF16)
                gmix = sb(ph, "o_gmix", [128, D])
                lb_b = sb(ph, "o_lb", [128, D]); oml_b = sb(ph, "o_oml", [128, D])
                ng_b = sb(ph, "o_ng", [128, 128])
                S = sb(ph, "o_S", [128, 8, 128])
                attmask = sb(ph, "o_attmask", [128, 128]); trirel = sb(ph, "o_trirel", [128, 128])
                selc = sb(ph, "o_selc", [128, 4]); rowm = sb(ph, "o_rowm", [128, 1])
                W = dict(junk=sb(ph, "o_junk", [128, D]), ss=sb(ph, "o_ss", [128, 4]))
                xt = sb(ph, "o_xt", [128, D])
                hn_bf = sb(ph, "o_hn_bf", [128, D], BF16)
                hnT = sb(ph, "o_hnT", [128, 8, 128], BF16)
                q = sb(ph, "o_q", [128, D]); sg = sb(ph, "o_sg", [128, D])
                fg = sb(ph, "o_fg", [128, D]); logf = sb(ph, "o_logf", [128, D]); kk = sb(ph, "o_kk", [128, D])
                ep = sb(ph, "o_ep", [128, D]); em = sb(ph, "o_em", [128, D])
                v_bf = sb(ph, "o_v_bf", [128, D], BF16)
                qt_bf = sb(ph, "o_qt_bf", [128, D], BF16); kt_bf = sb(ph, "o_kt_bf", [128, D], BF16)
                ktc = [sb(ph, "o_ktc%d" % c, [128, D], BF16) for c in range(2)]
                qT = sb(ph, "o_qT", [128, 8, 128], BF16); qT0 = sb(ph, "o_qT0", [128, 8, 128], BF16)
                qT1 = sb(ph, "o_qT1", [128, 8, 128], BF16); kT = sb(ph, "o_kT", [128, 8, 128], BF16)
                attm = sb(ph, "o_attm", [128, 8, 128], BF16)
                Sp0 = sb(ph, "o_Sp0", [128, 8, 128], BF16); Sp1 = sb(ph, "o_Sp1", [128, 8, 128], BF16)
                tmpS = sb(ph, "o_tmpS", [128, 8, 128])
                fs = sb(ph, "o_fs", [128, 8, 4]); arg = sb(ph, "o_arg", [128, 8, 6]); E = sb(ph, "o_E", [128, 8, 6])
                ssq = sb(ph, "o_ssq", [128, 8]); rst8 = sb(ph, "o_rst8", [128, 8])
                o = sb(ph, "o_o", [128, D]); o_bf = sb(ph, "o_o_bf", [128, D], BF16)
                oT = sb(ph, "o_oT", [128, 8, 128], BF16)
                pA = ps(ph, "o_pA", [128, 1024]); pB = ps(ph, "o_pB", [128, 1024]); pC = ps(ph, "o_pC", [128, 1024])
                ptp = ps(ph, "o_ptp", [128, 8, 128], BF16)
                psm = ps(ph, "o_psm", [128, 512])

                load_w_bf16(w_in, 'w_in', od_w_in[j], 4096)
                load_w_bf16(w_out, 'w_out', od_w_out[j], 1024)
                bcast_load(gmix, 'gmix', norm_mix[l])
                bcast_load(ng_b, 'ng_b', od_norm_g[j])
                for r in range(4):
                    dstt = [q, sg, fg, kk][r]
                    K.dma('sp', 'lw%d' % r, lambda e, r=r, dstt=dstt: e.dma_start(out=dstt[:], in_=od_lower[r].partition_broadcast(128)),
                          writes=['lw%d' % r])
                lw = [q, sg, fg, kk]
                K.op('dve', lambda e: e.tensor_tensor(out=ep[:], in0=lw[0][:], in1=lw[1][:], op=ALU.max), reads=['lw0', 'lw1'], writes=['ep'])
                K.op('dve', lambda e: e.tensor_tensor(out=ep[:], in0=ep[:], in1=lw[2][:], op=ALU.max), reads=['ep', 'lw2'], writes=['ep'])
                K.op('dve', lambda e: e.tensor_tensor(out=ep[:], in0=ep[:], in1=lw[3][:], op=ALU.max), reads=['ep', 'lw3'], writes=['ep'])
                for r in range(4):
                    K.op('dve', lambda e, r=r: e.tensor_tensor(out=lw[r][:], in0=lw[r][:], in1=ep[:], op=ALU.subtract),
                         reads=['lw%d' % r, 'ep'], writes=['lw%d' % r])
                    K.op('act', lambda e, r=r: e.activation(out=lw[r][:], in_=lw[r][:], func=AF.Exp), reads=['lw%d' % r], writes=['lw%d' % r])
                K.op('dve', lambda e: e.tensor_tensor(out=em[:], in0=lw[0][:], in1=lw[1][:], op=ALU.add), reads=['lw0', 'lw1'], writes=['em'])
                K.op('dve', lambda e: e.tensor_tensor(out=em[:], in0=em[:], in1=lw[2][:], op=ALU.add), reads=['em', 'lw2'], writes=['em'])
                K.op('dve', lambda e: e.tensor_tensor(out=em[:], in0=em[:], in1=lw[3][:], op=ALU.add), reads=['em', 'lw3'], writes=['em'])
                K.op('dve', lambda e: e.reciprocal(out=em[:], in_=em[:]), reads=['em'], writes=['em'])
                K.op('dve', lambda e: e.tensor_copy(out=lb_b[:], in_=lw[1][:]), reads=['lw1'], writes=['lb_b'])
                for r in range(2, l + 1):
                    K.op('dve', lambda e, r=r: e.tensor_tensor(out=lb_b[:], in0=lb_b[:], in1=lw[r][:], op=ALU.add),
                         reads=['lb_b', 'lw%d' % r], writes=['lb_b'])
                K.op('dve', lambda e: e.tensor_tensor(out=lb_b[:], in0=lb_b[:], in1=em[:], op=ALU.mult), reads=['lb_b', 'em'], writes=['lb_b'])
                K.op('dve', lambda e: e.tensor_scalar(out=oml_b[:], in0=lb_b[:], scalar1=-1.0, scalar2=1.0, op0=ALU.mult, op1=ALU.add),
                     reads=['lb_b'], writes=['oml_b'])
                le = ep; jc = em; same = o
                K.op('dve', lambda e: e.tensor_scalar(out=le[:, 0:128], in0=J_f[:], scalar1=P_f[:, 0:1], scalar2=None, op0=ALU.is_ge),
                     reads=['J_f', 'P_f', 'ep'], writes=['ep'])
                K.op('dve', lambda e: e.tensor_scalar(out=jc[:, 0:128], in0=J_f[:], scalar1=64.0, scalar2=None, op0=ALU.is_ge),
                     reads=['J_f', 'em'], writes=['em'])
                K.op('dve', lambda e: e.tensor_scalar(out=rowm[:], in0=P_f[:], scalar1=64.0, scalar2=None, op0=ALU.is_ge),
                     reads=['P_f'], writes=['rowm'])
                K.op('dve', lambda e: e.tensor_scalar(out=same[:, 0:128], in0=jc[:, 0:128], scalar1=rowm[:, 0:1], scalar2=None,
                                                      op0=ALU.is_equal), reads=['em', 'rowm'], writes=['o'])
                K.op('dve', lambda e: e.tensor_tensor(out=attmask[:], in0=le[:, 0:128], in1=same[:, 0:128], op=ALU.mult),
                     reads=['ep', 'o'], writes=['attmask'])
                K.op('dve', lambda e: e.tensor_scalar(out=jc[:, 0:128], in0=jc[:, 0:128], scalar1=64.0, scalar2=31.0,
                                                      op0=ALU.mult, op1=ALU.add), reads=['em'], writes=['em'])
                K.op('dve', lambda e: e.tensor_scalar(out=jc[:, 0:128], in0=jc[:, 0:128], scalar1=P_f[:, 0:1], scalar2=None,
                                                      op0=ALU.is_ge), reads=['em', 'P_f'], writes=['em'])
                K.op('dve', lambda e: e.tensor_tensor(out=le[:, 0:128], in0=le[:, 0:128], in1=jc[:, 0:128], op=ALU.subtract),
                     reads=['ep', 'em'], writes=['ep'])
                K.op('dve', lambda e: e.tensor_tensor(out=trirel[:], in0=le[:, 0:128], in1=same[:, 0:128], op=ALU.mult),
                     reads=['ep', 'o'], writes=['trirel'])
                K.op('dve', lambda e: e.tensor_scalar(out=selc[:, 0:1], in0=P_f[:], scalar1=64.0, scalar2=None, op0=ALU.is_lt),
                     reads=['P_f'], writes=['selc'])
                K.op('dve', lambda e: e.tensor_scalar(out=selc[:, 1:2], in0=P_f[:], scalar1=31.0, scalar2=None, op0=ALU.is_le),
                     reads=['P_f'], writes=['selc'])
                K.op('dve', lambda e: e.tensor_scalar(out=selc[:, 2:3], in0=P_f[:], scalar1=64.0, scalar2=None, op0=ALU.is_ge),
                     reads=['P_f'], writes=['selc'])
                K.op('dve', lambda e: e.tensor_scalar(out=selc[:, 3:4], in0=P_f[:], scalar1=95.0, scalar2=None, op0=ALU.is_le),
                     reads=['P_f'], writes=['selc'])
                K.op('dve', lambda e: e.tensor_tensor(out=selc[:, 3:4], in0=selc[:, 3:4], in1=selc[:, 2:3], op=ALU.mult),
                     reads=['selc'], writes=['selc'])
                K.op('dve', lambda e: e.tensor_scalar(out=rowm[:], in0=P_f[:], scalar1=32.0, scalar2=None, op0=ALU.is_lt),
                     reads=['P_f', 'rowm'], writes=['rowm'])
                K.op('dve', lambda e: e.memset(S[:], 0.0), writes=['S'])
                K.op('pool', lambda e: e.memset(qT0[:], 0.0), writes=['qT0'])
                K.op('pool', lambda e: e.memset(qT1[:], 0.0), writes=['qT1'])

                def hb(t, h):
                    return t[:, h * 128:(h + 1) * 128]

                def body(i, sample):
                    if sample:
                        if dbg_src:
                            K.op('dve', lambda e: e.memset(xt[:], 0.0), writes=['xt'])
                            K.dma('sp', 'xt', lambda e: e.dma_start(out=xt[0:32, :], in_=xs[:, :]), writes=['xt'])
                        else:
                            K.dma('sp', 'xt', lambda e: e.dma_start(out=xt[:], in_=xres[NT * 128:(NT + 1) * 128, :]), writes=['xt'])
                        K.dma('sp', 'S', lambda e: e.dma_start(out=S[:], in_=st_hgrn[j].rearrange("h d e -> d h e")), writes=['S'])
                        dst = xres[NT * 128:(NT + 1) * 128, :]
                    else:
                        srcT = xp if dbg_src else xres
                        K.dma('sp', 'xt', lambda e: e.dma_start(out=xt[:], in_=srcT[bass.ds(i * 128, 128), :]), writes=['xt'])
                        dst = xres[bass.ds(i * 128, 128), :]
                    rmsnorm(W, xt, gmix, out_bf=(hn_bf, 'hn_bf'))
                    transpose8(hn_bf, 'hn_bf', ptp, hnT, 'hnT')

                    def zmm(pt, pname, cb0):
                        for nb in range(2):
                            for k in range(8):
                                K.op('pe', lambda e, nb=nb, k=k: e.matmul(pt[:, nb * 512:(nb + 1) * 512], lhsT=hnT[:, k, :],
                                                                          rhs=w_in[:, k, cb0 + nb * 512:cb0 + (nb + 1) * 512],
                                                                          start=(k == 0), stop=(k == 7)),
                                     reads=['hnT', 'w_in'], writes=[pname])
                    zmm(pA, 'pA', 0)
                    K.op('act', lambda e: e.activation(out=q[:], in_=pA[:], func=AF.Silu), reads=['pA'], writes=['q'])
                    zmm(pB, 'pB', 1024)
                    K.op('act', lambda e: e.activation(out=fg[:], in_=pB[:], func=AF.Sigmoid), reads=['pB'], writes=['fg'])
                    K.op('dve', lambda e: e.tensor_tensor(out=fg[:], in0=fg[:], in1=oml_b[:], op=ALU.mult), reads=['fg', 'oml_b'], writes=['fg'])
                    K.op('dve', lambda e: e.tensor_tensor(out=fg[:], in0=fg[:], in1=lb_b[:], op=ALU.add), reads=['fg', 'lb_b'], writes=['fg'])
                    K.op('act', lambda e: e.activation(out=logf[:], in_=fg[:], func=AF.Ln), reads=['fg'], writes=['logf'])
                    K.op('dve', lambda e: e.tensor_scalar(out=kk[:], in0=fg[:], scalar1=-1.0, scalar2=1.0, op0=ALU.mult, op1=ALU.add),
                         reads=['fg'], writes=['kk'])
                    if sample:
                        K.op('dve', lambda e: e.tensor_scalar(out=logf[:], in0=logf[:], scalar1=rowm[:, 0:1], scalar2=None, op0=ALU.mult),
                             reads=['logf', 'rowm'], writes=['logf'])
                        K.op('dve', lambda e: e.tensor_scalar(out=kk[:], in0=kk[:], scalar1=rowm[:, 0:1], scalar2=None, op0=ALU.mult),
                             reads=['kk', 'rowm'], writes=['kk'])
                    zmm(pA, 'pA', 2048)
                    K.op('act', lambda e: e.copy(out=v_bf[:], in_=pA[:]), reads=['pA'], writes=['v_bf'])
                    zmm(pB, 'pB', 3072)
                    K.op('act', lambda e: e.activation(out=sg[:], in_=pB[:], func=AF.Silu), reads=['pB'], writes=['sg'])
                    if stop_at <= 1:
                        K.dma('sp', 'xt', lambda e: e.dma_start(out=dst, in_=xt[:]), reads=['xt'])
                        return
                    for nb in range(2):
                        K.op('pe', lambda e, nb=nb: e.matmul(pC[:, nb * 512:(nb + 1) * 512], lhsT=trirel[:], rhs=logf[:, nb * 512:(nb + 1) * 512],
                                                             start=True, stop=True), reads=['trirel', 'logf'], writes=['pC'])
                    K.op('act', lambda e: e.activation(out=ep[:], in_=pC[:], func=AF.Exp), reads=['pC'], writes=['ep'])
                    K.op('act', lambda e: e.activation(out=em[:], in_=pC[:], func=AF.Exp, scale=-1.0), reads=['pC'], writes=['em'])
                    K.op('dve', lambda e: e.tensor_tensor(out=qt_bf[:], in0=q[:], in1=ep[:], op=ALU.mult), reads=['q', 'ep'], writes=['qt_bf'])
                    K.op('dve', lambda e: e.tensor_tensor(out=kt_bf[:], in0=kk[:], in1=em[:], op=ALU.mult), reads=['kk', 'em'], writes=['kt_bf'])
                    K.op('dve', lambda e: e.scalar_tensor_tensor(out=ktc[0][:], in0=kk[:], scalar=selc[:, 0:1], in1=em[:], op0=ALU.mult, op1=ALU.mult),
                         reads=['kk', 'em', 'selc'], writes=['ktc0'])
                    K.op('dve', lambda e: e.scalar_tensor_tensor(out=ktc[1][:], in0=kk[:], scalar=selc[:, 2:3], in1=em[:], op0=ALU.mult, op1=ALU.mult),
                         reads=['kk', 'em', 'selc'], writes=['ktc1'])
                    if stop_at <= 2:
                        K.dma('sp', 'xt', lambda e: e.dma_start(out=dst, in_=xt[:]), reads=['xt'])
                        return
                    for h in range(8):
                        K.op('pe', lambda e, h=h: e.matmul(psm[:, h * 4:(h + 1) * 4], lhsT=hb(logf, h), rhs=selc[:], start=True, stop=True),
                             reads=['logf', 'selc'], writes=['psm'])
                    K.op('dve', lambda e: e.tensor_copy(out=fs[:], in_=psm[:, 0:32].rearrange("p (h c) -> p h c", c=4)),
                         reads=['psm'], writes=['fs'])
                    for c in range(2):
                        K.op('dve', lambda e, c=c: e.tensor_copy(out=arg[:, :, 3 * c:3 * c + 1], in_=fs[:, :, 2 * c:2 * c + 1]),
                             reads=['fs'], writes=['arg'])
                        K.op('dve', lambda e, c=c: e.tensor_tensor(out=arg[:, :, 3 * c + 1:3 * c + 2], in0=fs[:, :, 2 * c:2 * c + 1],
                                                                   in1=fs[:, :, 2 * c + 1:2 * c + 2], op=ALU.subtract),
                             reads=['fs'], writes=['arg'])
                        K.op('dve', lambda e, c=c: e.tensor_copy(out=arg[:, :, 3 * c + 2:3 * c + 3], in_=fs[:, :, 2 * c + 1:2 * c + 2]),
                             reads=['fs'], writes=['arg'])
                    K.op('act', lambda e: e.activation(out=E[:], in_=arg[:], func=AF.Exp), reads=['arg'], writes=['E'])
                    if stop_at <= 3:
                        K.dma('sp', 'xt', lambda e: e.dma_start(out=dst, in_=xt[:]), reads=['xt'])
                        return
                    for h in range(8):
                        K.op('pe', lambda e, h=h: e.transpose(out=ptp[:, h, :], in_=hb(qt_bf, h), identity=ident_bf[:]),
                             reads=['qt_bf', 'ident_bf'], writes=['ptp'])
                    K.op('act', lambda e: e.copy(out=qT[:], in_=ptp[:]), reads=['ptp'], writes=['qT'])
                    K.op('act', lambda e: e.copy(out=qT0[:, :, 0:64], in_=qT[:, :, 0:64]), reads=['qT'], writes=['qT0'])
                    K.op('act', lambda e: e.copy(out=qT1[:, :, 64:128], in_=qT[:, :, 64:128]), reads=['qT'], writes=['qT1'])
                    for h in range(8):
                        K.op('pe', lambda e, h=h: e.transpose(out=ptp[:, h, :], in_=hb(kt_bf, h), identity=ident_bf[:]),
                             reads=['kt_bf', 'ident_bf'], writes=['ptp'])
                    K.op('act', lambda e: e.copy(out=kT[:], in_=ptp[:]), reads=['ptp'], writes=['kT'])
                    if stop_at <= 4:
                        K.dma('sp', 'xt', lambda e: e.dma_start(out=dst, in_=xt[:]), reads=['xt'])
                        return
                    for h in range(8):
                        K.op('pe', lambda e, h=h: e.matmul(hb(pA, h), lhsT=kT[:, h, :], rhs=qT[:, h, :], start=True, stop=True),
                             reads=['kT', 'qT'], writes=['pA'])
                    K.op('dve', lambda e: e.tensor_tensor(out=attm[:], in0=pA[:].rearrange("p (h t) -> p h t", t=128),
                                                          in1=attmask[:].unsqueeze(1).to_broadcast([128, 8, 128]), op=ALU.mult),
                         reads=['pA', 'attmask'], writes=['attm'])
                    if stop_at <= 5:
                        K.dma('sp', 'xt', lambda e: e.dma_start(out=dst, in_=xt[:]), reads=['xt'])
                        return
                    for c in range(2):
                        Sp = Sp0 if c == 0 else Sp1
                        spn = 'Sp%d' % c
                        K.op('dve', lambda e, c=c, Sp=Sp: e.tensor_tensor(out=Sp[:], in0=S[:],
                                                                          in1=E[:, :, 3 * c + 2:3 * c + 3].to_broadcast([128, 8, 128]), op=ALU.mult),
                             reads=['S', 'E'], writes=[spn])
                        for h in range(8):
                            K.op('pe', lambda e, h=h, c=c: e.matmul(hb(pC, h), lhsT=hb(ktc[c], h), rhs=hb(v_bf, h), start=True, stop=True),
                                 reads=['ktc%d' % c, 'v_bf'], writes=['pC'])
                        K.op('dve', lambda e, c=c: e.tensor_tensor(out=tmpS[:], in0=pC[:].rearrange("p (h t) -> p h t", t=128),
                                                                   in1=E[:, :, 3 * c + 1:3 * c + 2].to_broadcast([128, 8, 128]), op=ALU.mult),
                             reads=['pC', 'E'], writes=['tmpS'])
                        K.op('dve', lambda e, c=c: e.tensor_tensor(out=S[:], in0=S[:], in1=E[:, :, 3 * c:3 * c + 1].to_broadcast([128, 8, 128]),
                                                                   op=ALU.mult), reads=['S', 'E'], writes=['S'])
                        K.op('dve', lambda e: e.tensor_tensor(out=S[:], in0=S[:], in1=tmpS[:], op=ALU.add), reads=['S', 'tmpS'], writes=['S'])
                    if stop_at <= 6:
                        K.dma('sp', 'xt', lambda e: e.dma_start(out=dst, in_=xt[:]), reads=['xt'])
                        return
                    for h in range(8):
                        K.op('pe', lambda e, h=h: e.matmul(hb(pB, h), lhsT=attm[:, h, :], rhs=hb(v_bf, h), start=True, stop=False),
                             reads=['attm', 'v_bf'], writes=['pB'])
                        K.op('pe', lambda e, h=h: e.matmul(hb(pB, h), lhsT=qT0[:, h, :], rhs=Sp0[:, h, :], start=False, stop=False),
                             reads=['qT0', 'Sp0'], writes=['pB'])
                        K.op('pe', lambda e, h=h: e.matmul(hb(pB, h), lhsT=qT1[:, h, :], rhs=Sp1[:, h, :], start=False, stop=True),
                             reads=['qT1', 'Sp1'], writes=['pB'])
                    if stop_at <= 7:
                        K.dma('sp', 'xt', lambda e: e.dma_start(out=dst, in_=xt[:]), reads=['xt'])
                        return
                    K.op('act', lambda e: e.activation(out=W['junk'][:], in_=pB[:], func=AF.Square), reads=['pB'], writes=['junk'])
                    K.op('dve', lambda e: e.tensor_reduce(out=ssq[:], in_=W['junk'][:].rearrange("p (h t) -> p h t", t=128), axis=AX.X, op=ALU.add),
                         reads=['junk'], writes=['ssq'])
                    K.op('act', lambda e: e.activation(out=ssq[:], in_=ssq[:], func=AF.Sqrt, scale=1.0 / 128, bias=eps_t[:, 0:1]),
                         reads=['ssq'], writes=['ssq'])
                    K.op('dve', lambda e: e.reciprocal(out=rst8[:], in_=ssq[:]), reads=['ssq'], writes=['rst8'])
                    K.op('dve', lambda e: e.tensor_tensor(out=o[:].rearrange("p (h t) -> p h t", t=128),
                                                          in0=pB[:].rearrange("p (h t) -> p h t", t=128),
                                                          in1=rst8[:].unsqueeze(2).to_broadcast([128, 8, 128]), op=ALU.mult),
                         reads=['pB', 'rst8'], writes=['o'])
                    K.op('dve', lambda e: e.tensor_tensor(out=o[:].rearrange("p (h t) -> p h t", t=128),
                                                          in0=o[:].rearrange("p (h t) -> p h t", t=128),
                                                          in1=ng_b[:].unsqueeze(1).to_broadcast([128, 8, 128]), op=ALU.mult),
                         reads=['o', 'ng_b'], writes=['o'])
                    K.op('dve', lambda e: e.tensor_tensor(out=o_bf[:], in0=o[:], in1=sg[:], op=ALU.mult), reads=['o', 'sg'], writes=['o_bf'])
                    transpose8(o_bf, 'o_bf', ptp, oT, 'oT')
                    for nb in range(2):
                        for k in range(8):
                            K.op('pe', lambda e, nb=nb, k=k: e.matmul(pA[:, nb * 512:(nb + 1) * 512], lhsT=oT[:, k, :],
                                                                      rhs=w_out[:, k, nb * 512:(nb + 1) * 512], start=(k == 0), stop=(k == 7)),
                                 reads=['oT', 'w_out'], writes=['pA'])
                    K.op('dve', lambda e: e.tensor_tensor(out=xt[:], in0=pA[:], in1=xt[:], op=ALU.add), reads=['pA', 'xt'], writes=['xt'])
                    K.dma('sp', 'xt', lambda e: e.dma_start(out=dst, in_=xt[:]), reads=['xt'])

                K.barrier_reset()
                with nc.Fori(0, NT) as i:
                    body(i, False)
                    K.barrier_reset()
                K.dma('sp', 'S', lambda e: e.dma_start(out=hg_p[j].rearrange("h d e -> d h e"), in_=S[:]), reads=['S'])
                body(None, True)
                K.dma('sp', 'S', lambda e: e.dma_start(out=hg_s[j].rearrange("h d e -> d h e"), in_=S[:]), reads=['S'])
                K.barrier_reset()

        def phase_peer(l, last):
            NB = 16
            with contextlib.ExitStack() as ph:
                wq = sb(ph, "p_wq", [128, 8, 2048], BF16)
                keysT = sb(ph, "p_keysT", [128, 16, 128], BF16)
                gffn = sb(ph, "p_gffn", [128, D])
                gfin = sb(ph, "p_gfin", [128, D]) if last else None
                W = dict(junk=sb(ph, "p_junk", [128, D], BF16), ss=sb(ph, "p_ss", [128, 4]))
                xt = sb(ph, "p_xt", [128, D])
                hn_bf = sb(ph, "p_hn_bf", [128, D], BF16)
                hnT = sb(ph, "p_hnT", [128, 8, 128], BF16)
                q_bf = sb(ph, "p_q_bf", [128, 2048], BF16)
                hn = q_bf[:].bitcast(F32)
                qT = sb(ph, "p_qT", [128, 16, 128], BF16)
                sc = sb(ph, "p_sc", [128, 2048]); sc2 = sb(ph, "p_sc2", [128, 2048])
                top = sb(ph, "p_top", [128, 16, 16]); topi = sb(ph, "p_topi", [128, 16, 16], U32)
                topf = sb(ph, "p_topf", [128, 16, 16])
                i128 = sb(ph, "p_i128", [128, 8, 16])
                cand = sb(ph, "p_cand", [128, 8, 256]); cand2 = sc2[:].rearrange("p (h c) -> p h c", c=256)
                best = sb(ph, "p_best", [128, 8, 16]); pos = sb(ph, "p_pos", [128, 8, 16], U32)
                a_u = sb(ph, "p_a_u", [128, 8, 16], U32); b_u = sb(ph, "p_b_u", [128, 8, 16], U32)
                a_f = sb(ph, "p_a_f", [128, 8, 16]); b_f = sb(ph, "p_b_f", [128, 8, 16])
                eq = sc[:].rearrange("p (h a b) -> p h a b", a=16, b=16)
                isel = sb(ph, "p_isel", [128, 8, 16]); jsel = sb(ph, "p_jsel", [128, 8, 16])
                eidx = sb(ph, "p_eidx", [128, 128], I32)
                ge = sb(ph, "p_ge", [128, 8, 16]); gs = sb(ph, "p_gs", [128, 8])
                actv = sb(ph, "p_actv", [128, 128]); wgt = sb(ph, "p_wgt", [128, 128])
                dg = [sb(ph, "p_dg%d" % b, [128, 8, 128], BF16) for b in range(2)]
                uvb = [sb(ph, "p_uvb%d" % b, [128, 2 * D], BF16) for b in range(NB)]
                pbig = ps(ph, "p_pbig", [128, 2048])
                pz = [pbig[:, b * 512:(b + 1) * 512] for b in range(4)]
                hn_ps = ps(ph, "p_hnps", [128, D])
                ptp = ps(ph, "p_ptp", [128, 8, 128], BF16)
                oacc = pbig[:, 0:1024]

                load_w_bf16(wq, 'wq', peer_wq[l], 2048)
                bcast_load(gffn, 'gffn', norm_ffn[l])
                if last:
                    bcast_load(gfin, 'gfin', norm_final[0])
                nat = sc[:, 0:1024].rearrange("p (a c) -> p a c", c=128)
                nat_bf = q_bf[:].rearrange("p (a c) -> p a c", c=128)
                for half in range(2):
                    K.dma('sp', 'nat', lambda e, half=half: e.dma_start(
                        out=nat, in_=peer_keys[l, half * 4:(half + 1) * 4].rearrange("h p k c -> k (h p) c")), writes=['nat'])
                    K.op('dve', lambda e, half=half: e.tensor_copy(out=nat_bf[:, half * 8:(half + 1) * 8, :], in_=nat), reads=['nat'], writes=['nat_bf'])
                for half in range(2):
                    for k in range(8):
                        K.op('pe', lambda e, k=k, half=half: e.transpose(out=ptp[:, k, :], in_=nat_bf[:, half * 8 + k, :], identity=ident_bf[:]),
                             reads=['nat_bf', 'ident_bf'], writes=['ptp'])
                    K.op('act', lambda e, half=half: e.copy(out=keysT[:, half * 8:(half + 1) * 8, :], in_=ptp[:]), reads=['ptp'], writes=['keysT'])


                def body(i, sample):
                    if sample:
                        K.dma('sp', 'xt', lambda e: e.dma_start(out=xt[:], in_=xres[NT * 128:(NT + 1) * 128, :]), writes=['xt'])
                    else:
                        K.dma('sp', 'xt', lambda e: e.dma_start(out=xt[:], in_=xres[bass.ds(i * 128, 128), :]), writes=['xt'])
                    rmsnorm(W, xt, gffn, out_f32=(hn, 'hn'), out_bf=(hn_bf, 'hn_bf'))
                    for hh in range(2):
                        K.op('act', lambda e, hh=hh: e.copy(out=hn_ps[:, hh * 512:(hh + 1) * 512], in_=hn[:, hh * 512:(hh + 1) * 512]),
                             reads=['hn'], writes=['hn_ps'])
                    transpose8(hn_bf, 'hn_bf', ptp, hnT, 'hnT')
                    for nb in range(4):
                        for k in range(8):
                            K.op('pe', lambda e, nb=nb, k=k: e.matmul(pz[nb][:], lhsT=hnT[:, k, :], rhs=wq[:, k, nb * 512:(nb + 1) * 512],
                                                                      start=(k == 0), stop=(k == 7)),
                                 reads=['hnT', 'wq'], writes=['pz%d' % nb])
                        K.op('act', lambda e, nb=nb: e.copy(out=q_bf[:, nb * 512:(nb + 1) * 512], in_=pz[nb][:]),
                             reads=['pz%d' % nb], writes=['q_bf', 'hn'])
                    for half in range(2):
                        for k in range(8):
                            K.op('pe', lambda e, k=k, half=half: e.transpose(out=ptp[:, k, :], in_=q_bf[:, (half * 8 + k) * 128:(half * 8 + k + 1) * 128],
                                                                             identity=ident_bf[:]),
                                 reads=['q_bf', 'ident_bf'], writes=['ptp'])
                        K.op('act', lambda e, half=half: e.copy(out=qT[:, half * 8:(half + 1) * 8, :], in_=ptp[:]), reads=['ptp'], writes=['qT'])
                    for hp in range(16):
                        K.op('pe', lambda e, hp=hp: e.matmul(pz[hp // 4][:, (hp % 4) * 128:(hp % 4 + 1) * 128], lhsT=qT[:, hp, :],
                                                             rhs=keysT[:, hp, :], start=True, stop=True),
                             reads=['qT', 'keysT'], writes=['pz%d' % (hp // 4)])
                    for nb in range(4):
                        K.op('act', lambda e, nb=nb: e.copy(out=sc[:, nb * 512:(nb + 1) * 512], in_=pz[nb][:]),
                             reads=['pz%d' % nb], writes=['sc%d' % nb])
                    def grp_steps(hp):
                        blk = slice(hp * 128, (hp + 1) * 128)
                        scn = 'sc%d' % (hp // 4)
                        T, TI, S2 = 'top_%d' % hp, 'topi_%d' % hp, 'sc2_%d' % hp
                        return [
                            lambda: K.op('dve', lambda e: e.max(out=top[:, hp, 0:8], in_=sc[:, blk]), reads=[scn], writes=[T]),
                            lambda: K.op('dve', lambda e: e.max_index(out=topi[:, hp, 0:8], in_max=top[:, hp, 0:8], in_values=sc[:, blk]),
                                         reads=[scn, T], writes=[TI]),
                            lambda: K.op('dve', lambda e: e.match_replace(out=sc2[:, blk], in_to_replace=top[:, hp, 0:8],
                                                                          in_values=sc[:, blk], imm_value=NEG), reads=[scn, T], writes=[S2]),
                            lambda: K.op('dve', lambda e: e.max(out=top[:, hp, 8:16], in_=sc2[:, blk]), reads=[S2], writes=[T]),
                            lambda: K.op('dve', lambda e: e.max_index(out=topi[:, hp, 8:16], in_max=top[:, hp, 8:16], in_values=sc2[:, blk]),
                                         reads=[S2, T], writes=[TI]),
                        ]
                    for hp0 in range(0, 16, 4):
                        chains = [grp_steps(hp0 + x) for x in range(4)]
                        for st in range(5):
                            for c in chains:
                                c[st]()
                    ALLT = ['top_%d' % x for x in range(16)]
                    ALLTI = ['topi_%d' % x for x in range(16)]
                    K.op('dve', lambda e: e.tensor_copy(out=topf[:], in_=topi[:]), reads=ALLTI, writes=['topf'])
                    top4 = top[:].rearrange("p (h two) k -> p h two k", two=2)
                    topf4 = topf[:].rearrange("p (h two) k -> p h two k", two=2)
                    K.op('dve', lambda e: e.tensor_tensor(out=cand[:].rearrange("p h (a b) -> p h a b", b=16),
                                                          in0=top4[:, :, 0, :].unsqueeze(3).to_broadcast([128, 8, 16, 16]),
                                                          in1=top4[:, :, 1, :].unsqueeze(2).to_broadcast([128, 8, 16, 16]), op=ALU.add),
                         reads=ALLT, writes=['cand'])
                    K.op('dve', lambda e: e.tensor_scalar(out=i128[:], in0=topf4[:, :, 0, :], scalar1=128.0, scalar2=None, op0=ALU.mult),
                         reads=['topf'], writes=['i128'])

                    def head_steps(h):
                        BE, PO = 'best_%d' % h, 'pos_%d' % h
                        S2 = ['sc2_%d' % (2 * h), 'sc2_%d' % (2 * h + 1)]
                        return [
                            lambda: K.op('dve', lambda e: e.max(out=best[:, h, 0:8], in_=cand[:, h, :]), reads=['cand'], writes=[BE]),
                            lambda: K.op('dve', lambda e: e.max_index(out=pos[:, h, 0:8], in_max=best[:, h, 0:8], in_values=cand[:, h, :]),
                                         reads=['cand', BE], writes=[PO]),
                            lambda: K.op('dve', lambda e: e.match_replace(out=cand2[:, h, :], in_to_replace=best[:, h, 0:8], in_values=cand[:, h, :],
                                                                          imm_value=NEG), reads=['cand', BE], writes=S2),
                            lambda: K.op('dve', lambda e: e.max(out=best[:, h, 8:16], in_=cand2[:, h, :]), reads=S2, writes=[BE]),
                            lambda: K.op('dve', lambda e: e.max_index(out=pos[:, h, 8:16], in_max=best[:, h, 8:16], in_values=cand2[:, h, :]),
                                         reads=S2 + [BE], writes=[PO]),
                        ]
                    for h0 in range(0, 8, 4):
                        chains = [head_steps(h0 + x) for x in range(4)]
                        for st in range(5):
                            for c in chains:
                                c[st]()
                    ALLB = ['best_%d' % x for x in range(8)]
                    ALLP = ['pos_%d' % x for x in range(8)]
                    K.op('dve', lambda e: e.tensor_single_scalar(out=a_u[:], in_=pos[:], scalar=4, op=ALU.logical_shift_right),
                         reads=ALLP, writes=['a_u'])
                    K.op('dve', lambda e: e.tensor_single_scalar(out=b_u[:], in_=pos[:], scalar=15, op=ALU.bitwise_and),
                         reads=ALLP, writes=['b_u'])
                    K.op('dve', lambda e: e.tensor_copy(out=a_f[:], in_=a_u[:]), reads=['a_u'], writes=['a_f'])
                    K.op('dve', lambda e: e.tensor_copy(out=b_f[:], in_=b_u[:]), reads=['b_u'], writes=['b_f'])
                    io4 = iota16[:].unsqueeze(1).unsqueeze(1).to_broadcast([128, 8, 16, 16])
                    for (xf, xn, src, srcn, dstt, dn) in ((a_f, 'a_f', i128[:], 'i128', isel, 'isel'),
                                                          (b_f, 'b_f', topf4[:, :, 1, :], 'topf', jsel, 'jsel')):
                        K.op('dve', lambda e, xf=xf: e.tensor_tensor(out=eq[:], in0=xf[:].unsqueeze(3).to_broadcast([128, 8, 16, 16]),
                                                                     in1=io4, op=ALU.is_equal), reads=[xn, 'iota16'],
                             writes=['eq', 'sc0', 'sc1', 'sc2', 'sc3'])
                        K.op('dve', lambda e, src=src: e.tensor_tensor(out=eq[:], in0=eq[:], in1=src.unsqueeze(2).to_broadcast([128, 8, 16, 16]),
                                                                       op=ALU.mult), reads=['eq', srcn], writes=['eq'])
                        K.op('dve', lambda e, dstt=dstt: e.tensor_reduce(out=dstt[:], in_=eq[:], axis=AX.X, op=ALU.add),
                             reads=['eq'], writes=[dn])
                    K.op('dve', lambda e: e.scalar_tensor_tensor(out=isel[:], in0=isel[:], scalar=float(l * NE), in1=jsel[:],
                                                                 op0=ALU.add, op1=ALU.add), reads=['isel', 'jsel'], writes=['isel'])
                    K.op('dve', lambda e: e.tensor_copy(out=eidx[:].rearrange("p (h k) -> p h k", k=16), in_=isel[:]), reads=['isel'], writes=['eidx'])
                    K.op('dve', lambda e: e.tensor_tensor(out=ge[:], in0=best[:], in1=best[:, :, 0:1].to_broadcast([128, 8, 16]), op=ALU.subtract),
                         reads=ALLB, writes=['ge'])
                    K.op('act', lambda e: e.activation(out=ge[:], in_=ge[:], func=AF.Exp), reads=['ge'], writes=['ge'])
                    K.op('dve', lambda e: e.tensor_reduce(out=gs[:], in_=ge[:], axis=AX.X, op=ALU.add), reads=['ge'], writes=['gs'])
                    K.op('dve', lambda e: e.reciprocal(out=gs[:], in_=gs[:]), reads=['gs'], writes=['gs'])
                    K.op('dve', lambda e: e.tensor_tensor(out=ge[:], in0=ge[:], in1=gs[:].unsqueeze(2).to_broadcast([128, 8, 16]), op=ALU.mult),
                         reads=['ge', 'gs'], writes=['ge'])
                    ge2 = ge[:].rearrange("p h k -> p (h k)")
                    def dots(g):
                        for ei in range(g * 8, g * 8 + 8):
                            b = ei % NB
                            K.dma('pool', 'uvb%d' % b, lambda e, ei=ei, b=b: e.indirect_dma_start(
                                out=uvb[b][:], out_offset=None, in_=tuv,
                                in_offset=bass.IndirectOffsetOnAxis(ap=eidx[:, ei:ei + 1], axis=0), bounds_check=bounds_reg, oob_is_err=False),
                                reads=['eidx'], writes=['uvb%d' % b])
                            K.op('dve', lambda e, ei=ei, b=b: e.scalar_tensor_tensor(out=W['junk'][:], in0=uvb[b][:, 0:D], scalar=1.0, in1=hn_ps[:],
                                                                                     op0=ALU.mult, op1=ALU.mult, accum_out=actv[:, ei:ei + 1]),
                                 reads=['uvb%d' % b, 'hn_ps'], writes=['actv_e%d' % ei])
                        gs8 = slice(g * 8, g * 8 + 8)
                        K.op('act', lambda e: e.activation(out=wgt[:, gs8], in_=actv[:, gs8], func=AF.Gelu_apprx_tanh),
                             reads=['actv_e%d' % x for x in range(g * 8, g * 8 + 8)], writes=['wgt%d' % g])

                    def finish(g):
                        gs8 = slice(g * 8, g * 8 + 8)
                        K.op('dve', lambda e: e.tensor_tensor(out=wgt[:, gs8], in0=wgt[:, gs8], in1=ge2[:, gs8], op=ALU.mult),
                             reads=['wgt%d' % g, 'ge'], writes=['wgt%d' % g])
                        dgt = dg[g % 2]
                        dgn = 'dg%d' % (g % 2)
                        K.op('dve', lambda e: e.tensor_tensor(
                            out=dgt[:], in0=ident_f[:].unsqueeze(1).to_broadcast([128, 8, 128]),
                            in1=wgt[:, gs8].unsqueeze(2).to_broadcast([128, 8, 128]), op=ALU.mult),
                            reads=['wgt%d' % g, 'ident_f'], writes=[dgn])
                        for ei in range(g * 8, g * 8 + 8):
                            b = ei % NB
                            for hh in range(2):
                                K.op('pe', lambda e, ei=ei, b=b, hh=hh: e.matmul(
                                    oacc[:, hh * 512:(hh + 1) * 512], lhsT=dgt[:, ei % 8, :], rhs=uvb[b][:, D + hh * 512:D + (hh + 1) * 512],
                                    start=(ei == 0), stop=(ei == 127)),
                                    reads=[dgn, 'uvb%d' % b], writes=['pz%d' % hh])

                    assert NB == 16
                    for g in range(16):
                        dots(g)
                        finish(g)
                    K.op('dve', lambda e: e.tensor_tensor(out=xt[:], in0=oacc[:], in1=xt[:], op=ALU.add), reads=['xt', 'pz0', 'pz1'], writes=['xt'])
                    if last:
                        rmsnorm(W, xt, gfin, out_f32=(hn, 'hn'))
                        if sample:
                            K.dma('sp', 'hn', lambda e: e.dma_start(out=ys[:, :], in_=hn[0:32, :]), reads=['hn'])
                        else:
                            K.dma('sp', 'hn', lambda e: e.dma_start(out=yp[bass.ds(i * 128, 128), :], in_=hn[:]), reads=['hn'])
                    else:
                        if sample:
                            K.dma('sp', 'xt', lambda e: e.dma_start(out=xres[NT * 128:(NT + 1) * 128, :], in_=xt[:]), reads=['xt'])
                        else:
                            K.dma('sp', 'xt', lambda e: e.dma_start(out=xres[bass.ds(i * 128, 128), :], in_=xt[:]), reads=['xt'])

                run_tiles(body)

        def phase_convert_tables():
            R = 4
            nrow = 4 * NE
            with contextlib.ExitStack() as ph:
                cin = [sb(ph, "cv_in%d" % t, [128, R, D]) for t in range(2)]
                cout = [sb(ph, "cv_out%d" % t, [128, R, D], BF16) for t in range(2)]
                srcs = [peer_u.rearrange("l e d -> (l e) d"), peer_v.rearrange("l e d -> (l e) d")]
                dsts = [tuv[:, 0:D], tuv[:, D:2 * D]]
                K.barrier_reset()
                with nc.Fori(0, nrow // (128 * R)) as i:
                    for t in range(2):
                        K.dma('sp', 'cin%d' % t, lambda e, t=t: e.dma_start(
                            out=cin[t][:], in_=srcs[t][bass.ds(i * (128 * R), 128 * R), :].rearrange("(p r) d -> p r d", r=R)),
                            writes=['cin%d' % t])
                    K.op('act', lambda e: e.copy(out=cout[0][:], in_=cin[0][:]), reads=['cin0'], writes=['cout0'])
                    K.op('dve', lambda e: e.tensor_copy(out=cout[1][:], in_=cin[1][:]), reads=['cin1'], writes=['cout1'])
                    for t in range(2):
                        K.dma('sp', 'cout%d' % t, lambda e, t=t: e.dma_start(
                            out=dsts[t][bass.ds(i * (128 * R), 128 * R), :].rearrange("(p r) d -> p r d", r=R), in_=cout[t][:]),
                            reads=['cout%d' % t])
                    K.barrier_reset()

        phases = []
        for l in range(4):
            if l % 2 == 0:
                phases.append(lambda l=l: phase_even(l, first=(l == 0)))
            else:
                phases.append(lambda l=l: phase_odd(l))
            phases.append(lambda l=l: phase_peer(l, last=(l == 3)))
        if only == 'odd':
            phase_odd(1, dbg_src=True)
        else:
            if n_phases >= 2:
                phase_convert_tables()
            for pi, p in enumerate(phases):
                if pi < n_phases:
                    p()
        if n_phases < 8:
            with contextlib.ExitStack() as ph:
                xt = sb(ph, "dbg_xt", [128, D])
                K.barrier_reset()
                for t in range(NT + 1):
                    K.dma('sp', 'dbg_xt', lambda e, t=t: e.dma_start(out=xt[:], in_=xres[t * 128:(t + 1) * 128, :]), writes=['dbg'])
                    if t < NT:
                        K.dma('sp', 'dbg_xt', lambda e, t=t: e.dma_start(out=yp[t * 128:(t + 1) * 128, :], in_=xt[:]), reads=['dbg'])
                    else:
                        K.dma('sp', 'dbg_xt', lambda e, t=t: e.dma_start(out=ys[:, :], in_=xt[0:32, :]), reads=['dbg'])
                K.barrier_reset()
    return nc


_W_NAMES = ["norm_mix", "norm_ffn", "ev_w_in", "ev_a_ln_g", "ev_a_ln_b", "ev_ws", "ev_bs", "ev_conv_w", "ev_conv_b",
            "ev_b_ln_g", "ev_b_ln_b", "ev_w_out", "od_w_in", "od_lower", "od_norm_g", "od_w_out", "peer_wq", "peer_keys",
            "peer_u", "peer_v"]


def run(inputs, n_phases=8, only=None, NE=16384, stop_at=99):
    x_prompt = np.asarray(inputs["x_prompt"], dtype=np.float32)
    x_sample = np.asarray(inputs["x_sample"], dtype=np.float32)
    B, T, _ = x_prompt.shape
    NT = T // 128
    nc = build_program(NT, n_phases, only=only, NE=NE, stop_at=stop_at)
    shared = {k: np.ascontiguousarray(np.asarray(inputs[k], dtype=np.float32)) for k in _W_NAMES}
    shared["norm_final"] = np.ascontiguousarray(np.asarray(inputs["norm_final"], dtype=np.float32).reshape(1, D))
    st_conv = np.asarray(inputs["state_conv"], dtype=np.float32)
    st_hgrn = np.asarray(inputs["state_hgrn"], dtype=np.float32)
    in_maps = []
    for c in range(8):
        m = dict(shared)
        m["xp"] = np.ascontiguousarray(x_prompt[c % B])
        m["xs"] = np.ascontiguousarray(x_sample[c])
        m["st_conv"] = np.ascontiguousarray(st_conv[:, c])
        m["st_hgrn"] = np.ascontiguousarray(st_hgrn[:, c])
        in_maps.append(m)
    res = run_bass_kernel_spmd(nc, in_maps, core_ids=list(range(8)))
    R = res.results
    y_prompt = np.stack([R[b]["yp"] for b in range(B)]).astype(np.float32)
    y_sample = np.stack([R[c]["ys"] for c in range(8)]).astype(np.float32)
    conv_prompt = np.stack([R[b]["conv_p"] for b in range(B)], axis=1).astype(np.float32)
    conv_sample = np.stack([R[c]["conv_s"] for c in range(8)], axis=1).astype(np.float32)
    hgrn_prompt = np.stack([R[b]["hg_p"] for b in range(B)], axis=1).astype(np.float32)
    hgrn_sample = np.stack([R[c]["hg_s"] for c in range(8)], axis=1).astype(np.float32)
    gmlp_v_sample = np.stack([R[c]["v_s"] for c in range(8)], axis=1).astype(np.float32)
    return (y_prompt, y_sample, conv_prompt, conv_sample, hgrn_prompt, hgrn_sample, gmlp_v_sample)


def kernel(**inputs):
    return run(inputs)
```
